# Optimizing a Trainium2 kernel written in Bass

```python
import jax, jax.numpy as jnp
from jax import lax
import numpy as np

D_MODEL = 1024
BATCH = 16
SEQ = 256
DEPTH = 1
DEC_BATCH = 2
DEC_SEQ = 1024
PAST_LEN = 256

GRID_W = 64
HEAD_DIM = 64
H_A = 8
H_B = 8
KV_B = 2
G_B = H_B // KV_B
W_A = H_A * HEAD_DIM
W_B = H_B * HEAD_DIM
KV_W_B = KV_B * HEAD_DIM
MIX_WIDTH = W_A + W_B
IN_WIDTH = 3 * W_A + W_B + 2 * KV_W_B
NA_ROWS = 8
NA_COLS = 16
SWA_WINDOW = 128
BLOCK = 128
D_FF = 2816
ROPE_THETA = 10000.0
EPS = 1e-6
NEG_INF = -1e30

kernel_name = 'hybrid_natten_swa_diffusion_step'


def rmsnorm(x, g):
    xf = x.astype(jnp.float32)
    var = jnp.mean(xf * xf, axis=-1, keepdims=True)
    return (xf * lax.rsqrt(var + EPS)).astype(x.dtype) * g


def adaln(cond, w_mod, b_mod):
    m = (jax.nn.silu(cond) @ w_mod + b_mod)[:, None, :]
    return jnp.split(m, 6, axis=-1)


def modulate(h, shift, scale):
    return h * (1 + scale) + shift


def project(h, w_in):
    B, T, _ = h.shape
    z = h @ w_in
    qa, ka, va, qb, kb, vb = jnp.split(z, [W_A, 2 * W_A, 3 * W_A, 3 * W_A + W_B, 3 * W_A + W_B + KV_W_B], axis=-1)
    r = lambda t, n: t.reshape(B, T, n, HEAD_DIM)
    return r(qa, H_A), r(ka, H_A), r(va, H_A), r(qb, H_B), r(kb, KV_B), r(vb, KV_B)


def rope_axis(x, pos):
    n = x.shape[-1] // 2
    inv = 1.0 / (ROPE_THETA ** (jnp.arange(n, dtype=jnp.float32) / n))
    ang = pos.astype(jnp.float32)[:, None] * inv[None, :]
    cos = jnp.cos(ang)[:, None, :].astype(x.dtype)
    sin = jnp.sin(ang)[:, None, :].astype(x.dtype)
    x1, x2 = x[..., :n], x[..., n:]
    return jnp.concatenate([x1 * cos - x2 * sin, x2 * cos + x1 * sin], axis=-1)


def rope_2d(x):
    t = jnp.arange(x.shape[1])
    half = x.shape[-1] // 2
    return jnp.concatenate([rope_axis(x[..., :half], t // GRID_W), rope_axis(x[..., half:], t % GRID_W)], axis=-1)


def ctx_self_attn(q, k, v, sink):
    B, L, KV, G, D = q.shape
    nb = L // BLOCK
    qb = jnp.moveaxis(q.reshape(B, nb, BLOCK, KV, G, D), 1, 0)
    scale = D ** -0.5

    def one(qblk):
        s = jnp.einsum('bqkgd,blkd->bkgql', qblk, k).astype(jnp.float32) * scale
        if sink is not None:
            sk = jnp.broadcast_to(sink.astype(jnp.float32)[None, :, :, None, None], s.shape[:-1] + (1,))
            p = jax.nn.softmax(jnp.concatenate([sk, s], axis=-1), axis=-1)[..., 1:]
        else:
            p = jax.nn.softmax(s, axis=-1)
        return jnp.einsum('bkgql,blkd->bqkgd', p.astype(v.dtype), v)

    o = lax.map(one, qb)
    return jnp.moveaxis(o, 0, 1).reshape(B, L, KV * G * D)


def neighborhood_attn(q, k, v, ck, cv, rpb):
    B, T, H, D = q.shape
    rows = T // GRID_W
    kr = min(NA_ROWS, rows)
    kc = NA_COLS
    r = jnp.arange(rows)
    c = jnp.arange(GRID_W)
    row_start = jnp.clip(r - kr // 2, 0, rows - kr)
    row_idx = row_start[:, None] + jnp.arange(kr)[None, :]
    col_start = jnp.clip(c - kc // 2, 0, GRID_W - kc)
    col_valid = (c[None, :] >= col_start[:, None]) & (c[None, :] < col_start[:, None] + kc)
    kg = k.reshape(B, rows, GRID_W, H, D)[:, row_idx].reshape(B, rows, kr * GRID_W, H, D)
    vg = v.reshape(B, rows, GRID_W, H, D)[:, row_idx].reshape(B, rows, kr * GRID_W, H, D)
    qg = q.reshape(B, rows, GRID_W, H, D)
    drow = row_idx - r[:, None] + (NA_ROWS - 1)
    dcol = jnp.clip(c[None, :] - c[:, None], -(kc - 1), kc - 1) + (kc - 1)
    bias = rpb[:, drow[:, None, :, None], dcol[None, :, None, :]]
    bias = bias.reshape(H, rows, GRID_W, kr * GRID_W).astype(jnp.float32)
    valid = jnp.broadcast_to(col_valid[:, None, :], (GRID_W, kr, GRID_W)).reshape(GRID_W, kr * GRID_W)
    scale = D ** -0.5
    s_nb = jnp.einsum('brqhd,brkhd->bhrqk', qg, kg).astype(jnp.float32) * scale + bias[None]
    s_nb = jnp.where(valid[None, None, None], s_nb, NEG_INF)
    s_cx = jnp.einsum('brqhd,blhd->bhrql', qg, ck).astype(jnp.float32) * scale
    p = jax.nn.softmax(jnp.concatenate([s_nb, s_cx], axis=-1), axis=-1).astype(v.dtype)
    n = kr * GRID_W
    o = (jnp.einsum('bhrqk,brkhd->brqhd', p[..., :n], vg)
         + jnp.einsum('bhrql,blhd->brqhd', p[..., n:], cv))
    return o.reshape(B, T, H * D)


def window_attn(q, k, v, ck, cv, sink):
    B, T, KV, G, D = q.shape
    nb = T // BLOCK
    qb = q.reshape(B, nb, BLOCK, KV, G, D)

    def band(x):
        xb = jnp.pad(x.reshape(B, nb, BLOCK, KV, D), ((0, 0), (1, 1), (0, 0), (0, 0), (0, 0)))
        return jnp.concatenate([xb[:, :-2], xb[:, 1:-1], xb[:, 2:]], axis=2)

    kb, vb = band(k), band(v)
    blk = jnp.arange(nb)[:, None]
    qpos = blk * BLOCK + jnp.arange(BLOCK)[None, :]
    kpos = (blk - 1) * BLOCK + jnp.arange(3 * BLOCK)[None, :]
    valid = ((jnp.abs(qpos[:, :, None] - kpos[:, None, :]) <= SWA_WINDOW)
             & (kpos[:, None, :] >= 0) & (kpos[:, None, :] < T))
    scale = D ** -0.5
    s = jnp.einsum('bnqkgd,bnskd->bnkgqs', qb, kb).astype(jnp.float32) * scale
    s = jnp.where(valid[None, :, None, None], s, NEG_INF)
    sc = jnp.einsum('bnqkgd,blkd->bnkgql', qb, ck).astype(jnp.float32) * scale
    sk = jnp.broadcast_to(sink.astype(jnp.float32)[None, None, :, :, None, None], s.shape[:-1] + (1,))
    p = jax.nn.softmax(jnp.concatenate([sk, s, sc], axis=-1), axis=-1).astype(v.dtype)
    nbk = 3 * BLOCK
    o = (jnp.einsum('bnkgqs,bnskd->bnqkgd', p[..., 1:1 + nbk], vb)
         + jnp.einsum('bnkgql,blkd->bnqkgd', p[..., 1 + nbk:], cv))
    return o.reshape(B, T, KV * G * D)


def merge_heads(oa, ob, g_a, g_b, w_out):
    return jnp.concatenate([rmsnorm(oa, g_a), rmsnorm(ob, g_b)], axis=-1) @ w_out


def conv_ffn(h, w_up, conv_w, conv_b, w_down):
    u = h @ w_up
    up = jnp.pad(u, ((0, 0), (1, 1), (0, 0)))
    u = up[:, :-2] * conv_w[0] + up[:, 1:-1] * conv_w[1] + up[:, 2:] * conv_w[2] + conv_b
    gate, val = jnp.split(u, 2, axis=-1)
    return (jax.nn.silu(gate) * val) @ w_down


def setup_inputs(seed: int = 0) -> dict:
    key = jax.random.key(seed)
    ks = jax.random.split(key, 24)
    nrm = lambda k, shape, s: jax.random.normal(k, shape, jnp.float32) * s
    gain = lambda k, shape: 1.0 + 0.01 * jax.random.normal(k, shape, jnp.float32)
    return {
        'x_prompt': nrm(ks[0], (BATCH, SEQ, D_MODEL), 1.0),
        'x_sample': nrm(ks[1], (DEC_BATCH, DEC_SEQ, D_MODEL), 1.0),
        'cache_a_k': nrm(ks[2], (DEC_BATCH, DEPTH, PAST_LEN, H_A, HEAD_DIM), 1.0),
        'cache_a_v': nrm(ks[3], (DEC_BATCH, DEPTH, PAST_LEN, H_A, HEAD_DIM), 1.0),
        'cache_b_k': nrm(ks[4], (DEC_BATCH, DEPTH, PAST_LEN, KV_B, HEAD_DIM), 1.0),
        'cache_b_v': nrm(ks[5], (DEC_BATCH, DEPTH, PAST_LEN, KV_B, HEAD_DIM), 1.0),
        'c': nrm(ks[6], (DEC_BATCH, D_MODEL), 1.0),
        'c_ctx': nrm(ks[7], (D_MODEL,), 1.0),
        'w_mod': nrm(ks[8], (DEPTH, D_MODEL, 6 * D_MODEL), 0.5 * D_MODEL ** -0.5),
        'b_mod': nrm(ks[9], (DEPTH, 6 * D_MODEL), 0.01),
        'g_mix_pre': gain(ks[10], (DEPTH, D_MODEL)),
        'g_mix_post': gain(ks[11], (DEPTH, D_MODEL)),
        'g_ffn_pre': gain(ks[12], (DEPTH, D_MODEL)),
        'g_ffn_post': gain(ks[13], (DEPTH, D_MODEL)),
        'w_in': nrm(ks[14], (DEPTH, D_MODEL, IN_WIDTH), D_MODEL ** -0.5),
        'rpb_a': nrm(ks[15], (DEPTH, H_A, 2 * NA_ROWS - 1, 2 * NA_COLS - 1), 0.1),
        'sink_b': nrm(ks[16], (DEPTH, KV_B, G_B), 0.5),
        'g_grp_a': gain(ks[17], (DEPTH, W_A)),
        'g_grp_b': gain(ks[18], (DEPTH, W_B)),
        'w_out': nrm(ks[19], (DEPTH, MIX_WIDTH, D_MODEL), MIX_WIDTH ** -0.5),
        'w_up': nrm(ks[20], (DEPTH, D_MODEL, 2 * D_FF), D_MODEL ** -0.5),
        'conv_w': nrm(ks[21], (DEPTH, 3, 2 * D_FF), 3 ** -0.5),
        'conv_b': nrm(ks[22], (DEPTH, 2 * D_FF), 0.01),
        'w_down': nrm(ks[23], (DEPTH, D_FF, D_MODEL), D_FF ** -0.5),
    }


def reference(x_prompt, x_sample, cache_a_k, cache_a_v, cache_b_k, cache_b_v, c, c_ctx,
              w_mod, b_mod, g_mix_pre, g_mix_post, g_ffn_pre, g_ffn_post, w_in, rpb_a, sink_b,
              g_grp_a, g_grp_b, w_out, w_up, conv_w, conv_b, w_down):
    xp = x_prompt
    xs = x_sample
    Bs, T, _ = x_sample.shape
    new_ak, new_av, new_bk, new_bv = [], [], [], []
    for l in range(DEPTH):
        sh1, sc1, gt1, sh2, sc2, gt2 = adaln(c_ctx[None], w_mod[l], b_mod[l])
        h = modulate(rmsnorm(xp, g_mix_pre[l]), sh1, sc1)
        qa, ka, va, qb, kb, vb = project(h, w_in[l])
        Bp, L = xp.shape[0], xp.shape[1]
        oa = ctx_self_attn(qa[:, :, :, None, :], ka, va, None)
        ob = ctx_self_attn(qb.reshape(Bp, L, KV_B, G_B, HEAD_DIM), kb, vb, sink_b[l])
        xp = xp + gt1 * rmsnorm(merge_heads(oa, ob, g_grp_a[l], g_grp_b[l], w_out[l]), g_mix_post[l])
        h = modulate(rmsnorm(xp, g_ffn_pre[l]), sh2, sc2)
        xp = xp + gt2 * rmsnorm(conv_ffn(h, w_up[l], conv_w[l], conv_b[l], w_down[l]), g_ffn_post[l])
        new_ak.append(ka)
        new_av.append(va)
        new_bk.append(kb)
        new_bv.append(vb)

        sh1, sc1, gt1, sh2, sc2, gt2 = adaln(c, w_mod[l], b_mod[l])
        h = modulate(rmsnorm(xs, g_mix_pre[l]), sh1, sc1)
        qa, ka, va, qb, kb, vb = project(h, w_in[l])
        qb = rope_2d(qb).reshape(Bs, T, KV_B, G_B, HEAD_DIM)
        kb = rope_2d(kb)
        oa = neighborhood_attn(qa, ka, va, cache_a_k[:, l], cache_a_v[:, l], rpb_a[l])
        ob = window_attn(qb, kb, vb, cache_b_k[:, l], cache_b_v[:, l], sink_b[l])
        xs = xs + gt1 * rmsnorm(merge_heads(oa, ob, g_grp_a[l], g_grp_b[l], w_out[l]), g_mix_post[l])
        h = modulate(rmsnorm(xs, g_ffn_pre[l]), sh2, sc2)
        xs = xs + gt2 * rmsnorm(conv_ffn(h, w_up[l], conv_w[l], conv_b[l], w_down[l]), g_ffn_post[l])
    new_a_k = jnp.stack(new_ak, axis=1)
    new_a_v = jnp.stack(new_av, axis=1)
    new_b_k = jnp.stack(new_bk, axis=1)
    new_b_v = jnp.stack(new_bv, axis=1)
    return (xp, xs, new_a_k, new_a_v, new_b_k, new_b_v)
```

```python
import numpy as np
import concourse.bass as bass
import concourse.mybir as mybir
from concourse.bass_utils import run_bass_kernel_spmd

F32 = mybir.dt.float32
BF16 = mybir.dt.bfloat16
AF = mybir.ActivationFunctionType
ALU = mybir.AluOpType

NDS = 48
EPS = 1e-6
NEG = -30000.0
D = 1024
NW = 896
NO = 258
DFF = 2816
NPAIR = 22


def _merge(d, o):
    for k, v in o.items():
        if d.get(k, 0) < v:
            d[k] = v


ALL_TKS = []


class Tk:
    def __init__(s, name="t", excl=False):
        ALL_TKS.append(s)
        s.name = name
        s.w = {}
        s.r = {}
        s.old = {}
        s.excl = excl
        s.acc = {}

    def all_tokens(s):
        d = {}
        _merge(d, s.w); _merge(d, s.r); _merge(d, s.old)
        return d


class Buf:
    def __init__(s, t, lo, hi, name):
        s.t = t
        s.lo = lo
        s.hi = hi
        s.name = name
        s.tks = []
        s.ghost = {}

    def tk(s, name=None):
        k = Tk(name or s.name)
        _merge(k.old, s.ghost)
        s.tks.append(k)
        return k

    def tks_n(s, n):
        return [s.tk("%s%d" % (s.name, i)) for i in range(n)]

    def __getitem__(s, key):
        return s.t[key]


class Sched:
    def __init__(s, nc, sbuf_lo=16512, sbuf_hi=229312):
        s.nc = nc
        s.eng = {'pe': nc.tensor, 'act': nc.scalar, 'dve': nc.vector, 'pool': nc.gpsimd, 'sp': nc.sync}
        s.sem = {k: nc.alloc_semaphore("sem_" + k) for k in s.eng}
        s.cnt = {k: 0 for k in s.eng}
        s.waited = {k: {} for k in s.eng}
        s.dsem = [nc.alloc_semaphore("dsem%d" % i) for i in range(NDS)]
        s.dcnt = [0] * NDS
        s.dnext2 = [0, 0]
        s.semobj = {}
        for k in s.eng:
            s.semobj[('e', k)] = s.sem[k]
        for i in range(NDS):
            s.semobj[('d', i)] = s.dsem[i]
        s.nwaits = 0
        s.nops = {k: 0 for k in s.eng}
        s.lo = sbuf_lo
        s.hi = sbuf_hi
        s.live = []
        s.ghosts = []
        s.uid = 0
        s.peak = 0

    def alloc(s, name, shape, dtype, align=64):
        size = int(np.prod(shape[1:])) * mybir.dt.size(dtype)
        size = (size + align - 1) // align * align
        ivs = sorted((b.lo, b.hi) for b in s.live)
        pos = s.lo
        found = None
        for lo, hi in ivs:
            if lo - pos >= size:
                found = pos
                break
            pos = max(pos, hi)
        if found is None:
            if s.hi - pos >= size:
                found = pos
            else:
                raise RuntimeError("SBUF OOM allocating %s size %d; live=%s" % (
                    name, size, [(b.name, b.lo, b.hi) for b in s.live]))
        s.uid += 1
        t = s.nc.alloc_sbuf_tensor_at("%s_%d" % (name, s.uid), list(shape), dtype, offset=found)
        b = Buf(t, found, found + size, name)
        g = {}
        for lo, hi, tok in s.ghosts:
            if lo < b.hi and b.lo < hi:
                _merge(g, tok)
        b.ghost = g
        s.live.append(b)
        s.peak = max(s.peak, b.hi)
        return b

    def free(s, *bufs):
        for b in bufs:
            tok = dict(b.ghost)
            for k in b.tks:
                _merge(tok, k.all_tokens())
            s.ghosts.append((b.lo, b.hi, tok))
            s.live.remove(b)

    def _deps(s, reads, writes, pwrites, me=None):
        d = {}
        for t in list(reads) + list(writes) + list(pwrites):
            if t.excl:
                for k, v in t.acc.items():
                    if k != me and d.get(k, 0) < v:
                        d[k] = v
        for t in reads:
            _merge(d, t.w)
        for t in writes:
            _merge(d, t.w); _merge(d, t.r); _merge(d, t.old)
        for t in pwrites:
            if t.r:
                _merge(t.old, t.w); _merge(t.old, t.r)
                t.w = {}; t.r = {}
            _merge(d, t.old)
        return d

    def _wait(s, e, d):
        for key, val in d.items():
            if key == ('e', 'pe') and e == 'pe':
                continue
            if s.waited[e].get(key, 0) < val:
                s.eng[e].wait_ge(s.semobj[key], val)
                s.waited[e][key] = val
                s.nwaits += 1

    def _mark(s, key, val, reads, writes, pwrites):
        for t in list(reads) + list(writes) + list(pwrites):
            if t.excl:
                t.acc[key] = val
        for t in reads:
            if t.r.get(key, 0) < val:
                t.r[key] = val
        for t in writes:
            t.w = {key: val}; t.r = {}; t.old = {}
        for t in pwrites:
            if t.w.get(key, 0) < val:
                t.w[key] = val

    def op(s, e, fn, reads=(), writes=(), pwrites=()):
        s._wait(e, s._deps(reads, writes, pwrites, ('e', e)))
        ins = fn(s.eng[e])
        s.cnt[e] += 1
        s.nops[e] += 1
        ins.then_inc(s.sem[e], 1)
        s._mark(('e', e), s.cnt[e], reads, writes, pwrites)

    def dma(s, q, out, in_, reads=(), writes=(), pwrites=(), **kw):
        d = s._deps(reads, writes, pwrites)
        half = NDS // 2
        qi = 0 if q == 'sp' else 1
        i = qi * half + s.dnext2[qi]
        s.dnext2[qi] = (s.dnext2[qi] + 1) % half
        key = ('d', i)
        if s.dcnt[i] > 0:
            if d.get(key, 0) < s.dcnt[i]:
                d[key] = s.dcnt[i]
        s._wait(q, d)
        ins = s.eng[q].dma_start(out=out, in_=in_, **kw)
        s.dcnt[i] += 16
        s.nops[q] += 1
        ins.then_inc(s.dsem[i], 16)
        s._mark(key, s.dcnt[i], reads, writes, pwrites)

    def finish(s, tks):
        d = {}
        for t in tks:
            _merge(d, t.all_tokens())
        s._wait('sp', d)


IN_SPECS = [
    ("xp", [512, D]), ("xo", [256, D]), ("xh", [2, D]), ("xw", [NW, D]),
    ("cak", [256, 512]), ("cav", [256, 512]), ("cbk", [256, 128]), ("cbv", [256, 128]),
    ("condT", [128, 8, 2]), ("w_mod", [D, 6 * D]), ("b_mod", [6 * D]), ("gains", [4 * D]),
    ("w_in", [D, 2304]), ("w_out", [D, D]), ("w_up", [D, 2 * DFF]), ("w_down", [DFF, D]),
    ("cwT", [128, 44, 3]), ("cbT", [128, 44]), ("sinkT", [128, 4]), ("ggT", [128, 8]),
    ("rpbsrc", [8, 19, 127]), ("namask", [7, 128, NO]), ("swmask", [7, 128, NO]),
    ("ropeq", [2, 128, NO]), ("ropek", [2, 128, NW]), ("flags", [128, 2]),
    ("ident", [128, 128]), ("antij", [128, 128]), ("sel", [2, 2, 128]),
]
OUT_SPECS = [
    ("y_p", [512, D]), ("y_s", [256, D]), ("nak", [512, 512]), ("nav", [512, 512]),
    ("nbk", [512, 128]), ("nbv", [512, 128]),
]


def build_program(STOP=99):
    nc = bass.Bass("TRN2", target_bir_lowering=False)
    S = Sched(nc)
    del ALL_TKS[:]

    def cut(n):
        if STOP == n:
            if n >= 5:
                for tt in range(4):
                    S.dma('sp', do["y_p"].ap()[tt * 128:(tt + 1) * 128, :], xP[:, tt, :], reads=[k_xP[tt]])
                for tt in range(2):
                    S.dma('sp', do["y_s"].ap()[tt * 128:(tt + 1) * 128, :], xO[:, tt, :], reads=[k_xO[tt]])
            S.finish(list(ALL_TKS))
            return True
        return False

    di = {n: nc.dram_tensor(n, list(sh), F32, kind="ExternalInput") for n, sh in IN_SPECS}
    do = {n: nc.dram_tensor(n, list(sh), F32, kind="ExternalOutput") for n, sh in OUT_SPECS}

    ps = [nc.alloc_psum_tensor("ps%d" % i, [128, 512], F32) for i in range(8)]
    kps = [Tk("ps%d" % i, excl=True) for i in range(8)]
    psb = [p.ap().bitcast(BF16) for p in ps]

    rr = {'ev': 0}

    def evac_eng():
        rr['ev'] += 1
        return 'act' if rr['ev'] % 2 == 0 else 'dve'

    def evac_copy(eng, out, in_, reads, writes=(), pwrites=(), scale=None):
        if eng == 'act':
            if scale is None:
                S.op('act', lambda e: e.activation(out=out, in_=in_, func=AF.Copy), reads=reads, writes=writes, pwrites=pwrites)
            else:
                S.op('act', lambda e: e.activation(out=out, in_=in_, func=AF.Copy, scale=scale), reads=reads, writes=writes, pwrites=pwrites)
        else:
            if scale is None:
                S.op(eng, lambda e: e.tensor_copy(out=out, in_=in_), reads=reads, writes=writes, pwrites=pwrites)
            else:
                S.op(eng, lambda e: e.tensor_scalar_mul(out=out, in0=in_, scalar1=scale), reads=reads, writes=writes, pwrites=pwrites)

    identb = S.alloc("identb", [128, 128], BF16); k_identb = identb.tk()
    identf = S.alloc("identf", [128, 128], F32); k_identf = identf.tk()
    antib = S.alloc("antib", [128, 128], BF16); k_antib = antib.tk()
    selb = S.alloc("selb", [2, 2, 128], F32); k_sel = selb.tk()
    ones64 = S.alloc("ones64", [128, 64], BF16); k_ones64 = ones64.tk()
    onesb = S.alloc("onesb", [128, 128], BF16); k_onesb = onesb.tk()
    epsb = S.alloc("epsb", [128, 1], F32); k_eps = epsb.tk()
    ggT = S.alloc("ggT", [128, 8], F32); k_gg = ggT.tk()
    cwT = S.alloc("cwT", [128, 44, 3], F32); k_cw = cwT.tk()
    cbT = S.alloc("cbT", [128, 44], F32); k_cb = cbT.tk()
    esT = S.alloc("esT", [128, 4], F32); k_es = esT.tk()
    flg = S.alloc("flg", [128, 2], F32); k_flg = flg.tk()
    ABT = S.alloc("ABT", [128, 4, 8, 2], F32); k_ABT = [ABT.tk("ABT0"), ABT.tk("ABT1")]
    Gbc = S.alloc("Gbc", [128, 2, 2, D], F32); k_Gbc = Gbc.tk()
    small = S.alloc("small", [128, 4, 4], F32); k_small = [small.tk("small%d" % i) for i in range(4)]
    NSM = 4

    S.dma('pool', identb[:], di["ident"].ap(), writes=[k_identb])
    S.dma('sp', identf[:], di["ident"].ap(), writes=[k_identf])
    S.dma('pool', antib[:], di["antij"].ap(), writes=[k_antib])
    S.dma('sp', selb[:], di["sel"].ap(), writes=[k_sel])
    S.dma('sp', ggT[:], di["ggT"].ap(), writes=[k_gg])
    S.dma('sp', cwT[:], di["cwT"].ap(), writes=[k_cw])
    S.dma('sp', cbT[:], di["cbT"].ap(), writes=[k_cb])
    S.dma('sp', esT[:], di["sinkT"].ap(), writes=[k_es])
    S.dma('sp', flg[:], di["flags"].ap(), writes=[k_flg])
    S.op('dve', lambda e: e.memset(ones64[:], 1.0), writes=[k_ones64])
    S.op('dve', lambda e: e.memset(onesb[:], 1.0), writes=[k_onesb])
    S.op('dve', lambda e: e.memset(epsb[:], EPS), writes=[k_eps])

    xP = S.alloc("xP", [128, 4, D], F32); k_xP = xP.tks_n(4)
    xO = S.alloc("xO", [128, 3, D], F32); k_xO = xO.tks_n(3)
    OTOK = [(0, 128), (1, 128), (2, 2)]

    def ocols(tt):
        if tt < 2:
            return slice(1 + 128 * tt, 129 + 128 * tt)
        return slice(0, NO, NO - 1)

    condT = S.alloc("condT", [128, 8, 2], F32); k_condT = condT.tk()
    sT = S.alloc("sT", [128, 8, 2], BF16); k_sT = sT.tk()
    m_sb = S.alloc("m_sb", [2, 6 * D], F32); k_m = m_sb.tks_n(12)
    gains2 = S.alloc("gains2", [2, 4, D], F32); k_g2 = gains2.tk()
    rows = S.alloc("rows", [2, 4, D], F32); k_rows = rows.tks_n(4)
    wmb = [S.alloc("wm%d" % i, [128, 8, 512], BF16) for i in range(3)]
    k_wm = [b.tk() for b in wmb]

    S.dma('sp', condT[:], di["condT"].ap(), writes=[k_condT])
    S.dma('sp', m_sb[:], bass.AP(di["b_mod"], 0, [[0, 2], [1, 6 * D]]), writes=k_m)
    S.dma('sp', gains2[:], bass.AP(di["gains"], 0, [[0, 2], [D, 4], [1, D]]), writes=[k_g2])
    for tt in range(4):
        S.dma('sp', xP[:, tt, :], di["xp"].ap()[tt * 128:(tt + 1) * 128, :], writes=[k_xP[tt]])
    for tt in range(2):
        S.dma('sp', xO[:, tt, :], di["xo"].ap()[tt * 128:(tt + 1) * 128, :], writes=[k_xO[tt]])
    S.dma('sp', xO[0:2, 2, :], di["xh"].ap(), writes=[k_xO[2]])
    S.op('act', lambda e: e.activation(out=sT[:], in_=condT[:], func=AF.Silu), reads=[k_condT], writes=[k_sT])

    wmod_v = di["w_mod"].ap().rearrange("(kc p) n -> p kc n", p=128)
    ada = {'n': 0}

    def ada_dma(jc):
        b = jc % 3
        S.dma('pool', wmb[b][:], wmod_v[:, :, jc * 512:(jc + 1) * 512], writes=[k_wm[b]])

    def ada_chunk(jc):
        b = jc % 3
        pb = 2 + (ada['n'] % 2); ada['n'] += 1

        def mm(e):
            for kc in range(8):
                ins = e.matmul(ps[pb][0:2, :], lhsT=sT[:, kc, :], rhs=wmb[b][:, kc, :], start=(kc == 0), stop=(kc == 7))
            return ins
        S.op('pe', mm, reads=[k_sT, k_wm[b]], writes=[kps[pb]])
        sl = slice(jc * 512, (jc + 1) * 512)
        S.op('dve', lambda e: e.tensor_tensor(out=m_sb[0:2, sl], in0=ps[pb][0:2, :], in1=m_sb[0:2, sl], op=ALU.add),
             reads=[kps[pb]], writes=[k_m[jc]])
        if jc + 3 < 12:
            ada_dma(jc + 3)

    for jc in range(3):
        ada_dma(jc)
    for jc in range(4):
        ada_chunk(jc)
    S.op('dve', lambda e: e.scalar_tensor_tensor(out=rows[0:2, 0, :], in0=m_sb[0:2, D:2 * D], scalar=1.0, in1=gains2[0:2, 0, :],
                                                 op0=ALU.add, op1=ALU.mult), reads=[k_m[2], k_m[3], k_g2], writes=[k_rows[0]])

    def tpAB(which):
        srcs = [rows[0:2, which, :], m_sb[0:2, 3 * which * D:(3 * which + 1) * D]]

        def fn(e):
            for idx in range(2):
                for kc in range(8):
                    o = (idx * 8 + kc) * 2
                    ins = e.transpose(out=ps[4][:, o:o + 2], in_=srcs[idx][:, kc * 128:(kc + 1) * 128], identity=identf[0:2, 0:2])
            return ins
        S.op('pe', fn, reads=[k_rows[which], k_m[6 * which], k_m[6 * which + 1], k_identf], writes=[kps[4]])
        S.op('dve', lambda e: e.tensor_copy(out=ABT[:, 2 * which:2 * which + 2, :, :].rearrange("p a k r -> p (a k r)"), in_=ps[4][:, 0:32]),
             reads=[kps[4]], writes=[k_ABT[which]])
    tpAB(0)

    nb = {}

    def nonlocal_set(xn_list, junk_list):
        nb['xn'] = xn_list; nb['k_xn'] = [b.tk() for b in xn_list]
        nb['junk'] = junk_list; nb['k_junk'] = [b.tk() for b in junk_list]
    nonlocal_set([S.alloc("xn%d" % i, [128, D], BF16) for i in range(2)], [S.alloc("junk%d" % i, [128, D], BF16) for i in range(2)])
    cnt = {'nt': 0, 'sm': 0}

    def rstd_from_ssq(col_in, col_out, p, ksm, scale):
        S.op('act', lambda e: e.activation(out=col_out, in_=col_in, func=AF.Ln, scale=scale, bias=epsb[0:p, 0:1]),
             reads=[ksm, k_eps], writes=[ksm])
        S.op('act', lambda e: e.activation(out=col_out, in_=col_out, func=AF.Exp, scale=-0.5), reads=[ksm], writes=[ksm])

    def norm_S(tile):
        (x_ap, k_x, p, hT, k_hT, colsel, ab, r, pre) = tile
        if pre is not None:
            pre()
        i = cnt['nt']; cnt['nt'] += 1
        si = cnt['sm'] % NSM; cnt['sm'] += 1
        sm = small[0:p, si, :]; ksm = k_small[si]
        jb = nb['junk'][i % 2]; kj = nb['k_junk'][i % 2]
        xb = nb['xn'][i % 2]; kxb = nb['k_xn'][i % 2]
        S.op('act', lambda e: e.activation(out=jb[0:p, :], in_=x_ap, func=AF.Square, accum_out=sm[:, 0:1]), reads=[k_x], writes=[kj, ksm])
        rstd_from_ssq(sm[:, 0:1], sm[:, 1:2], p, ksm, 1.0 / D)
        S.op('dve', lambda e: e.tensor_scalar_mul(out=xb[0:p, :], in0=x_ap, scalar1=sm[:, 1:2]), reads=[k_x, ksm], writes=[kxb])
        return (i, xb, kxb)

    def norm_T(tile, st):
        (x_ap, k_x, p, hT, k_hT, colsel, ab, r, pre) = tile
        i, xb, kxb = st
        pb = i % 2

        def tp(e):
            for kc in range(8):
                ins = e.transpose(out=psb[pb][:, kc * 128:kc * 128 + p], in_=xb[0:p, kc * 128:(kc + 1) * 128], identity=identb[0:p, 0:p])
            return ins
        S.op('pe', tp, reads=[kxb, k_identb], writes=[kps[pb]])
        return (i, pb)

    def norm_A(tile):
        return norm_T(tile, norm_S(tile))

    def norm_B(tile, st, force_eng=None):
        (x_ap, k_x, p, hT, k_hT, colsel, ab, r, pre) = tile
        i, pb = st
        eng = 'act' if i % 3 == 2 else 'dve'
        if force_eng is not None or (eng == 'dve' and nb.get('tmpN') is not None):
            tmpN = nb['tmpN']; ktmp = nb['k_tmpN']
            if isinstance(tmpN, Buf):
                tmpN = tmpN.t[:]
            a_b = bass.AP(ABT.t, (2 * ab) * 16 + r, [[64, 128], [2, 8], [0, p]])
            b_b = bass.AP(ABT.t, (2 * ab + 1) * 16 + r, [[64, 128], [2, 8], [0, p]])
            S.op('dve', lambda e: e.tensor_tensor(out=tmpN[:, :, 0:p], in0=psb[pb][:, :].rearrange("q (k n) -> q k n", k=8)[:, :, 0:p], in1=a_b, op=ALU.mult),
                 reads=[kps[pb], k_ABT[ab]], writes=[ktmp])
            S.op('dve', lambda e: e.tensor_tensor(out=hT[:, :, colsel], in0=tmpN[:, :, 0:p], in1=b_b, op=ALU.add),
                 reads=[ktmp, k_ABT[ab]], pwrites=[k_hT])
            return
        for kc in range(8):
            o = hT[:, kc, colsel]
            i_ = psb[pb][:, kc * 128:kc * 128 + p]
            a_ = ABT[:, 2 * ab, kc, r:r + 1]
            b_ = ABT[:, 2 * ab + 1, kc, r:r + 1]
            if eng == 'act':
                S.op('act', lambda e, o=o, i_=i_, a_=a_, b_=b_: e.activation(out=o, in_=i_, func=AF.Identity, bias=b_, scale=a_),
                     reads=[kps[pb], k_ABT[ab]], pwrites=[k_hT])
            else:
                S.op('dve', lambda e, o=o, i_=i_, a_=a_, b_=b_: e.tensor_scalar(out=o, in0=i_, scalar1=a_, scalar2=b_, op0=ALU.mult, op1=ALU.add),
                     reads=[kps[pb], k_ABT[ab]], pwrites=[k_hT])

    def norm_steps(tiles, force_eng=None):
        state = {}
        thunks = []
        for k in range(len(tiles) + 1):
            def th(k=k):
                if k < len(tiles):
                    state[k] = norm_S(tiles[k])
                if k >= 1:
                    st = norm_T(tiles[k - 1], state[k - 1])
                    norm_B(tiles[k - 1], st, force_eng)
            thunks.append(th)
        return thunks

    def norm_pipeline(tiles, extras=()):
        extras = list(extras)
        prev = None
        for n, t in enumerate(tiles):
            st = norm_A(t)
            if prev is not None:
                norm_B(*prev)
            prev = (t, st)
            if n % 2 == 1 and extras:
                extras.pop(0)()
        norm_B(*prev)
        for ex in extras:
            ex()

    hT_P = S.alloc("hT_P", [128, 8, 512], BF16); k_hTP = hT_P.tks_n(4)
    hT_W = S.alloc("hT_W", [128, 8, NW], BF16); k_hTW = hT_W.tks_n(7)
    hT_O = S.alloc("hT_O", [128, 8, NO], BF16); k_hTO = hT_O.tks_n(3)
    xWs = [S.alloc("xWs%d" % i, [128, D], F32) for i in range(3)]
    k_xWs = [b.tk() for b in xWs]

    tiles = []
    for tt in range(4):
        tiles.append((xP[:, tt, :], k_xP[tt], 128, hT_P, k_hTP[tt], slice(tt * 128, (tt + 1) * 128), 0, 0, None))
    for tt in range(7):
        b = tt % 3

        def pre(tt=tt, b=b):
            S.dma('sp', xWs[b][:], di["xw"].ap()[tt * 128:(tt + 1) * 128, :], writes=[k_xWs[b]])
        tiles.append((xWs[b][:], k_xWs[b], 128, hT_W, k_hTW[tt], slice(tt * 128, (tt + 1) * 128), 0, 1, pre))
    for tt, p in OTOK:
        tiles.append((xO[0:p, tt, :], k_xO[tt], p, hT_O, k_hTO[tt], ocols(tt), 0, 1, None))
    tmpN1 = S.alloc("tmpN1", [128, 8, 128], F32)
    nb['tmpN'] = tmpN1; nb['k_tmpN'] = tmpN1.tk()
    norm_pipeline(tiles, [lambda jc=jc: ada_chunk(jc) for jc in range(4, 12)])
    nb['tmpN'] = None
    S.free(*xWs, tmpN1)

    S.op('dve', lambda e: e.scalar_tensor_tensor(out=rows[0:2, 1, :], in0=m_sb[0:2, 4 * D:5 * D], scalar=1.0, in1=gains2[0:2, 2, :],
                                                 op0=ALU.add, op1=ALU.mult), reads=[k_m[8], k_m[9], k_g2], writes=[k_rows[1]])
    S.op('dve', lambda e: e.tensor_tensor(out=rows[0:2, 2, :], in0=m_sb[0:2, 2 * D:3 * D], in1=gains2[0:2, 1, :], op=ALU.mult),
         reads=[k_m[4], k_m[5], k_g2], writes=[k_rows[2]])
    S.op('dve', lambda e: e.tensor_tensor(out=rows[0:2, 3, :], in0=m_sb[0:2, 5 * D:6 * D], in1=gains2[0:2, 3, :], op=ALU.mult),
         reads=[k_m[10], k_m[11], k_g2], writes=[k_rows[3]])
    tpAB(1)
    n = 0
    for which in range(2):
        for r in range(2):
            for hf in range(2):
                pb = 5 + (n % 2)
                n += 1
                S.op('pe', lambda e, pb=pb, which=which, r=r, hf=hf: e.matmul(
                    ps[pb][:, :], lhsT=selb[0:2, r, :], rhs=rows[0:2, 2 + which, hf * 512:(hf + 1) * 512], start=True, stop=True),
                    reads=[k_sel, k_rows[2 + which]], writes=[kps[pb]])
                evac_copy(evac_eng(), Gbc[:, which, r, hf * 512:(hf + 1) * 512], ps[pb][:, :], reads=[kps[pb]], pwrites=[k_Gbc])
    S.op('act', lambda e: e.activation(out=esT[:], in_=esT[:], func=AF.Exp), reads=[k_es], writes=[k_es])
    S.free(condT, sT, m_sb, gains2, rows, *wmb)
    S.free(*nb['xn'], *nb['junk'])
    if cut(1):
        return nc, S

    QT_P = S.alloc("QT_P", [128, 8, 512], BF16); k_QTP = QT_P.tks_n(8)
    KT_P = S.alloc("KT_P", [128, 6, 512], BF16); k_KTP = KT_P.tks_n(6)
    V_P = S.alloc("V_P", [128, 4, 10, 64], BF16); k_VP = V_P.tks_n(4)
    QT_O = S.alloc("QT_O", [128, 8, NO], BF16); k_QTO = QT_O.tks_n(8)
    KT_W = S.alloc("KT_W", [128, 6, NW], BF16); k_KTW = KT_W.tks_n(6)
    V_W = S.alloc("V_W", [128, 7, 10, 64], BF16); k_VW = V_W.tks_n(7)
    KT_C = S.alloc("KT_C", [128, 6, 256], BF16); k_KTC = KT_C.tks_n(6)
    V_C = S.alloc("V_C", [128, 2, 10, 64], BF16); k_VC = V_C.tks_n(2)
    rope_q = S.alloc("rope_q", [128, 2, NO], F32); k_rq = rope_q.tk()
    rope_k = S.alloc("rope_k", [128, 2, NW], F32); k_rk = rope_k.tk()
    wg = [S.alloc("wg%d" % i, [128, 8, 512], BF16) for i in range(3)]
    k_wg = [b.tk() for b in wg]
    wrot = S.alloc("wrot", [128, 8, 512], BF16); k_wrot = wrot.tk()
    wdup = S.alloc("wdup", [128, 8, 2, 128], BF16); k_wdup = wdup.tk()
    wdupr = S.alloc("wdupr", [128, 8, 2, 128], BF16); k_wdupr = wdupr.tk()
    stg = [S.alloc("stg%d" % i, [128, 512], F32) for i in range(3)]
    k_stg = [b.tk() for b in stg]
    rtmp = [S.alloc("rtmp%d" % i, [128, 512], F32) for i in range(2)]
    k_rtmp = [b.tk() for b in rtmp]

    win_v = di["w_in"].ap().rearrange("(kc p) n -> p kc n", p=128)
    GCOLS = [(0, 512), (512, 512), (1024, 512), (1536, 512), (2048, 256)]
    for g in range(3):
        c0, w = GCOLS[g]
        S.dma('pool', wg[g][:, :, 0:w], win_v[:, :, c0:c0 + w], writes=[k_wg[g]])
    S.dma('sp', rope_q[:], di["ropeq"].ap().rearrange("t p n -> p t n"), writes=[k_rq])
    S.dma('sp', rope_k[:], di["ropek"].ap().rearrange("t p n -> p t n"), writes=[k_rk])
    ckst = S.alloc("ckst", [128, 2, 768], BF16); k_ckst = ckst.tk()
    for t in range(2):
        S.dma('pool', ckst[:, t, 0:512], di["cak"].ap()[t * 128:(t + 1) * 128, :], pwrites=[k_ckst])
        for kv in range(2):
            for hf in range(2):
                S.dma('pool', ckst[:, t, 512 + kv * 128 + hf * 64: 512 + kv * 128 + hf * 64 + 64],
                      di["cbk"].ap()[t * 128:(t + 1) * 128, kv * 64:(kv + 1) * 64], pwrites=[k_ckst])
        S.dma('pool', V_C[:, t, 0:8, :], di["cav"].ap()[t * 128:(t + 1) * 128, :].rearrange("p (h d) -> p h d", d=64), pwrites=[k_VC[t]])
        S.dma('pool', V_C[:, t, 8:10, :], di["cbv"].ap()[t * 128:(t + 1) * 128, :].rearrange("p (h d) -> p h d", d=64), pwrites=[k_VC[t]])

    pcnt = {'i': 0, 'st': 0, 'rt': 0}

    def nxt_ps():
        pcnt['i'] += 1
        return 2 + (pcnt['i'] % 6)

    def run_fm(w_ap_fn, w_tks, hT, k_hT_list, n0, n):
        pb = nxt_ps()

        def mm(e):
            for kc in range(8):
                ins = e.matmul(ps[pb][:, 0:n], lhsT=w_ap_fn(kc), rhs=hT[:, kc, n0:n0 + n], start=(kc == 0), stop=(kc == 7))
            return ins
        S.op('pe', mm, reads=list(w_tks) + list(k_hT_list), writes=[kps[pb]])
        return pb

    def tm_proj(hT, k_hT, colsel, p, w_ap_fn, w_tks, ncols):
        pb = nxt_ps()

        def mm(e):
            for kc in range(8):
                ins = e.matmul(ps[pb][0:p, 0:ncols], lhsT=hT[:, kc, colsel], rhs=w_ap_fn(kc), start=(kc == 0), stop=(kc == 7))
            return ins
        S.op('pe', mm, reads=list(w_tks) + [k_hT], writes=[kps[pb]])
        return pb

    WSEG = [(0, 512), (512, 384)]

    g = 0
    for c in range(4):
        wf = lambda kc, c=c, g=g: wg[g][:, kc, c * 128:(c + 1) * 128]
        pb = run_fm(wf, [k_wg[g]], hT_P, k_hTP, 0, 512)
        evac_copy(evac_eng(), QT_P[:, c, :], ps[pb][:, 0:512], reads=[kps[pb]], writes=[k_QTP[c]], scale=0.125)
        pb = run_fm(wf, [k_wg[g]], hT_O, k_hTO, 0, NO)
        evac_copy(evac_eng(), QT_O[:, c, :], ps[pb][:, 0:NO], reads=[kps[pb]], writes=[k_QTO[c]], scale=0.125)
    g = 1
    for c in range(4):
        wf = lambda kc, c=c, g=g: wg[g][:, kc, c * 128:(c + 1) * 128]
        pb = run_fm(wf, [k_wg[g]], hT_P, k_hTP, 0, 512)
        evac_copy(evac_eng(), KT_P[:, c, :], ps[pb][:, 0:512], reads=[kps[pb]], writes=[k_KTP[c]])
        for (n0, n) in WSEG:
            pb = run_fm(wf, [k_wg[g]], hT_W, k_hTW, n0, n)
            evac_copy(evac_eng(), KT_W[:, c, n0:n0 + n], ps[pb][:, 0:n], reads=[kps[pb]], pwrites=[k_KTW[c]])
    for tt in range(4):
        pb = tm_proj(hT_P, k_hTP[tt], slice(tt * 128, (tt + 1) * 128), 128, lambda kc, g=g: wg[g][:, kc, :], [k_wg[g]], 512)
        si = pcnt['st'] % 3; pcnt['st'] += 1
        evac_copy(evac_eng(), stg[si][:], ps[pb][:, :], reads=[kps[pb]], writes=[k_stg[si]])
        S.dma('sp', do["nak"].ap()[tt * 128:(tt + 1) * 128, :], stg[si][:], reads=[k_stg[si]])
    S.dma('pool', wg[0][:, :, 0:512], win_v[:, :, 1536:2048], writes=[k_wg[0]])
    g = 2
    for tt in range(4):
        pb = tm_proj(hT_P, k_hTP[tt], slice(tt * 128, (tt + 1) * 128), 128, lambda kc, g=g: wg[g][:, kc, :], [k_wg[g]], 512)
        si = pcnt['st'] % 3; pcnt['st'] += 1
        evac_copy('dve', stg[si][:], ps[pb][:, :], reads=[kps[pb]], writes=[k_stg[si]])
        evac_copy('act', V_P[:, tt, 0:8, :], ps[pb][:, :].rearrange("p (h d) -> p h d", d=64), reads=[kps[pb]], pwrites=[k_VP[tt]])
        S.dma('sp', do["nav"].ap()[tt * 128:(tt + 1) * 128, :], stg[si][:], reads=[k_stg[si]])
    for tt in range(7):
        pb = tm_proj(hT_W, k_hTW[tt], slice(tt * 128, (tt + 1) * 128), 128, lambda kc, g=g: wg[g][:, kc, :], [k_wg[g]], 512)
        evac_copy(evac_eng(), V_W[:, tt, 0:8, :], ps[pb][:, :].rearrange("p (h d) -> p h d", d=64), reads=[kps[pb]], pwrites=[k_VW[tt]])
    S.dma('pool', wg[1][:, :, 0:256], win_v[:, :, 2048:2304], writes=[k_wg[1]])

    def build_rot(dst, k_dst, src, k_src):
        sv = src.rearrange("p k (a t s) -> p (k a) t s", t=2, s=16)
        dv = dst.rearrange("p k (a t s) -> p (k a) t s", t=2, s=16)
        S.op('dve', lambda e: e.tensor_scalar_mul(out=dv[:, :, 0, :], in0=sv[:, :, 1, :], scalar1=-1.0), reads=[k_src], pwrites=[k_dst])
        S.op('dve', lambda e: e.tensor_copy(out=dv[:, :, 1, :], in_=sv[:, :, 0, :]), reads=[k_src], pwrites=[k_dst])

    build_rot(wrot[:, :, :], k_wrot, wg[0][:, :, :], k_wg[0])
    for c in range(4):
        wf = lambda kc, c=c: wg[0][:, kc, c * 128:(c + 1) * 128]
        wfr = lambda kc, c=c: wrot[:, kc, c * 128:(c + 1) * 128]
        pb = run_fm(wf, [k_wg[0]], hT_P, k_hTP, 0, 512)
        evac_copy(evac_eng(), QT_P[:, 4 + c, :], ps[pb][:, 0:512], reads=[kps[pb]], writes=[k_QTP[4 + c]], scale=0.125)
        pb1 = run_fm(wf, [k_wg[0]], hT_O, k_hTO, 0, NO)
        pb2 = run_fm(wfr, [k_wrot], hT_O, k_hTO, 0, NO)
        ri = pcnt['rt'] % 2; pcnt['rt'] += 1
        ri2 = pcnt['rt'] % 2; pcnt['rt'] += 1
        S.op('dve', lambda e, pb1=pb1, ri=ri: e.scalar_tensor_tensor(out=rtmp[ri][:, 0:NO], in0=ps[pb1][:, 0:NO], scalar=0.125, in1=rope_q[:, 0, :],
                                                                     op0=ALU.mult, op1=ALU.mult), reads=[kps[pb1], k_rq], writes=[k_rtmp[ri]])
        S.op('dve', lambda e, pb2=pb2, ri2=ri2: e.scalar_tensor_tensor(out=rtmp[ri2][:, 0:NO], in0=ps[pb2][:, 0:NO], scalar=0.125, in1=rope_q[:, 1, :],
                                                                       op0=ALU.mult, op1=ALU.mult), reads=[kps[pb2], k_rq], writes=[k_rtmp[ri2]])
        S.op('pool', lambda e, ri=ri, ri2=ri2, c=c: e.tensor_tensor(out=QT_O[:, 4 + c, :], in0=rtmp[ri][:, 0:NO], in1=rtmp[ri2][:, 0:NO], op=ALU.add),
             reads=[k_rtmp[ri], k_rtmp[ri2]], writes=[k_QTO[4 + c]])
    for kv in range(2):
        for hf in range(2):
            S.op('dve', lambda e, kv=kv, hf=hf: e.tensor_copy(out=wdup[:, :, kv, hf * 64:(hf + 1) * 64], in_=wg[1][:, :, kv * 64:(kv + 1) * 64]),
                 reads=[k_wg[1]], pwrites=[k_wdup])
    build_rot(wdupr[:].rearrange("p k v n -> p k (v n)"), k_wdupr, wdup[:].rearrange("p k v n -> p k (v n)"), k_wdup)
    for kv in range(2):
        wf = lambda kc, kv=kv: wdup[:, kc, kv, :]
        wfr = lambda kc, kv=kv: wdupr[:, kc, kv, :]
        pb = run_fm(wf, [k_wdup], hT_P, k_hTP, 0, 512)
        evac_copy(evac_eng(), KT_P[:, 4 + kv, :], ps[pb][:, 0:512], reads=[kps[pb]], writes=[k_KTP[4 + kv]])
        for (n0, n) in WSEG:
            pb1 = run_fm(wf, [k_wdup], hT_W, k_hTW, n0, n)
            pb2 = run_fm(wfr, [k_wdupr], hT_W, k_hTW, n0, n)
            ri = pcnt['rt'] % 2; pcnt['rt'] += 1
            ri2 = pcnt['rt'] % 2; pcnt['rt'] += 1
            S.op('dve', lambda e, pb1=pb1, ri=ri, n0=n0, n=n: e.tensor_tensor(out=rtmp[ri][:, 0:n], in0=ps[pb1][:, 0:n], in1=rope_k[:, 0, n0:n0 + n], op=ALU.mult),
                 reads=[kps[pb1], k_rk], writes=[k_rtmp[ri]])
            S.op('dve', lambda e, pb2=pb2, ri2=ri2, n0=n0, n=n: e.tensor_tensor(out=rtmp[ri2][:, 0:n], in0=ps[pb2][:, 0:n], in1=rope_k[:, 1, n0:n0 + n], op=ALU.mult),
                 reads=[kps[pb2], k_rk], writes=[k_rtmp[ri2]])
            S.op('pool', lambda e, ri=ri, ri2=ri2, n0=n0, n=n, kv=kv: e.tensor_tensor(out=KT_W[:, 4 + kv, n0:n0 + n], in0=rtmp[ri][:, 0:n], in1=rtmp[ri2][:, 0:n], op=ALU.add),
                 reads=[k_rtmp[ri], k_rtmp[ri2]], pwrites=[k_KTW[4 + kv]])
    for tt in range(4):
        pb = tm_proj(hT_P, k_hTP[tt], slice(tt * 128, (tt + 1) * 128), 128, lambda kc: wg[1][:, kc, 0:256], [k_wg[1]], 256)
        si = pcnt['st'] % 3; pcnt['st'] += 1
        evac_copy('dve', stg[si][:, 0:256], ps[pb][:, 0:256], reads=[kps[pb]], writes=[k_stg[si]])
        evac_copy('act', V_P[:, tt, 8:10, :], ps[pb][:, 128:256].rearrange("p (h d) -> p h d", d=64), reads=[kps[pb]], pwrites=[k_VP[tt]])
        S.dma('sp', do["nbk"].ap()[tt * 128:(tt + 1) * 128, :], stg[si][:, 0:128], reads=[k_stg[si]])
        S.dma('sp', do["nbv"].ap()[tt * 128:(tt + 1) * 128, :], stg[si][:, 128:256], reads=[k_stg[si]])
    for tt in range(7):
        pb = tm_proj(hT_W, k_hTW[tt], slice(tt * 128, (tt + 1) * 128), 128, lambda kc: wg[1][:, kc, 128:256], [k_wg[1]], 128)
        evac_copy(evac_eng(), V_W[:, tt, 8:10, :], ps[pb][:, 0:128].rearrange("p (h d) -> p h d", d=64), reads=[kps[pb]], pwrites=[k_VW[tt]])
    for t in range(2):
        pb = nxt_ps()

        def tpc(e, t=t, pb=pb):
            for c in range(6):
                ins = e.transpose(out=psb[pb][:, c * 128:(c + 1) * 128], in_=ckst[:, t, c * 128:(c + 1) * 128], identity=identb[:])
            return ins
        S.op('pe', tpc, reads=[k_ckst, k_identb], writes=[kps[pb]])
        evac_copy(evac_eng(), KT_C[:, :, t * 128:(t + 1) * 128], psb[pb][:, 0:768].rearrange("p (c n) -> p c n", n=128), reads=[kps[pb]],
                  pwrites=k_KTC)
    S.free(hT_P, hT_W, hT_O, rope_q, rope_k, wrot, wdup, wdupr, ckst, *wg, *stg, *rtmp)
    if cut(2):
        return nc, S

    wo = S.alloc("wo", [128, 8, D], BF16); k_wo = wo.tk()
    S.dma('pool', wo[:], di["w_out"].ap().rearrange("(kc p) n -> p kc n", p=128), writes=[k_wo])
    NPT = 3
    PT = [[S.alloc("PT%d_%d" % (i, j), [128, 512], BF16) for j in range(2)] for i in range(NPT)]
    k_PT = [[b.tk() for b in row] for row in PT]
    SB = [[S.alloc("SB%d_%d" % (i, j), [128, NO], F32) for j in range(2)] for i in range(2)]
    k_SB = [[b.tk() for b in row] for row in SB]
    OT = S.alloc("OT", [128, 8, NO], F32); k_OT = OT.tks_n(8)
    OTn_b = [S.alloc("OTn%d" % i, [128, 8, NO], BF16) for i in range(2)]
    k_OTn_b = [b.tks_n(8) for b in OTn_b]
    sq = S.alloc("sq", [128, 8, NO], BF16); k_sq = sq.tks_n(8)
    Rb = [S.alloc("Rb%d" % i, [128, NO], F32) for i in range(2)]
    k_Rb = [b.tk() for b in Rb]
    rg = S.alloc("rg", [128, NO], F32); k_rg = rg.tk()
    ytmp = [S.alloc("ytmp%d" % i, [128, D], F32) for i in range(2)]
    k_ytmp = [b.tk() for b in ytmp]
    yjunk = [S.alloc("yjunk%d" % i, [128, 512], BF16) for i in range(2)]
    k_yjunk = [b.tk() for b in yjunk]
    HB = S.alloc("HB", [128, 7, 8, NO], BF16); k_HB = HB.tks_n(7)
    SWM = S.alloc("SWM", [128, 7, NO], BF16); k_SWM = SWM.tk()
    NAM = S.alloc("NAM", [128, 7, NO], BF16); k_NAM = NAM.tk()
    hbst = [S.alloc("hbst%d" % i, [128, 8, NO], F32) for i in range(2)]
    k_hbst = [b.tk() for b in hbst]
    S.dma('pool', SWM[:], di["swmask"].ap().rearrange("k p n -> p k n"), writes=[k_SWM])
    S.dma('pool', NAM[:], di["namask"].ap().rearrange("k p n -> p k n"), writes=[k_NAM])
    rs = di["rpbsrc"]
    def hb_step(kt):
        b = kt % 2
        for a in range(2):
            u0 = 13 - 2 * kt + a
            for qr in range(4):
                S.dma('sp', hbst[b][64 * a:64 * a + 64, :, 1 + 64 * qr:65 + 64 * qr],
                      bass.AP(rs, (u0 + qr) * 127, [[1, 64], [19 * 127, 8], [1, 64]]), pwrites=[k_hbst[b]])
            S.dma('sp', hbst[b][64 * a:64 * a + 64, :, 0:1],
                  bass.AP(rs, (u0 - 1) * 127 + 63, [[1, 64], [19 * 127, 8], [1, 1]]), pwrites=[k_hbst[b]], allow_slow_non_contiguous=True)
            S.dma('sp', hbst[b][64 * a:64 * a + 64, :, 257:258],
                  bass.AP(rs, (u0 + 4) * 127, [[1, 64], [19 * 127, 8], [1, 1]]), pwrites=[k_hbst[b]], allow_slow_non_contiguous=True)
        nam_b = bass.AP(NAM.t, kt * NO, [[7 * NO, 128], [0, 8], [1, NO]])
        S.op('pool', lambda e, kt=kt, b=b, nam_b=nam_b: e.tensor_tensor(out=HB[:, kt, :, :], in0=hbst[b][:], in1=nam_b, op=ALU.add),
             reads=[k_hbst[b], k_NAM], writes=[k_HB[kt]])
    if cut(3):
        return nc, S

    acnt = {'s': 0, 'pt': 0, 'od': 0, 'od4': 0, 'rb': 0, 'y': 0}

    def s_slot():
        sl = acnt['s'] % 2; acnt['s'] += 1
        return sl

    def attention(dus):
        LAG = 2
        FINLAG = 1
        FINLAG_B = 4
        finq = []
        finq_b = []
        pend = []
        pair_od = {}
        for idx in range(len(dus) + LAG):
            if idx < len(dus):
                u = dus[idx]
                sl = s_slot()
                pt = acnt['pt'] % NPT; acnt['pt'] += 1
                b0, b1 = 2 * sl, 2 * sl + 1
                S.op('pe', lambda e, u=u, b0=b0, b1=b1: u['qk'](e, ps[b0], ps[b1]), reads=u['qk_reads'], writes=[kps[b0], kps[b1]])
                n = u['n']
                for h2 in range(2):
                    bk = (b0, b1)[h2]
                    bias = u['bias'][h2]
                    if bias is not None:
                        bap, btk = bias
                        S.op('dve', lambda e, bk=bk, sl=sl, h2=h2, bap=bap, n=n: e.tensor_tensor(out=SB[sl][h2][:, 0:n], in0=ps[bk][:, 0:n], in1=bap, op=ALU.add),
                             reads=[kps[bk], btk], writes=[k_SB[sl][h2]])
                        S.op('act', lambda e, sl=sl, h2=h2, pt=pt, n=n: e.activation(out=PT[pt][h2][:, 0:n], in_=SB[sl][h2][:, 0:n], func=AF.Exp),
                             reads=[k_SB[sl][h2]], writes=[k_PT[pt][h2]])
                    else:
                        S.op('act', lambda e, bk=bk, h2=h2, pt=pt, n=n: e.activation(out=PT[pt][h2][:, 0:n], in_=ps[bk][:, 0:n], func=AF.Exp),
                             reads=[kps[bk]], writes=[k_PT[pt][h2]])
                pend.append((u, pt))
            if idx >= LAG:
                u, pt = pend[idx - LAG]
                pi = u['pair']
                if pi not in pair_od:
                    if u['n'] == 512:
                        b = 4 + (acnt['od4'] % 4); acnt['od4'] += 1
                        pair_od[pi] = (b, b, ps[b][:, 0:256], ps[b][:, 256:512])
                    else:
                        od = acnt['od'] % 2; acnt['od'] += 1
                        pair_od[pi] = (4 + od, 6 + od, ps[4 + od], ps[6 + od])
                bO, bD, apO, apD = pair_od[pi]
                first = u['first']
                bks = [kps[bO]] if bO == bD else [kps[bO], kps[bD]]
                S.op('pe', lambda e, u=u, pt=pt, apO=apO, apD=apD: u['pv'](e, PT[pt][0], PT[pt][1], apO, apD),
                     reads=[k_PT[pt][0], k_PT[pt][1]] + u['pv_reads'],
                     writes=(bks if first else []), pwrites=([] if first else bks))
                if u['last']:
                    finq.append((idx + FINLAG, u['fin'], pair_od[pi]))
                    if u.get('fin_b') is not None:
                        finq_b.append((idx + FINLAG_B, u['fin_b']))
                for hk in u['hooks']:
                    hk()
            while finq and (finq[0][0] <= idx or idx == len(dus) + LAG - 1):
                _, f, od_ = finq.pop(0)
                f(od_)
            while finq_b and (finq_b[0][0] <= idx or idx == len(dus) + LAG - 1):
                _, f = finq_b.pop(0)
                f()

    def finish_pair_common(i, od, N, is_b):
        bO, bD, apO, apD = od
        rb = acnt['rb'] % 2; acnt['rb'] += 1
        if is_b:
            S.op('act', lambda e: e.activation(out=Rb[rb][:, 0:N], in_=apD[:, 0:N], func=AF.Ln, bias=esT[:, i - 4:i - 3]),
                 reads=[kps[bD], k_es], writes=[k_Rb[rb]])
        else:
            S.op('act', lambda e: e.activation(out=Rb[rb][:, 0:N], in_=apD[:, 0:N], func=AF.Ln), reads=[kps[bD]], writes=[k_Rb[rb]])
        S.op('act', lambda e: e.activation(out=Rb[rb][:, 0:N], in_=Rb[rb][:, 0:N], func=AF.Exp, scale=-1.0), reads=[k_Rb[rb]], writes=[k_Rb[rb]])
        S.op('dve', lambda e: e.tensor_tensor(out=OT[:, i, 0:N], in0=apO[:, 0:N], in1=Rb[rb][:, 0:N], op=ALU.mult),
             reads=[kps[bO], k_Rb[rb]], writes=[k_OT[i]])
        S.op('dve', lambda e: e.tensor_tensor(out=sq[:, i, 0:N], in0=OT[:, i, 0:N], in1=OT[:, i, 0:N], op=ALU.mult),
             reads=[k_OT[i]], writes=[k_sq[i]])

    def group_norm(grp, N, ob):
        OTn = OTn_b[ob]; k_OTn = k_OTn_b[ob]
        sb = 2 * s_slot()

        def mm(e):
            for c in range(4):
                ins = e.matmul(ps[sb][:, 0:N], lhsT=onesb[:], rhs=sq[:, grp * 4 + c, 0:N], start=(c == 0), stop=(c == 3))
            return ins
        S.op('pe', mm, reads=[k_onesb] + k_sq[grp * 4:grp * 4 + 4], writes=[kps[sb]])
        S.op('act', lambda e: e.activation(out=rg[:, 0:N], in_=ps[sb][:, 0:N], func=AF.Ln, scale=1.0 / 512, bias=epsb[:, 0:1]),
             reads=[kps[sb], k_eps], writes=[k_rg])
        S.op('act', lambda e: e.activation(out=rg[:, 0:N], in_=rg[:, 0:N], func=AF.Exp, scale=-0.5), reads=[k_rg], writes=[k_rg])
        for c in range(4):
            i = grp * 4 + c
            S.op('dve', lambda e, i=i: e.scalar_tensor_tensor(out=OTn[:, i, 0:N], in0=OT[:, i, 0:N], scalar=ggT[:, i:i + 1], in1=rg[:, 0:N],
                                                              op0=ALU.mult, op1=ALU.mult), reads=[k_OT[i], k_gg, k_rg], writes=[k_OTn[i]])

    def post_norm_residual(x_ap, k_x, p, mm_fn, mm_reads, which, r, banks):
        yi = acnt['y'] % 2; acnt['y'] += 1
        si = cnt['sm'] % NSM; cnt['sm'] += 1
        sm = small[0:p, si, :]; ksm = k_small[si]
        for hf in range(2):
            pb = banks[hf]
            S.op('pe', lambda e, hf=hf, pb=pb: mm_fn(e, ps[pb], hf), reads=mm_reads, writes=[kps[pb]])
            S.op('act', lambda e, hf=hf, pb=pb: e.activation(out=yjunk[hf][0:p, :], in_=ps[pb][0:p, :], func=AF.Square, accum_out=sm[:, hf:hf + 1]),
                 reads=[kps[pb]], writes=[k_yjunk[hf]], pwrites=[ksm])
            S.op('dve', lambda e, hf=hf, pb=pb: e.tensor_copy(out=ytmp[yi][0:p, hf * 512:(hf + 1) * 512], in_=ps[pb][0:p, :]),
                 reads=[kps[pb]], pwrites=[k_ytmp[yi]])
        S.op('dve', lambda e: e.tensor_tensor(out=sm[:, 2:3], in0=sm[:, 0:1], in1=sm[:, 1:2], op=ALU.add), reads=[ksm], writes=[ksm])
        rstd_from_ssq(sm[:, 2:3], sm[:, 3:4], p, ksm, 1.0 / D)
        S.op('dve', lambda e: e.scalar_tensor_tensor(out=ytmp[yi][0:p, :], in0=ytmp[yi][0:p, :], scalar=sm[:, 3:4], in1=Gbc[0:p, which, r, :],
                                                     op0=ALU.mult, op1=ALU.mult), reads=[ksm, k_Gbc], writes=[k_ytmp[yi]])
        S.op('dve', lambda e: e.tensor_tensor(out=x_ap, in0=x_ap, in1=ytmp[yi][0:p, :], op=ALU.add), reads=[k_ytmp[yi]], writes=[k_x])

    def wout_tile(x_ap, k_x, p, cols, r, ob):
        OTn = OTn_b[ob]; k_OTn = k_OTn_b[ob]

        def mm_fn(e, pst, hf):
            for c in range(8):
                ins = e.matmul(pst[0:p, :], lhsT=OTn[:, c, cols], rhs=wo[:, c, hf * 512:(hf + 1) * 512], start=(c == 0), stop=(c == 7))
            return ins
        sl = s_slot()
        post_norm_residual(x_ap, k_x, p, mm_fn, [k_wo] + k_OTn, 0, r, (2 * sl, 2 * sl + 1))

    def head_src(i, pi_):
        if i < 4:
            return i, 2 * i + pi_, 2 * i + pi_
        hb = 2 * (i - 4) + pi_
        kvh = hb // 4
        return 4 + kvh, 8 + kvh, None

    all_dus = []
    for s in range(2):
        for i in range(8):
            src = [head_src(i, 0), head_src(i, 1)]

            def qk(e, pA, pB, i=i, s=s, src=src):
                for kt in range(2):
                    for pi_, pst in ((0, pA), (1, pB)):
                        lo = 64 * pi_
                        kc = src[pi_][0]
                        ins = e.matmul(pst[:, kt * 256:(kt + 1) * 256], lhsT=KT_P[lo:lo + 64, kc, s * 256 + kt * 128: s * 256 + (kt + 1) * 128],
                                       rhs=QT_P[lo:lo + 64, i, s * 256:(s + 1) * 256], start=True, stop=True)
                return ins

            def pv(e, pt0, pt1, pO, pD, s=s, src=src):
                for kt in range(2):
                    for pi_, ptb in ((0, pt0), (1, pt1)):
                        lo = 64 * pi_
                        e.matmul(pO[lo:lo + 64, 0:256], lhsT=V_P[:, 2 * s + kt, src[pi_][1], :], rhs=ptb[:, kt * 256:(kt + 1) * 256],
                                 start=(kt == 0), stop=(kt == 1), tile_position=(0, lo))
                for kt in range(2):
                    for pi_, ptb in ((0, pt0), (1, pt1)):
                        lo = 64 * pi_
                        ins = e.matmul(pD[lo:lo + 64, 0:256], lhsT=ones64[:], rhs=ptb[:, kt * 256:(kt + 1) * 256],
                                       start=(kt == 0), stop=(kt == 1), tile_position=(0, lo))
                return ins

            def fin(od, i=i, s=s):
                finish_pair_common(i, od, 256, i >= 4)
                if s == 0 and i < 7:
                    hb_step(i)
            fin_b = (lambda i=i, s=s: group_norm(i // 4, 256, s)) if i % 4 == 3 else None
            all_dus.append(dict(qk=qk, qk_reads=[k_KTP[src[0][0]], k_KTP[src[1][0]], k_QTP[i]], pv=pv,
                                pv_reads=[k_VP[2 * s], k_VP[2 * s + 1], k_ones64], n=512, bias=(None, None), pair=(s, i),
                                first=True, last=True, fin=fin, fin_b=fin_b, hooks=[]))

    def wout_prompt(s):
        for t2 in range(2):
            tt = 2 * s + t2
            wout_tile(xP[:, tt, :], k_xP[tt], 128, slice(t2 * 128, (t2 + 1) * 128), 0, s)
    all_dus[8 + 4]['hooks'].append(lambda: wout_prompt(0))

    for i in (4, 5, 6, 7, 0, 1, 2, 3):
        src = [head_src(i, 0), head_src(i, 1)]
        for kt in range(9):
            if kt < 7:
                ks = [KT_W[64 * p_:64 * p_ + 64, src[p_][0], kt * 128:(kt + 1) * 128] for p_ in range(2)]
                kk = [k_KTW[src[0][0]], k_KTW[src[1][0]]]
                vs = [V_W[:, kt, src[p_][1], :] for p_ in range(2)]
                kv_ = k_VW[kt]
                if i < 4:
                    bias = (None, None)
                    hb = [HB[:, kt, src[p_][2], :] for p_ in range(2)]
                    kk = kk + [k_antib, k_HB[kt]]
                else:
                    bias = ((SWM[:, kt, :], k_SWM), (SWM[:, kt, :], k_SWM))
                    hb = None
            else:
                ks = [KT_C[64 * p_:64 * p_ + 64, src[p_][0], (kt - 7) * 128:(kt - 6) * 128] for p_ in range(2)]
                kk = [k_KTC[src[0][0]], k_KTC[src[1][0]]]
                vs = [V_C[:, kt - 7, src[p_][1], :] for p_ in range(2)]
                kv_ = k_VC[kt - 7]
                bias = (None, None)
                hb = None

            def qk(e, pA, pB, i=i, ks=ks, hb=hb):
                e.matmul(pA[:, 0:NO], lhsT=ks[0], rhs=QT_O[0:64, i, :], start=True, stop=(hb is None))
                ins = e.matmul(pB[:, 0:NO], lhsT=ks[1], rhs=QT_O[64:128, i, :], start=True, stop=(hb is None))
                if hb is not None:
                    e.matmul(pA[:, 0:NO], lhsT=antib[:], rhs=hb[0], start=False, stop=True)
                    ins = e.matmul(pB[:, 0:NO], lhsT=antib[:], rhs=hb[1], start=False, stop=True)
                return ins

            def pv(e, pt0, pt1, pO, pD, vs=vs, kt=kt):
                e.matmul(pO[0:64, 0:NO], lhsT=vs[0], rhs=pt0[:, 0:NO], start=(kt == 0), stop=(kt == 8), tile_position=(0, 0))
                e.matmul(pO[64:128, 0:NO], lhsT=vs[1], rhs=pt1[:, 0:NO], start=(kt == 0), stop=(kt == 8), tile_position=(0, 64))
                e.matmul(pD[0:64, 0:NO], lhsT=ones64[:], rhs=pt0[:, 0:NO], start=(kt == 0), stop=(kt == 8), tile_position=(0, 0))
                return e.matmul(pD[64:128, 0:NO], lhsT=ones64[:], rhs=pt1[:, 0:NO], start=(kt == 0), stop=(kt == 8), tile_position=(0, 64))

            def fin_s(od, i=i):
                finish_pair_common(i, od, NO, i >= 4)
            fin_b = (lambda i=i: group_norm(i // 4, NO, 0)) if (i % 4 == 3 and kt == 8) else None
            all_dus.append(dict(qk=qk, qk_reads=kk + [k_QTO[i]], pv=pv, pv_reads=[kv_, k_ones64], n=NO, bias=bias, pair=(2, i),
                                first=(kt == 0), last=(kt == 8), fin=fin_s, fin_b=fin_b, hooks=[]))
    all_dus[16 + 5]['hooks'].append(lambda: wout_prompt(1))
    ffn_pre = {}

    def np_setup():
        S.free(*hbst, NAM)
        ffn_pre['xn'] = [S.alloc("xn%d" % i, [128, D], BF16) for i in range(2)]
        ffn_pre['junk'] = [S.alloc("junk%d" % i, [128, D], BF16) for i in range(2)]
        ffn_pre['h2P'] = S.alloc("h2P", [128, 8, 512], BF16)
        ffn_pre['k_h2P'] = ffn_pre['h2P'].tks_n(4)
        nonlocal_set(ffn_pre['xn'], ffn_pre['junk'])
        ffn_pre['tmpN'] = S.alloc("tmpN", [128, 8, 128], F32)
        nb['tmpN'] = ffn_pre['tmpN']; nb['k_tmpN'] = ffn_pre['tmpN'].tk()
        tl = []
        for tt in range(4):
            tl.append((xP[:, tt, :], k_xP[tt], 128, ffn_pre['h2P'], ffn_pre['k_h2P'][tt], slice(tt * 128, (tt + 1) * 128), 1, 0, None))
        ffn_pre['thunks'] = norm_steps(tl, 'dve')
    def wu_early():
        S.free(QT_P, KT_P, V_P)
        ffn_pre['wu'] = [S.alloc("wu%d" % i, [128, 8, 2, 128], BF16) for i in range(4)]
        ffn_pre['k_wu'] = [b.tk() for b in ffn_pre['wu']]
        for g in range(3):
            for gv in range(2):
                S.dma('pool', ffn_pre['wu'][g][:, :, gv, :], wup_v[:, :, gv * DFF + g * 128:gv * DFF + (g + 1) * 128], pwrites=[ffn_pre['k_wu'][g]])
    wup_v = di["w_up"].ap().rearrange("(kc p) n -> p kc n", p=128)
    all_dus[16 + 20]['hooks'].append(wu_early)
    all_dus[16 + 36]['hooks'].append(np_setup)
    for k in range(5):
        all_dus[16 + 38 + 6 * k]['hooks'].append(lambda k=k: ffn_pre['thunks'][k]())

    def np_done():
        S.free(ffn_pre['tmpN'])
        nb['tmpN'] = ytmp[0].t[:].rearrange("p (k n) -> p k n", k=8)
        nb['k_tmpN'] = k_ytmp[0]
    all_dus[16 + 38 + 6 * 4]['hooks'].append(np_done)
    attention(all_dus)
    for tt, p in OTOK:
        wout_tile(xO[0:p, tt, :], k_xO[tt], p, ocols(tt), 1, 0)
    S.free(QT_O, KT_W, V_W, KT_C, V_C, HB, SWM, OT, sq, rg, wo, *Rb, *OTn_b)
    for row in PT:
        S.free(*row)
    for row in SB:
        S.free(*row)
    if cut(5):
        return nc, S

    wd = S.alloc("wd", [128, NPAIR, D], BF16); k_wd = wd.tk()
    h2P = ffn_pre['h2P']; k_h2P = ffn_pre['k_h2P']
    h2O = S.alloc("h2O", [128, 8, NO], BF16); k_h2O = h2O.tks_n(3)
    aP = S.alloc("aP", [128, NPAIR, 512], BF16); k_aP = aP.tks_n(NPAIR)
    aO = S.alloc("aO", [128, NPAIR, 256], BF16); k_aO = aO.tks_n(NPAIR)
    NWU = 7
    LAGO = 3
    wu = ffn_pre['wu'] + [S.alloc("wu%d" % i, [128, 8, 2, 128], BF16) for i in range(4, NWU)]
    k_wu = ffn_pre['k_wu'] + [b.tk() for b in wu[4:]]

    def load_wu(g):
        b = g % NWU
        S.dma('pool', wu[b][:, :, 0, :], wup_v[:, :, g * 128:(g + 1) * 128], pwrites=[k_wu[b]])
        S.dma('pool', wu[b][:, :, 1, :], wup_v[:, :, DFF + g * 128:DFF + (g + 1) * 128], pwrites=[k_wu[b]])
    wd_v = di["w_down"].ap().rearrange("(g p) n -> p g n", p=128)

    def load_wd(q):
        lo, hi = [(0, 6), (6, 12), (12, 17), (17, 22)][q]
        S.dma('pool', wd[:, lo:hi, :], wd_v[:, lo:hi, :], pwrites=[k_wd])
    tiles = []
    for tt, p in OTOK:
        tiles.append((xO[0:p, tt, :], k_xO[tt], p, h2O, k_h2O[tt], ocols(tt), 1, 1, None))
    ns_thunks = norm_steps(tiles, 'dve')

    def ns_flags():
        S.op('dve', lambda e: e.tensor_scalar_mul(out=h2O[:, :, 0:1], in0=h2O[:, :, 0:1], scalar1=flg[:, 0:1]), reads=[k_flg], writes=[k_h2O[2]])
        S.op('dve', lambda e: e.tensor_scalar_mul(out=h2O[:, :, NO - 1:NO], in0=h2O[:, :, NO - 1:NO], scalar1=flg[:, 1:2]), reads=[k_flg], writes=[k_h2O[2]])

    NT = 3
    pend_ep = []
    ta = [[S.alloc("ta%d_%d" % (i, j), [128, 512], F32) for j in range(2)] for i in range(NT)]
    k_ta = [[b.tk() for b in row] for row in ta]
    tb = [[S.alloc("tb%d_%d" % (i, j), [128, 512], F32) for j in range(2)] for i in range(2)]
    k_tb = [[b.tk() for b in row] for row in tb]
    fcnt = {'t': 0}

    def conv_epilogue(pbg, pbv, g, N, is_p):
        ti = fcnt['t'] % NT; fcnt['t'] += 1
        for gv, (pb, ch) in enumerate(((pbg, g), (pbv, NPAIR + g))):
            a_ = ta[ti][gv]; ka = k_ta[ti][gv]
            b_ = tb[ti % 2][gv]; kb = k_tb[ti % 2][gv]
            if is_p:
                a3 = a_[:, 0:512].rearrange("p (s n) -> p s n", s=2)
                b3 = b_[:, 0:512].rearrange("p (s n) -> p s n", s=2)
                u3 = ps[pb][:, 0:512].rearrange("p (s n) -> p s n", s=2)
                a_hi, a_lo, b_hi, u_lo, u_hi = a3[:, :, 1:256], a3[:, :, 0:255], b3[:, :, 1:256], u3[:, :, 0:255], u3[:, :, 1:256]
            else:
                a_hi, a_lo, b_hi, u_lo, u_hi = a_[:, 1:NO], a_[:, 0:NO - 1], b_[:, 1:NO], ps[pb][:, 0:NO - 1], ps[pb][:, 1:NO]
            S.op('act', lambda e, a_=a_, pb=pb, ch=ch: e.activation(out=a_[:, 0:N], in_=ps[pb][:, 0:N], func=AF.Identity,
                                                                    bias=cbT[:, ch:ch + 1], scale=cwT[:, ch, 1:2]),
                 reads=[kps[pb], k_cw, k_cb], writes=[ka])
            S.op('act', lambda e, b_hi=b_hi, u_lo=u_lo, ch=ch: e.activation(out=b_hi, in_=u_lo, func=AF.Copy, scale=cwT[:, ch, 0:1]),
                 reads=[kps[pb], k_cw], writes=[kb])
            S.op('dve', lambda e, a_lo=a_lo, u_hi=u_hi, ch=ch: e.scalar_tensor_tensor(out=a_lo, in0=u_hi, scalar=cwT[:, ch, 2:3], in1=a_lo,
                                                                                      op0=ALU.mult, op1=ALU.add), reads=[kps[pb], k_cw], writes=[ka])
            S.op('dve' if is_p else 'pool', lambda e, a_hi=a_hi, b_hi=b_hi: e.tensor_tensor(out=a_hi, in0=a_hi, in1=b_hi, op=ALU.add), reads=[kb], writes=[ka])
        pend_ep.append((ti, g, N, is_p))
        if len(pend_ep) > 1:
            conv_part2(*pend_ep.pop(0))

    def conv_part2(ti, g, N, is_p):
        S.op('act', lambda e: e.activation(out=ta[ti][0][:, 0:N], in_=ta[ti][0][:, 0:N], func=AF.Silu), reads=[k_ta[ti][0]], writes=[k_ta[ti][0]])
        if is_p:
            S.op('dve', lambda e: e.tensor_tensor(out=aP[:, g, :], in0=ta[ti][0][:, 0:512], in1=ta[ti][1][:, 0:512], op=ALU.mult),
                 reads=[k_ta[ti][0], k_ta[ti][1]], writes=[k_aP[g]])
        else:
            S.op('dve', lambda e: e.tensor_tensor(out=aO[:, g, :], in0=ta[ti][0][:, 1:257], in1=ta[ti][1][:, 1:257], op=ALU.mult),
                 reads=[k_ta[ti][0], k_ta[ti][1]], writes=[k_aO[g]])

    def ffn_mm(g, hT, khT, N, off):
        b = g % NWU
        base = 4 * (g % 2)
        for gv in range(2):
            pb = base + off + gv

            def mm(e, pb=pb, gv=gv):
                for kc in range(8):
                    ins = e.matmul(ps[pb][:, 0:N], lhsT=wu[b][:, kc, gv, :], rhs=hT[:, kc, 0:N], start=(kc == 0), stop=(kc == 7))
                return ins
            S.op('pe', mm, reads=[k_wu[b]] + khT, writes=[kps[pb]])
        conv_epilogue(base + off, base + off + 1, g, N, off == 0)

    for g in range(NPAIR + LAGO):
        if g < NPAIR:
            if g + 3 < NPAIR:
                load_wu(g + 3)
            if g in (5, 9, 13, 17):
                load_wd((g - 5) // 4)
            ffn_mm(g, h2P, k_h2P, 512, 0)
        if g < len(ns_thunks):
            ns_thunks[g]()
            if g == len(ns_thunks) - 1:
                ns_flags()
                S.free(*nb['xn'], *nb['junk'])
        if g >= LAGO:
            ffn_mm(g - LAGO, h2O, k_h2O, NO, 2)
    while pend_ep:
        conv_part2(*pend_ep.pop(0))
    S.free(h2P, h2O, *wu)
    for row in ta:
        S.free(*row)
    for row in tb:
        S.free(*row)
    if cut(6):
        return nc, S

    dcnt = {'b': 0}

    def wdown_tile(x_ap, k_x, aT, k_aT, cols, r, out_ap):
        def mm_fn(e, pst, hf):
            for g in range(NPAIR):
                ins = e.matmul(pst[:, :], lhsT=aT[:, g, cols], rhs=wd[:, g, hf * 512:(hf + 1) * 512], start=(g == 0), stop=(g == NPAIR - 1))
            return ins
        b0 = 2 * (dcnt['b'] % 4); dcnt['b'] += 1
        post_norm_residual(x_ap, k_x, 128, mm_fn, [k_wd] + k_aT, 1, r, (b0, b0 + 1))
        S.dma('sp', out_ap, x_ap, reads=[k_x])

    for tt in range(4):
        wdown_tile(xP[:, tt, :], k_xP[tt], aP, k_aP, slice(tt * 128, (tt + 1) * 128), 0, do["y_p"].ap()[tt * 128:(tt + 1) * 128, :])
    for tt in range(2):
        wdown_tile(xO[:, tt, :], k_xO[tt], aO, k_aO, slice(tt * 128, (tt + 1) * 128), 1, do["y_s"].ap()[tt * 128:(tt + 1) * 128, :])

    S.finish(list(ALL_TKS))
    return nc, S


def _host_consts(j):
    ws = [0, 0, 2, 2][j]
    qpos = np.concatenate([[256 * j - 1], 256 * j + np.arange(256), [256 * j + 256]])
    qvalid = (qpos >= 0) & (qpos < 1024)
    qp = np.clip(qpos, 0, 1023)
    kpos = 64 * ws + np.arange(NW)
    r, c = qp // 64, qp % 64
    rk, ck = kpos // 64, kpos % 64
    row_start = np.clip(r - 4, 0, 8)
    col_start = np.clip(c - 8, 0, 48)
    na_valid = ((rk[:, None] >= row_start[None, :]) & (rk[:, None] < row_start[None, :] + 8) &
                (ck[:, None] >= col_start[None, :]) & (ck[:, None] < col_start[None, :] + 16) & qvalid[None, :])
    na = np.where(na_valid, 0.0, NEG).astype(np.float32).reshape(7, 128, NO)
    namask = np.ascontiguousarray(na[:, ::-1, :])
    sw_valid = (np.abs(qp[None, :] - kpos[:, None]) <= 128) & qvalid[None, :]
    swmask = np.where(sw_valid, 0.0, NEG).astype(np.float32).reshape(7, 128, NO)

    def rope_tab(pos):
        n = 16
        inv = (1.0 / (10000.0 ** (np.arange(n, dtype=np.float32) / n))).astype(np.float32)
        pr = (pos // 64).astype(np.float32)
        pc = (pos % 64).astype(np.float32)
        cos = np.zeros((64, len(pos)), np.float32)
        sin = np.zeros((64, len(pos)), np.float32)
        for d in range(64):
            p_ = pr if d < 32 else pc
            ang = (p_ * inv[d % 16]).astype(np.float32)
            cos[d] = np.cos(ang)
            sin[d] = np.sin(ang)
        return np.stack([np.concatenate([cos, cos], 0), np.concatenate([sin, sin], 0)]).astype(np.float32)
    ropeq = rope_tab(qp)
    ropek = rope_tab(kpos)
    flags = np.zeros((128, 2), np.float32)
    flags[:, 0] = 1.0 if j > 0 else 0.0
    flags[:, 1] = 1.0 if j < 3 else 0.0
    return ws, namask, swmask, ropeq, ropek, flags


def _rpbsrc(rpb, j, ws):
    out = np.zeros((8, 19, 127), np.float32)
    delta = ws - 4 * j
    for up in range(19):
        u = 18 - up
        i = (u - 4) + delta + 7
        if 0 <= i < 15:
            out[:, up, 48:79] = rpb[:, i, ::-1]
    return out


_CACHE = {}


def kernel(x_prompt, x_sample, cache_a_k, cache_a_v, cache_b_k, cache_b_v, c, c_ctx,
           w_mod, b_mod, g_mix_pre, g_mix_post, g_ffn_pre, g_ffn_post, w_in, rpb_a, sink_b,
           g_grp_a, g_grp_b, w_out, w_up, conv_w, conv_b, w_down):
    f = lambda a: np.ascontiguousarray(np.asarray(a, dtype=np.float32))
    x_prompt, x_sample = f(x_prompt), f(x_sample)
    if 'nc' not in _CACHE:
        _CACHE['nc'] = build_program()
    nc, S = _CACHE['nc']
    shared = {
        "w_mod": f(w_mod[0]), "b_mod": f(b_mod[0]),
        "gains": f(np.concatenate([g_mix_pre[0], g_mix_post[0], g_ffn_pre[0], g_ffn_post[0]])),
        "w_in": f(w_in[0]), "w_out": f(w_out[0]), "w_up": f(w_up[0]), "w_down": f(w_down[0]),
        "cwT": f(np.asarray(conv_w[0]).T.reshape(44, 128, 3).transpose(1, 0, 2)),
        "cbT": f(np.asarray(conv_b[0]).reshape(44, 128).T),
        "ggT": f(np.concatenate([g_grp_a[0], g_grp_b[0]]).reshape(8, 128).T),
        "ident": np.eye(128, dtype=np.float32),
        "antij": np.ascontiguousarray(np.eye(128, dtype=np.float32)[::-1]),
        "sel": np.stack([np.stack([np.ones(128), np.zeros(128)]), np.stack([np.zeros(128), np.ones(128)])], 1).astype(np.float32),
    }
    sk = np.asarray(sink_b[0], np.float32).reshape(8)
    sinkT = np.zeros((128, 4), np.float32)
    for i in range(4):
        sinkT[0:64, i] = sk[2 * i]
        sinkT[64:128, i] = sk[2 * i + 1]
    shared["sinkT"] = sinkT
    in_maps = []
    for core in range(8):
        b, j = core // 4, core % 4
        ws, namask, swmask, ropeq, ropek, flags = _host_consts(j)
        xs = x_sample[b]
        halo = np.zeros((2, D), np.float32)
        if j > 0:
            halo[0] = xs[256 * j - 1]
        if j < 3:
            halo[1] = xs[256 * j + 256]
        cond = np.stack([np.asarray(c_ctx, np.float32), np.asarray(c[b], np.float32)], 1)
        m = dict(shared)
        m.update({
            "xp": f(x_prompt[2 * core:2 * core + 2].reshape(512, D)),
            "xo": f(xs[256 * j:256 * j + 256]), "xh": halo, "xw": f(xs[64 * ws:64 * ws + NW]),
            "cak": f(np.asarray(cache_a_k)[b, 0].reshape(256, 512)), "cav": f(np.asarray(cache_a_v)[b, 0].reshape(256, 512)),
            "cbk": f(np.asarray(cache_b_k)[b, 0].reshape(256, 128)), "cbv": f(np.asarray(cache_b_v)[b, 0].reshape(256, 128)),
            "condT": f(cond.reshape(8, 128, 2).transpose(1, 0, 2)),
            "rpbsrc": _rpbsrc(np.asarray(rpb_a[0], np.float32), j, ws),
            "namask": namask, "swmask": swmask, "ropeq": ropeq, "ropek": ropek, "flags": flags,
        })
        in_maps.append(m)
    res = run_bass_kernel_spmd(nc, in_maps, core_ids=list(range(8)))
    R = res.results
    y_p = np.concatenate([R[i]["y_p"].reshape(2, 256, D) for i in range(8)], 0)
    y_s = np.stack([np.concatenate([R[4 * b + j]["y_s"] for j in range(4)], 0) for b in range(2)], 0)
    nak = np.concatenate([R[i]["nak"].reshape(2, 1, 256, 8, 64) for i in range(8)], 0)
    nav = np.concatenate([R[i]["nav"].reshape(2, 1, 256, 8, 64) for i in range(8)], 0)
    nbk = np.concatenate([R[i]["nbk"].reshape(2, 1, 256, 2, 64) for i in range(8)], 0)
    nbv = np.concatenate([R[i]["nbv"].reshape(2, 1, 256, 2, 64) for i in range(8)], 0)
    return (y_p.astype(np.float32), y_s.astype(np.float32), nak.astype(np.float32), nav.astype(np.float32),
            nbk.astype(np.float32), nbv.astype(np.float32))
```

```python
import numpy as np
import concourse.bass as bass
import concourse.mybir as mybir
from concourse.bass_utils import run_bass_kernel_spmd

F32 = mybir.dt.float32
BF16 = mybir.dt.bfloat16
AF = mybir.ActivationFunctionType
ALU = mybir.AluOpType

NDS = 48
EPS = 1e-6
NEG = -30000.0
D = 1024
NW = 896
NO = 258
DFF = 2816
NPAIR = 22


def _merge(d, o):
    for k, v in o.items():
        if d.get(k, 0) < v:
            d[k] = v


ALL_TKS = []


class Tk:
    def __init__(s, name="t", excl=False):
        ALL_TKS.append(s)
        s.name = name
        s.w = {}
        s.r = {}
        s.old = {}
        s.excl = excl
        s.acc = {}

    def all_tokens(s):
        d = {}
        _merge(d, s.w); _merge(d, s.r); _merge(d, s.old)
        return d


class Buf:
    def __init__(s, t, lo, hi, name):
        s.t = t
        s.lo = lo
        s.hi = hi
        s.name = name
        s.tks = []
        s.ghost = {}

    def tk(s, name=None):
        k = Tk(name or s.name)
        _merge(k.old, s.ghost)
        s.tks.append(k)
        return k

    def tks_n(s, n):
        return [s.tk("%s%d" % (s.name, i)) for i in range(n)]

    def __getitem__(s, key):
        return s.t[key]


class Sched:
    def __init__(s, nc, sbuf_lo=16512, sbuf_hi=229312):
        s.nc = nc
        s.eng = {'pe': nc.tensor, 'act': nc.scalar, 'dve': nc.vector, 'pool': nc.gpsimd, 'sp': nc.sync}
        s.sem = {k: nc.alloc_semaphore("sem_" + k) for k in s.eng}
        s.cnt = {k: 0 for k in s.eng}
        s.waited = {k: {} for k in s.eng}
        s.dsem = [nc.alloc_semaphore("dsem%d" % i) for i in range(NDS)]
        s.dcnt = [0] * NDS
        s.dnext2 = [0, 0]
        s.semobj = {}
        for k in s.eng:
            s.semobj[('e', k)] = s.sem[k]
        for i in range(NDS):
            s.semobj[('d', i)] = s.dsem[i]
        s.nwaits = 0
        s.nops = {k: 0 for k in s.eng}
        s.lo = sbuf_lo
        s.hi = sbuf_hi
        s.live = []
        s.ghosts = []
        s.uid = 0
        s.peak = 0

    def alloc(s, name, shape, dtype, align=64):
        size = int(np.prod(shape[1:])) * mybir.dt.size(dtype)
        size = (size + align - 1) // align * align
        ivs = sorted((b.lo, b.hi) for b in s.live)
        pos = s.lo
        found = None
        for lo, hi in ivs:
            if lo - pos >= size:
                found = pos
                break
            pos = max(pos, hi)
        if found is None:
            if s.hi - pos >= size:
                found = pos
            else:
                raise RuntimeError("SBUF OOM allocating %s size %d; live=%s" % (
                    name, size, [(b.name, b.lo, b.hi) for b in s.live]))
        s.uid += 1
        t = s.nc.alloc_sbuf_tensor_at("%s_%d" % (name, s.uid), list(shape), dtype, offset=found)
        b = Buf(t, found, found + size, name)
        g = {}
        for lo, hi, tok in s.ghosts:
            if lo < b.hi and b.lo < hi:
                _merge(g, tok)
        b.ghost = g
        s.live.append(b)
        s.peak = max(s.peak, b.hi)
        return b

    def free(s, *bufs):
        for b in bufs:
            tok = dict(b.ghost)
            for k in b.tks:
                _merge(tok, k.all_tokens())
            s.ghosts.append((b.lo, b.hi, tok))
            s.live.remove(b)

    def _deps(s, reads, writes, pwrites, me=None):
        d = {}
        for t in list(reads) + list(writes) + list(pwrites):
            if t.excl:
                for k, v in t.acc.items():
                    if k != me and d.get(k, 0) < v:
                        d[k] = v
        for t in reads:
            _merge(d, t.w)
        for t in writes:
            _merge(d, t.w); _merge(d, t.r); _merge(d, t.old)
        for t in pwrites:
            if t.r:
                _merge(t.old, t.w); _merge(t.old, t.r)
                t.w = {}; t.r = {}
            _merge(d, t.old)
        return d

    def _wait(s, e, d):
        for key, val in d.items():
            if key == ('e', 'pe') and e == 'pe':
                continue
            if s.waited[e].get(key, 0) < val:
                s.eng[e].wait_ge(s.semobj[key], val)
                s.waited[e][key] = val
                s.nwaits += 1

    def _mark(s, key, val, reads, writes, pwrites):
        for t in list(reads) + list(writes) + list(pwrites):
            if t.excl:
                t.acc[key] = val
        for t in reads:
            if t.r.get(key, 0) < val:
                t.r[key] = val
        for t in writes:
            t.w = {key: val}; t.r = {}; t.old = {}
        for t in pwrites:
            if t.w.get(key, 0) < val:
                t.w[key] = val

    def op(s, e, fn, reads=(), writes=(), pwrites=()):
        s._wait(e, s._deps(reads, writes, pwrites, ('e', e)))
        ins = fn(s.eng[e])
        s.cnt[e] += 1
        s.nops[e] += 1
        ins.then_inc(s.sem[e], 1)
        s._mark(('e', e), s.cnt[e], reads, writes, pwrites)

    def dma(s, q, out, in_, reads=(), writes=(), pwrites=(), **kw):
        d = s._deps(reads, writes, pwrites)
        half = NDS // 2
        qi = 0 if q == 'sp' else 1
        i = qi * half + s.dnext2[qi]
        s.dnext2[qi] = (s.dnext2[qi] + 1) % half
        key = ('d', i)
        if s.dcnt[i] > 0:
            if d.get(key, 0) < s.dcnt[i]:
                d[key] = s.dcnt[i]
        s._wait(q, d)
        ins = s.eng[q].dma_start(out=out, in_=in_, **kw)
        s.dcnt[i] += 16
        s.nops[q] += 1
        ins.then_inc(s.dsem[i], 16)
        s._mark(key, s.dcnt[i], reads, writes, pwrites)

    def finish(s, tks):
        d = {}
        for t in tks:
            _merge(d, t.all_tokens())
        s._wait('sp', d)


IN_SPECS = [
    ("xp", [512, D]), ("xo", [256, D]), ("xh", [2, D]), ("xw", [NW, D]),
    ("cak", [256, 512]), ("cav", [256, 512]), ("cbk", [256, 128]), ("cbv", [256, 128]),
    ("condT", [128, 8, 2]), ("w_mod", [D, 6 * D]), ("b_mod", [6 * D]), ("gains", [4 * D]),
    ("w_in", [D, 2304]), ("w_out", [D, D]), ("w_up", [D, 2 * DFF]), ("w_down", [DFF, D]),
    ("cwT", [128, 44, 3]), ("cbT", [128, 44]), ("sinkT", [128, 4]), ("ggT", [128, 8]),
    ("rpbsrc", [8, 19, 127]), ("namask", [7, 128, NO]), ("swmask", [7, 128, NO]),
    ("ropeq", [2, 128, NO]), ("ropek", [2, 128, NW]), ("flags", [128, 2]),
    ("ident", [128, 128]), ("antij", [128, 128]), ("sel", [2, 2, 128]),
]
OUT_SPECS = [
    ("y_p", [512, D]), ("y_s", [256, D]), ("nak", [512, 512]), ("nav", [512, 512]),
    ("nbk", [512, 128]), ("nbv", [512, 128]),
]


def build_program(STOP=99):
    nc = bass.Bass("TRN2", target_bir_lowering=False)
    S = Sched(nc)
    del ALL_TKS[:]

    def cut(n):
        if STOP == n:
            if n >= 5:
                for tt in range(4):
                    S.dma('sp', do["y_p"].ap()[tt * 128:(tt + 1) * 128, :], xP[:, tt, :], reads=[k_xP[tt]])
                for tt in range(2):
                    S.dma('sp', do["y_s"].ap()[tt * 128:(tt + 1) * 128, :], xO[:, tt, :], reads=[k_xO[tt]])
            S.finish(list(ALL_TKS))
            return True
        return False

    di = {n: nc.dram_tensor(n, list(sh), F32, kind="ExternalInput") for n, sh in IN_SPECS}
    do = {n: nc.dram_tensor(n, list(sh), F32, kind="ExternalOutput") for n, sh in OUT_SPECS}

    ps = [nc.alloc_psum_tensor("ps%d" % i, [128, 512], F32) for i in range(8)]
    kps = [Tk("ps%d" % i, excl=True) for i in range(8)]
    psb = [p.ap().bitcast(BF16) for p in ps]

    rr = {'ev': 0}

    def evac_eng():
        rr['ev'] += 1
        return 'act' if rr['ev'] % 2 == 0 else 'dve'

    def evac_copy(eng, out, in_, reads, writes=(), pwrites=(), scale=None):
        if eng == 'act':
            if scale is None:
                S.op('act', lambda e: e.activation(out=out, in_=in_, func=AF.Copy), reads=reads, writes=writes, pwrites=pwrites)
            else:
                S.op('act', lambda e: e.activation(out=out, in_=in_, func=AF.Copy, scale=scale), reads=reads, writes=writes, pwrites=pwrites)
        else:
            if scale is None:
                S.op(eng, lambda e: e.tensor_copy(out=out, in_=in_), reads=reads, writes=writes, pwrites=pwrites)
            else:
                S.op(eng, lambda e: e.tensor_scalar_mul(out=out, in0=in_, scalar1=scale), reads=reads, writes=writes, pwrites=pwrites)

    identb = S.alloc("identb", [128, 128], BF16); k_identb = identb.tk()
    identf = S.alloc("identf", [128, 128], F32); k_identf = identf.tk()
    antib = S.alloc("antib", [128, 128], BF16); k_antib = antib.tk()
    selb = S.alloc("selb", [2, 2, 128], F32); k_sel = selb.tk()
    ones64 = S.alloc("ones64", [128, 64], BF16); k_ones64 = ones64.tk()
    onesb = S.alloc("onesb", [128, 128], BF16); k_onesb = onesb.tk()
    epsb = S.alloc("epsb", [128, 1], F32); k_eps = epsb.tk()
    ggT = S.alloc("ggT", [128, 8], F32); k_gg = ggT.tk()
    cwT = S.alloc("cwT", [128, 44, 3], F32); k_cw = cwT.tk()
    cbT = S.alloc("cbT", [128, 44], F32); k_cb = cbT.tk()
    esT = S.alloc("esT", [128, 4], F32); k_es = esT.tk()
    flg = S.alloc("flg", [128, 2], F32); k_flg = flg.tk()
    ABT = S.alloc("ABT", [128, 4, 8, 2], F32); k_ABT = [ABT.tk("ABT0"), ABT.tk("ABT1")]
    Gbc = S.alloc("Gbc", [128, 2, 2, D], F32); k_Gbc = Gbc.tk()
    small = S.alloc("small", [128, 4, 4], F32); k_small = [small.tk("small%d" % i) for i in range(4)]
    NSM = 4

    S.dma('pool', identb[:], di["ident"].ap(), writes=[k_identb])
    S.dma('sp', identf[:], di["ident"].ap(), writes=[k_identf])
    S.dma('pool', antib[:], di["antij"].ap(), writes=[k_antib])
    S.dma('sp', selb[:], di["sel"].ap(), writes=[k_sel])
    S.dma('sp', ggT[:], di["ggT"].ap(), writes=[k_gg])
    S.dma('sp', cwT[:], di["cwT"].ap(), writes=[k_cw])
    S.dma('sp', cbT[:], di["cbT"].ap(), writes=[k_cb])
    S.dma('sp', esT[:], di["sinkT"].ap(), writes=[k_es])
    S.dma('sp', flg[:], di["flags"].ap(), writes=[k_flg])
    S.op('dve', lambda e: e.memset(ones64[:], 1.0), writes=[k_ones64])
    S.op('dve', lambda e: e.memset(onesb[:], 1.0), writes=[k_onesb])
    S.op('dve', lambda e: e.memset(epsb[:], EPS), writes=[k_eps])

    xP = S.alloc("xP", [128, 4, D], F32); k_xP = xP.tks_n(4)
    xO = S.alloc("xO", [128, 3, D], F32); k_xO = xO.tks_n(3)
    OTOK = [(0, 128), (1, 128), (2, 2)]

    def ocols(tt):
        if tt < 2:
            return slice(1 + 128 * tt, 129 + 128 * tt)
        return slice(0, NO, NO - 1)

    condT = S.alloc("condT", [128, 8, 2], F32); k_condT = condT.tk()
    sT = S.alloc("sT", [128, 8, 2], BF16); k_sT = sT.tk()
    m_sb = S.alloc("m_sb", [2, 6 * D], F32); k_m = m_sb.tks_n(12)
    gains2 = S.alloc("gains2", [2, 4, D], F32); k_g2 = gains2.tk()
    rows = S.alloc("rows", [2, 4, D], F32); k_rows = rows.tks_n(4)
    wmb = [S.alloc("wm%d" % i, [128, 8, 512], BF16) for i in range(3)]
    k_wm = [b.tk() for b in wmb]

    S.dma('sp', condT[:], di["condT"].ap(), writes=[k_condT])
    S.dma('sp', m_sb[:], bass.AP(di["b_mod"], 0, [[0, 2], [1, 6 * D]]), writes=k_m)
    S.dma('sp', gains2[:], bass.AP(di["gains"], 0, [[0, 2], [D, 4], [1, D]]), writes=[k_g2])
    for tt in range(4):
        S.dma('sp', xP[:, tt, :], di["xp"].ap()[tt * 128:(tt + 1) * 128, :], writes=[k_xP[tt]])
    for tt in range(2):
        S.dma('sp', xO[:, tt, :], di["xo"].ap()[tt * 128:(tt + 1) * 128, :], writes=[k_xO[tt]])
    S.dma('sp', xO[0:2, 2, :], di["xh"].ap(), writes=[k_xO[2]])
    S.op('act', lambda e: e.activation(out=sT[:], in_=condT[:], func=AF.Silu), reads=[k_condT], writes=[k_sT])

    wmod_v = di["w_mod"].ap().rearrange("(kc p) n -> p kc n", p=128)
    ada = {'n': 0}

    def ada_dma(jc):
        b = jc % 3
        S.dma('pool', wmb[b][:], wmod_v[:, :, jc * 512:(jc + 1) * 512], writes=[k_wm[b]])

    def ada_chunk(jc):
        b = jc % 3
        pb = 2 + (ada['n'] % 2); ada['n'] += 1

        def mm(e):
            for kc in range(8):
                ins = e.matmul(ps[pb][0:2, :], lhsT=sT[:, kc, :], rhs=wmb[b][:, kc, :], start=(kc == 0), stop=(kc == 7))
            return ins
        S.op('pe', mm, reads=[k_sT, k_wm[b]], writes=[kps[pb]])
        sl = slice(jc * 512, (jc + 1) * 512)
        S.op('dve', lambda e: e.tensor_tensor(out=m_sb[0:2, sl], in0=ps[pb][0:2, :], in1=m_sb[0:2, sl], op=ALU.add),
             reads=[kps[pb]], writes=[k_m[jc]])
        if jc + 3 < 12:
            ada_dma(jc + 3)

    for jc in range(3):
        ada_dma(jc)
    for jc in range(4):
        ada_chunk(jc)
    S.op('dve', lambda e: e.scalar_tensor_tensor(out=rows[0:2, 0, :], in0=m_sb[0:2, D:2 * D], scalar=1.0, in1=gains2[0:2, 0, :],
                                                 op0=ALU.add, op1=ALU.mult), reads=[k_m[2], k_m[3], k_g2], writes=[k_rows[0]])

    def tpAB(which):
        srcs = [rows[0:2, which, :], m_sb[0:2, 3 * which * D:(3 * which + 1) * D]]

        def fn(e):
            for idx in range(2):
                for kc in range(8):
                    o = (idx * 8 + kc) * 2
                    ins = e.transpose(out=ps[4][:, o:o + 2], in_=srcs[idx][:, kc * 128:(kc + 1) * 128], identity=identf[0:2, 0:2])
            return ins
        S.op('pe', fn, reads=[k_rows[which], k_m[6 * which], k_m[6 * which + 1], k_identf], writes=[kps[4]])
        S.op('dve', lambda e: e.tensor_copy(out=ABT[:, 2 * which:2 * which + 2, :, :].rearrange("p a k r -> p (a k r)"), in_=ps[4][:, 0:32]),
             reads=[kps[4]], writes=[k_ABT[which]])
    tpAB(0)

    nb = {}

    def nonlocal_set(xn_list, junk_list):
        nb['xn'] = xn_list; nb['k_xn'] = [b.tk() for b in xn_list]
        nb['junk'] = junk_list; nb['k_junk'] = [b.tk() for b in junk_list]
    nonlocal_set([S.alloc("xn%d" % i, [128, D], BF16) for i in range(2)], [S.alloc("junk%d" % i, [128, D], BF16) for i in range(2)])
    cnt = {'nt': 0, 'sm': 0}

    def rstd_from_ssq(col_in, col_out, p, ksm, scale):
        S.op('act', lambda e: e.activation(out=col_out, in_=col_in, func=AF.Ln, scale=scale, bias=epsb[0:p, 0:1]),
             reads=[ksm, k_eps], writes=[ksm])
        S.op('act', lambda e: e.activation(out=col_out, in_=col_out, func=AF.Exp, scale=-0.5), reads=[ksm], writes=[ksm])

    def norm_S(tile):
        (x_ap, k_x, p, hT, k_hT, colsel, ab, r, pre) = tile
        if pre is not None:
            pre()
        i = cnt['nt']; cnt['nt'] += 1
        si = cnt['sm'] % NSM; cnt['sm'] += 1
        sm = small[0:p, si, :]; ksm = k_small[si]
        jb = nb['junk'][i % 2]; kj = nb['k_junk'][i % 2]
        xb = nb['xn'][i % 2]; kxb = nb['k_xn'][i % 2]
        S.op('act', lambda e: e.activation(out=jb[0:p, :], in_=x_ap, func=AF.Square, accum_out=sm[:, 0:1]), reads=[k_x], writes=[kj, ksm])
        rstd_from_ssq(sm[:, 0:1], sm[:, 1:2], p, ksm, 1.0 / D)
        S.op('dve', lambda e: e.tensor_scalar_mul(out=xb[0:p, :], in0=x_ap, scalar1=sm[:, 1:2]), reads=[k_x, ksm], writes=[kxb])
        return (i, xb, kxb)

    def norm_T(tile, st):
        (x_ap, k_x, p, hT, k_hT, colsel, ab, r, pre) = tile
        i, xb, kxb = st
        pb = nb.get('tp_base', 0) + i % 2

        def tp(e):
            for kc in range(8):
                ins = e.transpose(out=psb[pb][:, kc * 128:kc * 128 + p], in_=xb[0:p, kc * 128:(kc + 1) * 128], identity=identb[0:p, 0:p])
            return ins
        S.op('pe', tp, reads=[kxb, k_identb], writes=[kps[pb]])
        return (i, pb)

    def norm_A(tile):
        return norm_T(tile, norm_S(tile))

    def norm_B(tile, st, force_eng=None):
        (x_ap, k_x, p, hT, k_hT, colsel, ab, r, pre) = tile
        i, pb = st
        eng = 'act' if i % 2 == 1 else 'dve'
        if force_eng is not None:
            tmpN = nb['tmpN']; ktmp = nb['k_tmpN']
            if isinstance(tmpN, Buf):
                tmpN = tmpN.t[:]
            a_b = bass.AP(ABT.t, (2 * ab) * 16 + r, [[64, 128], [2, 8], [0, p]])
            b_b = bass.AP(ABT.t, (2 * ab + 1) * 16 + r, [[64, 128], [2, 8], [0, p]])
            S.op('dve', lambda e: e.tensor_tensor(out=tmpN[:, :, 0:p], in0=psb[pb][:, :].rearrange("q (k n) -> q k n", k=8)[:, :, 0:p], in1=a_b, op=ALU.mult),
                 reads=[kps[pb], k_ABT[ab]], writes=[ktmp])
            S.op('dve', lambda e: e.tensor_tensor(out=hT[:, :, colsel], in0=tmpN[:, :, 0:p], in1=b_b, op=ALU.add),
                 reads=[ktmp, k_ABT[ab]], pwrites=[k_hT])
            return
        for kc in range(8):
            o = hT[:, kc, colsel]
            i_ = psb[pb][:, kc * 128:kc * 128 + p]
            a_ = ABT[:, 2 * ab, kc, r:r + 1]
            b_ = ABT[:, 2 * ab + 1, kc, r:r + 1]
            if eng == 'act':
                S.op('act', lambda e, o=o, i_=i_, a_=a_, b_=b_: e.activation(out=o, in_=i_, func=AF.Identity, bias=b_, scale=a_),
                     reads=[kps[pb], k_ABT[ab]], pwrites=[k_hT])
            else:
                S.op('dve', lambda e, o=o, i_=i_, a_=a_, b_=b_: e.tensor_scalar(out=o, in0=i_, scalar1=a_, scalar2=b_, op0=ALU.mult, op1=ALU.add),
                     reads=[kps[pb], k_ABT[ab]], pwrites=[k_hT])

    def norm_steps(tiles, force_eng=None):
        state = {}
        thunks = []
        for k in range(len(tiles) + 1):
            def th(k=k):
                if k < len(tiles):
                    state[k] = norm_S(tiles[k])
                if k >= 1:
                    st = norm_T(tiles[k - 1], state[k - 1])
                    norm_B(tiles[k - 1], st, force_eng)
            thunks.append(th)
        return thunks

    def norm_pipeline(tiles, extras=()):
        extras = list(extras)
        prev = None
        for n, t in enumerate(tiles):
            st = norm_A(t)
            if prev is not None:
                norm_B(*prev)
            prev = (t, st)
            if n % 2 == 1 and extras:
                extras.pop(0)()
        norm_B(*prev)
        for ex in extras:
            ex()

    hT_P = S.alloc("hT_P", [128, 8, 512], BF16); k_hTP = hT_P.tks_n(4)
    hT_W = S.alloc("hT_W", [128, 8, NW], BF16); k_hTW = hT_W.tks_n(7)
    hT_O = S.alloc("hT_O", [128, 8, NO], BF16); k_hTO = hT_O.tks_n(3)
    xWs = [S.alloc("xWs%d" % i, [128, D], F32) for i in range(3)]
    k_xWs = [b.tk() for b in xWs]

    tiles = []
    for tt in range(4):
        tiles.append((xP[:, tt, :], k_xP[tt], 128, hT_P, k_hTP[tt], slice(tt * 128, (tt + 1) * 128), 0, 0, None))
    for tt in range(7):
        b = tt % 3

        def pre(tt=tt, b=b):
            S.dma('sp', xWs[b][:], di["xw"].ap()[tt * 128:(tt + 1) * 128, :], writes=[k_xWs[b]])
        tiles.append((xWs[b][:], k_xWs[b], 128, hT_W, k_hTW[tt], slice(tt * 128, (tt + 1) * 128), 0, 1, pre))
    for tt, p in OTOK:
        tiles.append((xO[0:p, tt, :], k_xO[tt], p, hT_O, k_hTO[tt], ocols(tt), 0, 1, None))
    norm_pipeline(tiles, [lambda jc=jc: ada_chunk(jc) for jc in range(4, 12)])
    S.free(*xWs)

    S.op('dve', lambda e: e.scalar_tensor_tensor(out=rows[0:2, 1, :], in0=m_sb[0:2, 4 * D:5 * D], scalar=1.0, in1=gains2[0:2, 2, :],
                                                 op0=ALU.add, op1=ALU.mult), reads=[k_m[8], k_m[9], k_g2], writes=[k_rows[1]])
    S.op('dve', lambda e: e.tensor_tensor(out=rows[0:2, 2, :], in0=m_sb[0:2, 2 * D:3 * D], in1=gains2[0:2, 1, :], op=ALU.mult),
         reads=[k_m[4], k_m[5], k_g2], writes=[k_rows[2]])
    S.op('dve', lambda e: e.tensor_tensor(out=rows[0:2, 3, :], in0=m_sb[0:2, 5 * D:6 * D], in1=gains2[0:2, 3, :], op=ALU.mult),
         reads=[k_m[10], k_m[11], k_g2], writes=[k_rows[3]])
    tpAB(1)
    n = 0
    for which in range(2):
        for r in range(2):
            for hf in range(2):
                pb = 5 + (n % 2)
                n += 1
                S.op('pe', lambda e, pb=pb, which=which, r=r, hf=hf: e.matmul(
                    ps[pb][:, :], lhsT=selb[0:2, r, :], rhs=rows[0:2, 2 + which, hf * 512:(hf + 1) * 512], start=True, stop=True),
                    reads=[k_sel, k_rows[2 + which]], writes=[kps[pb]])
                evac_copy(evac_eng(), Gbc[:, which, r, hf * 512:(hf + 1) * 512], ps[pb][:, :], reads=[kps[pb]], pwrites=[k_Gbc])
    S.op('act', lambda e: e.activation(out=esT[:], in_=esT[:], func=AF.Exp), reads=[k_es], writes=[k_es])
    S.free(condT, sT, m_sb, gains2, rows, *wmb)
    S.free(*nb['xn'], *nb['junk'])
    if cut(1):
        return nc, S

    QT_P = S.alloc("QT_P", [128, 8, 512], BF16); k_QTP = QT_P.tks_n(8)
    KT_P = S.alloc("KT_P", [128, 6, 512], BF16); k_KTP = KT_P.tks_n(6)
    V_P = S.alloc("V_P", [128, 4, 10, 64], BF16); k_VP = V_P.tks_n(4)
    QT_O = S.alloc("QT_O", [128, 8, NO], BF16); k_QTO = QT_O.tks_n(8)
    KT_W = S.alloc("KT_W", [128, 6, NW], BF16); k_KTW = KT_W.tks_n(6)
    V_W = S.alloc("V_W", [128, 7, 10, 64], BF16); k_VW = V_W.tks_n(7)
    KT_C = S.alloc("KT_C", [128, 6, 256], BF16); k_KTC = KT_C.tks_n(6)
    V_C = S.alloc("V_C", [128, 2, 10, 64], BF16); k_VC = V_C.tks_n(2)
    rope_q = S.alloc("rope_q", [128, 2, NO], F32); k_rq = rope_q.tk()
    rope_k = S.alloc("rope_k", [128, 2, NW], F32); k_rk = rope_k.tk()
    wg = [S.alloc("wg%d" % i, [128, 8, 512], BF16) for i in range(3)]
    k_wg = [b.tk() for b in wg]
    wrot = S.alloc("wrot", [128, 8, 512], BF16); k_wrot = wrot.tk()
    wdup = S.alloc("wdup", [128, 8, 2, 128], BF16); k_wdup = wdup.tk()
    wdupr = S.alloc("wdupr", [128, 8, 2, 128], BF16); k_wdupr = wdupr.tk()
    stg = [S.alloc("stg%d" % i, [128, 512], F32) for i in range(3)]
    k_stg = [b.tk() for b in stg]
    rtmp = [S.alloc("rtmp%d" % i, [128, 512], F32) for i in range(2)]
    k_rtmp = [b.tk() for b in rtmp]

    win_v = di["w_in"].ap().rearrange("(kc p) n -> p kc n", p=128)
    GCOLS = [(0, 512), (512, 512), (1024, 512), (1536, 512), (2048, 256)]
    for g in range(3):
        c0, w = GCOLS[g]
        S.dma('pool', wg[g][:, :, 0:w], win_v[:, :, c0:c0 + w], writes=[k_wg[g]])
    S.dma('sp', rope_q[:], di["ropeq"].ap().rearrange("t p n -> p t n"), writes=[k_rq])
    S.dma('sp', rope_k[:], di["ropek"].ap().rearrange("t p n -> p t n"), writes=[k_rk])
    ckst = S.alloc("ckst", [128, 2, 768], BF16); k_ckst = ckst.tk()
    for t in range(2):
        S.dma('pool', ckst[:, t, 0:512], di["cak"].ap()[t * 128:(t + 1) * 128, :], pwrites=[k_ckst])
        for kv in range(2):
            for hf in range(2):
                S.dma('pool', ckst[:, t, 512 + kv * 128 + hf * 64: 512 + kv * 128 + hf * 64 + 64],
                      di["cbk"].ap()[t * 128:(t + 1) * 128, kv * 64:(kv + 1) * 64], pwrites=[k_ckst])
        S.dma('pool', V_C[:, t, 0:8, :], di["cav"].ap()[t * 128:(t + 1) * 128, :].rearrange("p (h d) -> p h d", d=64), pwrites=[k_VC[t]])
        S.dma('pool', V_C[:, t, 8:10, :], di["cbv"].ap()[t * 128:(t + 1) * 128, :].rearrange("p (h d) -> p h d", d=64), pwrites=[k_VC[t]])

    pcnt = {'i': 0, 'st': 0, 'rt': 0}

    def nxt_ps():
        pcnt['i'] += 1
        return 2 + (pcnt['i'] % 6)

    def run_fm(w_ap_fn, w_tks, hT, k_hT_list, n0, n):
        pb = nxt_ps()

        def mm(e):
            for kc in range(8):
                ins = e.matmul(ps[pb][:, 0:n], lhsT=w_ap_fn(kc), rhs=hT[:, kc, n0:n0 + n], start=(kc == 0), stop=(kc == 7))
            return ins
        S.op('pe', mm, reads=list(w_tks) + list(k_hT_list), writes=[kps[pb]])
        return pb

    def tm_proj(hT, k_hT, colsel, p, w_ap_fn, w_tks, ncols):
        pb = nxt_ps()

        def mm(e):
            for kc in range(8):
                ins = e.matmul(ps[pb][0:p, 0:ncols], lhsT=hT[:, kc, colsel], rhs=w_ap_fn(kc), start=(kc == 0), stop=(kc == 7))
            return ins
        S.op('pe', mm, reads=list(w_tks) + [k_hT], writes=[kps[pb]])
        return pb

    WSEG = [(0, 512), (512, 384)]

    g = 0
    for c in range(4):
        wf = lambda kc, c=c, g=g: wg[g][:, kc, c * 128:(c + 1) * 128]
        pb = run_fm(wf, [k_wg[g]], hT_P, k_hTP, 0, 512)
        evac_copy(evac_eng(), QT_P[:, c, :], ps[pb][:, 0:512], reads=[kps[pb]], writes=[k_QTP[c]], scale=0.125)
        pb = run_fm(wf, [k_wg[g]], hT_O, k_hTO, 0, NO)
        evac_copy(evac_eng(), QT_O[:, c, :], ps[pb][:, 0:NO], reads=[kps[pb]], writes=[k_QTO[c]], scale=0.125)
    g = 1
    for c in range(4):
        wf = lambda kc, c=c, g=g: wg[g][:, kc, c * 128:(c + 1) * 128]
        pb = run_fm(wf, [k_wg[g]], hT_P, k_hTP, 0, 512)
        evac_copy(evac_eng(), KT_P[:, c, :], ps[pb][:, 0:512], reads=[kps[pb]], writes=[k_KTP[c]])
        for (n0, n) in WSEG:
            pb = run_fm(wf, [k_wg[g]], hT_W, k_hTW, n0, n)
            evac_copy(evac_eng(), KT_W[:, c, n0:n0 + n], ps[pb][:, 0:n], reads=[kps[pb]], pwrites=[k_KTW[c]])
    for tt in range(4):
        pb = tm_proj(hT_P, k_hTP[tt], slice(tt * 128, (tt + 1) * 128), 128, lambda kc, g=g: wg[g][:, kc, :], [k_wg[g]], 512)
        si = pcnt['st'] % 3; pcnt['st'] += 1
        evac_copy(evac_eng(), stg[si][:], ps[pb][:, :], reads=[kps[pb]], writes=[k_stg[si]])
        S.dma('sp', do["nak"].ap()[tt * 128:(tt + 1) * 128, :], stg[si][:], reads=[k_stg[si]])
    S.dma('pool', wg[0][:, :, 0:512], win_v[:, :, 1536:2048], writes=[k_wg[0]])
    g = 2
    for tt in range(4):
        pb = tm_proj(hT_P, k_hTP[tt], slice(tt * 128, (tt + 1) * 128), 128, lambda kc, g=g: wg[g][:, kc, :], [k_wg[g]], 512)
        si = pcnt['st'] % 3; pcnt['st'] += 1
        evac_copy('dve', stg[si][:], ps[pb][:, :], reads=[kps[pb]], writes=[k_stg[si]])
        evac_copy('act', V_P[:, tt, 0:8, :], ps[pb][:, :].rearrange("p (h d) -> p h d", d=64), reads=[kps[pb]], pwrites=[k_VP[tt]])
        S.dma('sp', do["nav"].ap()[tt * 128:(tt + 1) * 128, :], stg[si][:], reads=[k_stg[si]])
    for tt in range(7):
        pb = tm_proj(hT_W, k_hTW[tt], slice(tt * 128, (tt + 1) * 128), 128, lambda kc, g=g: wg[g][:, kc, :], [k_wg[g]], 512)
        evac_copy(evac_eng(), V_W[:, tt, 0:8, :], ps[pb][:, :].rearrange("p (h d) -> p h d", d=64), reads=[kps[pb]], pwrites=[k_VW[tt]])
    S.dma('pool', wg[1][:, :, 0:256], win_v[:, :, 2048:2304], writes=[k_wg[1]])

    def build_rot(dst, k_dst, src, k_src):
        sv = src.rearrange("p k (a t s) -> p (k a) t s", t=2, s=16)
        dv = dst.rearrange("p k (a t s) -> p (k a) t s", t=2, s=16)
        S.op('dve', lambda e: e.tensor_scalar_mul(out=dv[:, :, 0, :], in0=sv[:, :, 1, :], scalar1=-1.0), reads=[k_src], pwrites=[k_dst])
        S.op('dve', lambda e: e.tensor_copy(out=dv[:, :, 1, :], in_=sv[:, :, 0, :]), reads=[k_src], pwrites=[k_dst])

    build_rot(wrot[:, :, :], k_wrot, wg[0][:, :, :], k_wg[0])
    for c in range(4):
        wf = lambda kc, c=c: wg[0][:, kc, c * 128:(c + 1) * 128]
        wfr = lambda kc, c=c: wrot[:, kc, c * 128:(c + 1) * 128]
        pb = run_fm(wf, [k_wg[0]], hT_P, k_hTP, 0, 512)
        evac_copy(evac_eng(), QT_P[:, 4 + c, :], ps[pb][:, 0:512], reads=[kps[pb]], writes=[k_QTP[4 + c]], scale=0.125)
        pb1 = run_fm(wf, [k_wg[0]], hT_O, k_hTO, 0, NO)
        pb2 = run_fm(wfr, [k_wrot], hT_O, k_hTO, 0, NO)
        ri = pcnt['rt'] % 2; pcnt['rt'] += 1
        ri2 = pcnt['rt'] % 2; pcnt['rt'] += 1
        S.op('dve', lambda e, pb1=pb1, ri=ri: e.scalar_tensor_tensor(out=rtmp[ri][:, 0:NO], in0=ps[pb1][:, 0:NO], scalar=0.125, in1=rope_q[:, 0, :],
                                                                     op0=ALU.mult, op1=ALU.mult), reads=[kps[pb1], k_rq], writes=[k_rtmp[ri]])
        S.op('dve', lambda e, pb2=pb2, ri2=ri2: e.scalar_tensor_tensor(out=rtmp[ri2][:, 0:NO], in0=ps[pb2][:, 0:NO], scalar=0.125, in1=rope_q[:, 1, :],
                                                                       op0=ALU.mult, op1=ALU.mult), reads=[kps[pb2], k_rq], writes=[k_rtmp[ri2]])
        S.op('pool', lambda e, ri=ri, ri2=ri2, c=c: e.tensor_tensor(out=QT_O[:, 4 + c, :], in0=rtmp[ri][:, 0:NO], in1=rtmp[ri2][:, 0:NO], op=ALU.add),
             reads=[k_rtmp[ri], k_rtmp[ri2]], writes=[k_QTO[4 + c]])
    for kv in range(2):
        for hf in range(2):
            S.op('dve', lambda e, kv=kv, hf=hf: e.tensor_copy(out=wdup[:, :, kv, hf * 64:(hf + 1) * 64], in_=wg[1][:, :, kv * 64:(kv + 1) * 64]),
                 reads=[k_wg[1]], pwrites=[k_wdup])
    build_rot(wdupr[:].rearrange("p k v n -> p k (v n)"), k_wdupr, wdup[:].rearrange("p k v n -> p k (v n)"), k_wdup)
    for kv in range(2):
        wf = lambda kc, kv=kv: wdup[:, kc, kv, :]
        wfr = lambda kc, kv=kv: wdupr[:, kc, kv, :]
        pb = run_fm(wf, [k_wdup], hT_P, k_hTP, 0, 512)
        evac_copy(evac_eng(), KT_P[:, 4 + kv, :], ps[pb][:, 0:512], reads=[kps[pb]], writes=[k_KTP[4 + kv]])
        for (n0, n) in WSEG:
            pb1 = run_fm(wf, [k_wdup], hT_W, k_hTW, n0, n)
            pb2 = run_fm(wfr, [k_wdupr], hT_W, k_hTW, n0, n)
            ri = pcnt['rt'] % 2; pcnt['rt'] += 1
            ri2 = pcnt['rt'] % 2; pcnt['rt'] += 1
            S.op('dve', lambda e, pb1=pb1, ri=ri, n0=n0, n=n: e.tensor_tensor(out=rtmp[ri][:, 0:n], in0=ps[pb1][:, 0:n], in1=rope_k[:, 0, n0:n0 + n], op=ALU.mult),
                 reads=[kps[pb1], k_rk], writes=[k_rtmp[ri]])
            S.op('dve', lambda e, pb2=pb2, ri2=ri2, n0=n0, n=n: e.tensor_tensor(out=rtmp[ri2][:, 0:n], in0=ps[pb2][:, 0:n], in1=rope_k[:, 1, n0:n0 + n], op=ALU.mult),
                 reads=[kps[pb2], k_rk], writes=[k_rtmp[ri2]])
            S.op('pool', lambda e, ri=ri, ri2=ri2, n0=n0, n=n, kv=kv: e.tensor_tensor(out=KT_W[:, 4 + kv, n0:n0 + n], in0=rtmp[ri][:, 0:n], in1=rtmp[ri2][:, 0:n], op=ALU.add),
                 reads=[k_rtmp[ri], k_rtmp[ri2]], pwrites=[k_KTW[4 + kv]])
    for tt in range(4):
        pb = tm_proj(hT_P, k_hTP[tt], slice(tt * 128, (tt + 1) * 128), 128, lambda kc: wg[1][:, kc, 0:256], [k_wg[1]], 256)
        si = pcnt['st'] % 3; pcnt['st'] += 1
        evac_copy('dve', stg[si][:, 0:256], ps[pb][:, 0:256], reads=[kps[pb]], writes=[k_stg[si]])
        evac_copy('act', V_P[:, tt, 8:10, :], ps[pb][:, 128:256].rearrange("p (h d) -> p h d", d=64), reads=[kps[pb]], pwrites=[k_VP[tt]])
        S.dma('sp', do["nbk"].ap()[tt * 128:(tt + 1) * 128, :], stg[si][:, 0:128], reads=[k_stg[si]])
        S.dma('sp', do["nbv"].ap()[tt * 128:(tt + 1) * 128, :], stg[si][:, 128:256], reads=[k_stg[si]])
    for tt in range(7):
        pb = tm_proj(hT_W, k_hTW[tt], slice(tt * 128, (tt + 1) * 128), 128, lambda kc: wg[1][:, kc, 128:256], [k_wg[1]], 128)
        evac_copy(evac_eng(), V_W[:, tt, 8:10, :], ps[pb][:, 0:128].rearrange("p (h d) -> p h d", d=64), reads=[kps[pb]], pwrites=[k_VW[tt]])
    for t in range(2):
        pb = nxt_ps()

        def tpc(e, t=t, pb=pb):
            for c in range(6):
                ins = e.transpose(out=psb[pb][:, c * 128:(c + 1) * 128], in_=ckst[:, t, c * 128:(c + 1) * 128], identity=identb[:])
            return ins
        S.op('pe', tpc, reads=[k_ckst, k_identb], writes=[kps[pb]])
        evac_copy(evac_eng(), KT_C[:, :, t * 128:(t + 1) * 128], psb[pb][:, 0:768].rearrange("p (c n) -> p c n", n=128), reads=[kps[pb]],
                  pwrites=k_KTC)
    S.free(hT_P, hT_W, hT_O, rope_q, rope_k, wrot, wdup, wdupr, ckst, *wg, *stg, *rtmp)
    if cut(2):
        return nc, S

    wo = S.alloc("wo", [128, 8, D], BF16); k_wo = wo.tk()
    S.dma('pool', wo[:], di["w_out"].ap().rearrange("(kc p) n -> p kc n", p=128), writes=[k_wo])
    NPT = 3
    PT = [[S.alloc("PT%d_%d" % (i, j), [128, 512], BF16) for j in range(2)] for i in range(NPT)]
    k_PT = [[b.tk() for b in row] for row in PT]
    SB = [[S.alloc("SB%d_%d" % (i, j), [128, NO], F32) for j in range(2)] for i in range(2)]
    k_SB = [[b.tk() for b in row] for row in SB]
    OT = S.alloc("OT", [128, 8, NO], F32); k_OT = OT.tks_n(8)
    OTn_b = [S.alloc("OTn%d" % i, [128, 8, NO], BF16) for i in range(2)]
    k_OTn_b = [b.tks_n(8) for b in OTn_b]
    sq = S.alloc("sq", [128, 8, NO], BF16); k_sq = sq.tks_n(8)
    Rb = [S.alloc("Rb%d" % i, [128, NO], F32) for i in range(2)]
    k_Rb = [b.tk() for b in Rb]
    rg = S.alloc("rg", [128, NO], F32); k_rg = rg.tk()
    ytmp = [S.alloc("ytmp%d" % i, [128, D], F32) for i in range(2)]
    k_ytmp = [b.tk() for b in ytmp]
    yjunk = [S.alloc("yjunk%d" % i, [128, 512], BF16) for i in range(2)]
    k_yjunk = [b.tk() for b in yjunk]
    HB = S.alloc("HB", [128, 7, 8, NO], BF16); k_HB = HB.tks_n(7)
    SWM = S.alloc("SWM", [128, 7, NO], BF16); k_SWM = SWM.tk()
    NAM = S.alloc("NAM", [128, 7, NO], BF16); k_NAM = NAM.tk()
    hbst = [S.alloc("hbst%d" % i, [128, 8, NO], F32) for i in range(2)]
    k_hbst = [b.tk() for b in hbst]
    S.dma('pool', SWM[:], di["swmask"].ap().rearrange("k p n -> p k n"), writes=[k_SWM])
    S.dma('pool', NAM[:], di["namask"].ap().rearrange("k p n -> p k n"), writes=[k_NAM])
    rs = di["rpbsrc"]
    def hb_step(kt):
        b = kt % 2
        for a in range(2):
            u0 = 13 - 2 * kt + a
            for qr in range(4):
                S.dma('sp', hbst[b][64 * a:64 * a + 64, :, 1 + 64 * qr:65 + 64 * qr],
                      bass.AP(rs, (u0 + qr) * 127, [[1, 64], [19 * 127, 8], [1, 64]]), pwrites=[k_hbst[b]])
            S.dma('sp', hbst[b][64 * a:64 * a + 64, :, 0:1],
                  bass.AP(rs, (u0 - 1) * 127 + 63, [[1, 64], [19 * 127, 8], [1, 1]]), pwrites=[k_hbst[b]], allow_slow_non_contiguous=True)
            S.dma('sp', hbst[b][64 * a:64 * a + 64, :, 257:258],
                  bass.AP(rs, (u0 + 4) * 127, [[1, 64], [19 * 127, 8], [1, 1]]), pwrites=[k_hbst[b]], allow_slow_non_contiguous=True)
        nam_b = bass.AP(NAM.t, kt * NO, [[7 * NO, 128], [0, 8], [1, NO]])
        S.op('pool', lambda e, kt=kt, b=b, nam_b=nam_b: e.tensor_tensor(out=HB[:, kt, :, :], in0=hbst[b][:], in1=nam_b, op=ALU.add),
             reads=[k_hbst[b], k_NAM], writes=[k_HB[kt]])
    if cut(3):
        return nc, S

    acnt = {'s': 0, 'pt': 0, 'od': 0, 'od4': 0, 'rb': 0, 'y': 0}

    def s_slot():
        sl = acnt['s'] % 2; acnt['s'] += 1
        return sl

    def attention(dus):
        LAG = 2
        FINLAG = 1
        FINLAG_B = 4
        finq = []
        finq_b = []
        pend = []
        pair_od = {}
        for idx in range(len(dus) + LAG):
            if idx < len(dus):
                u = dus[idx]
                sl = s_slot()
                pt = acnt['pt'] % NPT; acnt['pt'] += 1
                b0, b1 = 2 * sl, 2 * sl + 1
                S.op('pe', lambda e, u=u, b0=b0, b1=b1: u['qk'](e, ps[b0], ps[b1]), reads=u['qk_reads'], writes=[kps[b0], kps[b1]])
                n = u['n']
                for h2 in range(2):
                    bk = (b0, b1)[h2]
                    bias = u['bias'][h2]
                    if bias is not None:
                        bap, btk = bias
                        S.op('dve', lambda e, bk=bk, sl=sl, h2=h2, bap=bap, n=n: e.tensor_tensor(out=SB[sl][h2][:, 0:n], in0=ps[bk][:, 0:n], in1=bap, op=ALU.add),
                             reads=[kps[bk], btk], writes=[k_SB[sl][h2]])
                        S.op('act', lambda e, sl=sl, h2=h2, pt=pt, n=n: e.activation(out=PT[pt][h2][:, 0:n], in_=SB[sl][h2][:, 0:n], func=AF.Exp),
                             reads=[k_SB[sl][h2]], writes=[k_PT[pt][h2]])
                    else:
                        S.op('act', lambda e, bk=bk, h2=h2, pt=pt, n=n: e.activation(out=PT[pt][h2][:, 0:n], in_=ps[bk][:, 0:n], func=AF.Exp),
                             reads=[kps[bk]], writes=[k_PT[pt][h2]])
                pend.append((u, pt))
            if idx >= LAG:
                u, pt = pend[idx - LAG]
                pi = u['pair']
                if pi not in pair_od:
                    if u['n'] == 512:
                        b = 4 + (acnt['od4'] % 4); acnt['od4'] += 1
                        pair_od[pi] = (b, b, ps[b][:, 0:256], ps[b][:, 256:512])
                    else:
                        od = acnt['od'] % 2; acnt['od'] += 1
                        pair_od[pi] = (4 + od, 6 + od, ps[4 + od], ps[6 + od])
                bO, bD, apO, apD = pair_od[pi]
                first = u['first']
                bks = [kps[bO]] if bO == bD else [kps[bO], kps[bD]]
                S.op('pe', lambda e, u=u, pt=pt, apO=apO, apD=apD: u['pv'](e, PT[pt][0], PT[pt][1], apO, apD),
                     reads=[k_PT[pt][0], k_PT[pt][1]] + u['pv_reads'],
                     writes=(bks if first else []), pwrites=([] if first else bks))
                if u['last']:
                    finq.append((idx + FINLAG, u['fin'], pair_od[pi]))
                    if u.get('fin_b') is not None:
                        finq_b.append((idx + FINLAG_B, u['fin_b']))
                for hk in u['hooks']:
                    hk()
            while finq and (finq[0][0] <= idx or idx == len(dus) + LAG - 1):
                _, f, od_ = finq.pop(0)
                f(od_)
            while finq_b and (finq_b[0][0] <= idx or idx == len(dus) + LAG - 1):
                _, f = finq_b.pop(0)
                f()

    def finish_pair_common(i, od, N, is_b):
        bO, bD, apO, apD = od
        rb = acnt['rb'] % 2; acnt['rb'] += 1
        if is_b:
            S.op('act', lambda e: e.activation(out=Rb[rb][:, 0:N], in_=apD[:, 0:N], func=AF.Ln, bias=esT[:, i - 4:i - 3]),
                 reads=[kps[bD], k_es], writes=[k_Rb[rb]])
        else:
            S.op('act', lambda e: e.activation(out=Rb[rb][:, 0:N], in_=apD[:, 0:N], func=AF.Ln), reads=[kps[bD]], writes=[k_Rb[rb]])
        S.op('act', lambda e: e.activation(out=Rb[rb][:, 0:N], in_=Rb[rb][:, 0:N], func=AF.Exp, scale=-1.0), reads=[k_Rb[rb]], writes=[k_Rb[rb]])
        S.op('dve', lambda e: e.tensor_tensor(out=OT[:, i, 0:N], in0=apO[:, 0:N], in1=Rb[rb][:, 0:N], op=ALU.mult),
             reads=[kps[bO], k_Rb[rb]], writes=[k_OT[i]])
        S.op('dve', lambda e: e.tensor_tensor(out=sq[:, i, 0:N], in0=OT[:, i, 0:N], in1=OT[:, i, 0:N], op=ALU.mult),
             reads=[k_OT[i]], writes=[k_sq[i]])

    def group_norm(grp, N, ob):
        OTn = OTn_b[ob]; k_OTn = k_OTn_b[ob]
        sb = 2 * s_slot()

        def mm(e):
            for c in range(4):
                ins = e.matmul(ps[sb][:, 0:N], lhsT=onesb[:], rhs=sq[:, grp * 4 + c, 0:N], start=(c == 0), stop=(c == 3))
            return ins
        S.op('pe', mm, reads=[k_onesb] + k_sq[grp * 4:grp * 4 + 4], writes=[kps[sb]])
        S.op('act', lambda e: e.activation(out=rg[:, 0:N], in_=ps[sb][:, 0:N], func=AF.Ln, scale=1.0 / 512, bias=epsb[:, 0:1]),
             reads=[kps[sb], k_eps], writes=[k_rg])
        S.op('act', lambda e: e.activation(out=rg[:, 0:N], in_=rg[:, 0:N], func=AF.Exp, scale=-0.5), reads=[k_rg], writes=[k_rg])
        for c in range(4):
            i = grp * 4 + c
            S.op('dve', lambda e, i=i: e.scalar_tensor_tensor(out=OTn[:, i, 0:N], in0=OT[:, i, 0:N], scalar=ggT[:, i:i + 1], in1=rg[:, 0:N],
                                                              op0=ALU.mult, op1=ALU.mult), reads=[k_OT[i], k_gg, k_rg], writes=[k_OTn[i]])

    def post_norm_residual(x_ap, k_x, p, mm_fn, mm_reads, which, r, banks):
        yi = acnt['y'] % 2; acnt['y'] += 1
        si = cnt['sm'] % NSM; cnt['sm'] += 1
        sm = small[0:p, si, :]; ksm = k_small[si]
        for hf in range(2):
            pb = banks[hf]
            S.op('pe', lambda e, hf=hf, pb=pb: mm_fn(e, ps[pb], hf), reads=mm_reads, writes=[kps[pb]])
            S.op('act', lambda e, hf=hf, pb=pb: e.activation(out=yjunk[hf][0:p, :], in_=ps[pb][0:p, :], func=AF.Square, accum_out=sm[:, hf:hf + 1]),
                 reads=[kps[pb]], writes=[k_yjunk[hf]], pwrites=[ksm])
            S.op('dve', lambda e, hf=hf, pb=pb: e.tensor_copy(out=ytmp[yi][0:p, hf * 512:(hf + 1) * 512], in_=ps[pb][0:p, :]),
                 reads=[kps[pb]], pwrites=[k_ytmp[yi]])
        S.op('dve', lambda e: e.tensor_tensor(out=sm[:, 2:3], in0=sm[:, 0:1], in1=sm[:, 1:2], op=ALU.add), reads=[ksm], writes=[ksm])
        rstd_from_ssq(sm[:, 2:3], sm[:, 3:4], p, ksm, 1.0 / D)
        S.op('dve', lambda e: e.scalar_tensor_tensor(out=ytmp[yi][0:p, :], in0=ytmp[yi][0:p, :], scalar=sm[:, 3:4], in1=Gbc[0:p, which, r, :],
                                                     op0=ALU.mult, op1=ALU.mult), reads=[ksm, k_Gbc], writes=[k_ytmp[yi]])
        S.op('dve', lambda e: e.tensor_tensor(out=x_ap, in0=x_ap, in1=ytmp[yi][0:p, :], op=ALU.add), reads=[k_ytmp[yi]], writes=[k_x])

    def wout_tile(x_ap, k_x, p, cols, r, ob):
        OTn = OTn_b[ob]; k_OTn = k_OTn_b[ob]

        def mm_fn(e, pst, hf):
            for c in range(8):
                ins = e.matmul(pst[0:p, :], lhsT=OTn[:, c, cols], rhs=wo[:, c, hf * 512:(hf + 1) * 512], start=(c == 0), stop=(c == 7))
            return ins
        sl = s_slot()
        post_norm_residual(x_ap, k_x, p, mm_fn, [k_wo] + k_OTn, 0, r, (2 * sl, 2 * sl + 1))

    def head_src(i, pi_):
        if i < 4:
            return i, 2 * i + pi_, 2 * i + pi_
        hb = 2 * (i - 4) + pi_
        kvh = hb // 4
        return 4 + kvh, 8 + kvh, None

    all_dus = []
    for s in range(2):
        for i in range(8):
            src = [head_src(i, 0), head_src(i, 1)]

            def qk(e, pA, pB, i=i, s=s, src=src):
                for kt in range(2):
                    for pi_, pst in ((0, pA), (1, pB)):
                        lo = 64 * pi_
                        kc = src[pi_][0]
                        ins = e.matmul(pst[:, kt * 256:(kt + 1) * 256], lhsT=KT_P[lo:lo + 64, kc, s * 256 + kt * 128: s * 256 + (kt + 1) * 128],
                                       rhs=QT_P[lo:lo + 64, i, s * 256:(s + 1) * 256], start=True, stop=True)
                return ins

            def pv(e, pt0, pt1, pO, pD, s=s, src=src):
                for kt in range(2):
                    for pi_, ptb in ((0, pt0), (1, pt1)):
                        lo = 64 * pi_
                        e.matmul(pO[lo:lo + 64, 0:256], lhsT=V_P[:, 2 * s + kt, src[pi_][1], :], rhs=ptb[:, kt * 256:(kt + 1) * 256],
                                 start=(kt == 0), stop=(kt == 1), tile_position=(0, lo))
                for kt in range(2):
                    for pi_, ptb in ((0, pt0), (1, pt1)):
                        lo = 64 * pi_
                        ins = e.matmul(pD[lo:lo + 64, 0:256], lhsT=ones64[:], rhs=ptb[:, kt * 256:(kt + 1) * 256],
                                       start=(kt == 0), stop=(kt == 1), tile_position=(0, lo))
                return ins

            def fin(od, i=i, s=s):
                finish_pair_common(i, od, 256, i >= 4)
                if s == 0 and i < 7:
                    hb_step(i)
            fin_b = (lambda i=i, s=s: group_norm(i // 4, 256, s)) if i % 4 == 3 else None
            all_dus.append(dict(qk=qk, qk_reads=[k_KTP[src[0][0]], k_KTP[src[1][0]], k_QTP[i]], pv=pv,
                                pv_reads=[k_VP[2 * s], k_VP[2 * s + 1], k_ones64], n=512, bias=(None, None), pair=(s, i),
                                first=True, last=True, fin=fin, fin_b=fin_b, hooks=[]))

    def wout_prompt(s):
        for t2 in range(2):
            tt = 2 * s + t2
            wout_tile(xP[:, tt, :], k_xP[tt], 128, slice(t2 * 128, (t2 + 1) * 128), 0, s)
    all_dus[8 + 4]['hooks'].append(lambda: wout_prompt(0))

    for i in (4, 5, 6, 7, 0, 1, 2, 3):
        src = [head_src(i, 0), head_src(i, 1)]
        for kt in range(9):
            if kt < 7:
                ks = [KT_W[64 * p_:64 * p_ + 64, src[p_][0], kt * 128:(kt + 1) * 128] for p_ in range(2)]
                kk = [k_KTW[src[0][0]], k_KTW[src[1][0]]]
                vs = [V_W[:, kt, src[p_][1], :] for p_ in range(2)]
                kv_ = k_VW[kt]
                if i < 4:
                    bias = (None, None)
                    hb = [HB[:, kt, src[p_][2], :] for p_ in range(2)]
                    kk = kk + [k_antib, k_HB[kt]]
                else:
                    bias = ((SWM[:, kt, :], k_SWM), (SWM[:, kt, :], k_SWM))
                    hb = None
            else:
                ks = [KT_C[64 * p_:64 * p_ + 64, src[p_][0], (kt - 7) * 128:(kt - 6) * 128] for p_ in range(2)]
                kk = [k_KTC[src[0][0]], k_KTC[src[1][0]]]
                vs = [V_C[:, kt - 7, src[p_][1], :] for p_ in range(2)]
                kv_ = k_VC[kt - 7]
                bias = (None, None)
                hb = None

            def qk(e, pA, pB, i=i, ks=ks, hb=hb):
                e.matmul(pA[:, 0:NO], lhsT=ks[0], rhs=QT_O[0:64, i, :], start=True, stop=(hb is None))
                ins = e.matmul(pB[:, 0:NO], lhsT=ks[1], rhs=QT_O[64:128, i, :], start=True, stop=(hb is None))
                if hb is not None:
                    e.matmul(pA[:, 0:NO], lhsT=antib[:], rhs=hb[0], start=False, stop=True)
                    ins = e.matmul(pB[:, 0:NO], lhsT=antib[:], rhs=hb[1], start=False, stop=True)
                return ins

            def pv(e, pt0, pt1, pO, pD, vs=vs, kt=kt):
                e.matmul(pO[0:64, 0:NO], lhsT=vs[0], rhs=pt0[:, 0:NO], start=(kt == 0), stop=(kt == 8), tile_position=(0, 0))
                e.matmul(pO[64:128, 0:NO], lhsT=vs[1], rhs=pt1[:, 0:NO], start=(kt == 0), stop=(kt == 8), tile_position=(0, 64))
                e.matmul(pD[0:64, 0:NO], lhsT=ones64[:], rhs=pt0[:, 0:NO], start=(kt == 0), stop=(kt == 8), tile_position=(0, 0))
                return e.matmul(pD[64:128, 0:NO], lhsT=ones64[:], rhs=pt1[:, 0:NO], start=(kt == 0), stop=(kt == 8), tile_position=(0, 64))

            def fin_s(od, i=i):
                finish_pair_common(i, od, NO, i >= 4)
            fin_b = (lambda i=i: group_norm(i // 4, NO, 0)) if (i % 4 == 3 and kt == 8) else None
            all_dus.append(dict(qk=qk, qk_reads=kk + [k_QTO[i]], pv=pv, pv_reads=[kv_, k_ones64], n=NO, bias=bias, pair=(2, i),
                                first=(kt == 0), last=(kt == 8), fin=fin_s, fin_b=fin_b, hooks=[]))
    all_dus[16 + 5]['hooks'].append(lambda: wout_prompt(1))
    ffn_pre = {}

    def np_setup():
        S.free(*hbst, NAM)
        ffn_pre['xn'] = [S.alloc("xn%d" % i, [128, D], BF16) for i in range(2)]
        ffn_pre['junk'] = [S.alloc("junk%d" % i, [128, D], BF16) for i in range(2)]
        ffn_pre['h2P'] = S.alloc("h2P", [128, 8, 512], BF16)
        ffn_pre['k_h2P'] = ffn_pre['h2P'].tks_n(4)
        nonlocal_set(ffn_pre['xn'], ffn_pre['junk'])
        ffn_pre['tmpN'] = S.alloc("tmpN", [128, 8, 128], F32)
        nb['tmpN'] = ffn_pre['tmpN']; nb['k_tmpN'] = ffn_pre['tmpN'].tk()
        tl = []
        for tt in range(4):
            tl.append((xP[:, tt, :], k_xP[tt], 128, ffn_pre['h2P'], ffn_pre['k_h2P'][tt], slice(tt * 128, (tt + 1) * 128), 1, 0, None))
        ffn_pre['thunks'] = norm_steps(tl, 'dve')
    def wu_early():
        S.free(QT_P, KT_P, V_P)
        ffn_pre['wu'] = [S.alloc("wu%d" % i, [128, 8, 2, 128], BF16) for i in range(4)]
        ffn_pre['k_wu'] = [b.tk() for b in ffn_pre['wu']]
        for g in range(3):
            for gv in range(2):
                S.dma('pool', ffn_pre['wu'][g][:, :, gv, :], wup_v[:, :, gv * DFF + g * 128:gv * DFF + (g + 1) * 128], pwrites=[ffn_pre['k_wu'][g]])
    wup_v = di["w_up"].ap().rearrange("(kc p) n -> p kc n", p=128)
    all_dus[16 + 20]['hooks'].append(wu_early)
    all_dus[16 + 36]['hooks'].append(np_setup)
    for k in range(5):
        all_dus[16 + 38 + 6 * k]['hooks'].append(lambda k=k: ffn_pre['thunks'][k]())

    def np_done():
        S.free(ffn_pre['tmpN'])
        nb['tmpN'] = ytmp[0].t[:].rearrange("p (k n) -> p k n", k=8)
        nb['k_tmpN'] = k_ytmp[0]
    all_dus[16 + 38 + 6 * 4]['hooks'].append(np_done)
    attention(all_dus)
    for tt, p in OTOK:
        wout_tile(xO[0:p, tt, :], k_xO[tt], p, ocols(tt), 1, 0)
    S.free(QT_O, KT_W, V_W, KT_C, V_C, HB, SWM, OT, sq, rg, wo, *Rb, *OTn_b)
    for row in PT:
        S.free(*row)
    for row in SB:
        S.free(*row)
    if cut(5):
        return nc, S

    wd = S.alloc("wd", [128, NPAIR, D], BF16); k_wd = wd.tk()
    h2P = ffn_pre['h2P']; k_h2P = ffn_pre['k_h2P']
    h2O = S.alloc("h2O", [128, 8, NO], BF16); k_h2O = h2O.tks_n(3)
    aP = S.alloc("aP", [128, NPAIR, 512], BF16); k_aP = aP.tks_n(NPAIR)
    aO = S.alloc("aO", [128, NPAIR, 256], BF16); k_aO = aO.tks_n(NPAIR)
    NWU = 7
    LAGO = 3
    wu = ffn_pre['wu'] + [S.alloc("wu%d" % i, [128, 8, 2, 128], BF16) for i in range(4, NWU)]
    k_wu = ffn_pre['k_wu'] + [b.tk() for b in wu[4:]]

    def load_wu(g):
        b = g % NWU
        S.dma('pool', wu[b][:, :, 0, :], wup_v[:, :, g * 128:(g + 1) * 128], pwrites=[k_wu[b]])
        S.dma('pool', wu[b][:, :, 1, :], wup_v[:, :, DFF + g * 128:DFF + (g + 1) * 128], pwrites=[k_wu[b]])
    wd_v = di["w_down"].ap().rearrange("(g p) n -> p g n", p=128)

    def load_wd(q):
        lo, hi = [(0, 6), (6, 12), (12, 17), (17, 22)][q]
        S.dma('pool', wd[:, lo:hi, :], wd_v[:, lo:hi, :], pwrites=[k_wd])
    tiles = []
    for tt, p in OTOK:
        tiles.append((xO[0:p, tt, :], k_xO[tt], p, h2O, k_h2O[tt], ocols(tt), 1, 1, None))
    ns_thunks = norm_steps(tiles, 'dve')

    def ns_flags():
        S.op('dve', lambda e: e.tensor_scalar_mul(out=h2O[:, :, 0:1], in0=h2O[:, :, 0:1], scalar1=flg[:, 0:1]), reads=[k_flg], writes=[k_h2O[2]])
        S.op('dve', lambda e: e.tensor_scalar_mul(out=h2O[:, :, NO - 1:NO], in0=h2O[:, :, NO - 1:NO], scalar1=flg[:, 1:2]), reads=[k_flg], writes=[k_h2O[2]])

    NT = 3
    pend_ep = []
    ta = [[S.alloc("ta%d_%d" % (i, j), [128, 512], F32) for j in range(2)] for i in range(NT)]
    k_ta = [[b.tk() for b in row] for row in ta]
    tb = [[S.alloc("tb%d_%d" % (i, j), [128, 512], F32) for j in range(2)] for i in range(2)]
    k_tb = [[b.tk() for b in row] for row in tb]
    fcnt = {'t': 0}

    def conv_epilogue(pbg, pbv, g, N, is_p):
        ti = fcnt['t'] % NT; fcnt['t'] += 1
        for gv, (pb, ch) in enumerate(((pbg, g), (pbv, NPAIR + g))):
            a_ = ta[ti][gv]; ka = k_ta[ti][gv]
            b_ = tb[ti % 2][gv]; kb = k_tb[ti % 2][gv]
            if is_p:
                a3 = a_[:, 0:512].rearrange("p (s n) -> p s n", s=2)
                b3 = b_[:, 0:512].rearrange("p (s n) -> p s n", s=2)
                u3 = ps[pb][:, 0:512].rearrange("p (s n) -> p s n", s=2)
                a_hi, a_lo, b_hi, u_lo, u_hi = a3[:, :, 1:256], a3[:, :, 0:255], b3[:, :, 1:256], u3[:, :, 0:255], u3[:, :, 1:256]
            else:
                a_hi, a_lo, b_hi, u_lo, u_hi = a_[:, 1:NO], a_[:, 0:NO - 1], b_[:, 1:NO], ps[pb][:, 0:NO - 1], ps[pb][:, 1:NO]
            S.op('act', lambda e, a_=a_, pb=pb, ch=ch: e.activation(out=a_[:, 0:N], in_=ps[pb][:, 0:N], func=AF.Identity,
                                                                    bias=cbT[:, ch:ch + 1], scale=cwT[:, ch, 1:2]),
                 reads=[kps[pb], k_cw, k_cb], writes=[ka])
            S.op('act', lambda e, b_hi=b_hi, u_lo=u_lo, ch=ch: e.activation(out=b_hi, in_=u_lo, func=AF.Copy, scale=cwT[:, ch, 0:1]),
                 reads=[kps[pb], k_cw], writes=[kb])
            S.op('dve', lambda e, a_lo=a_lo, u_hi=u_hi, ch=ch: e.scalar_tensor_tensor(out=a_lo, in0=u_hi, scalar=cwT[:, ch, 2:3], in1=a_lo,
                                                                                      op0=ALU.mult, op1=ALU.add), reads=[kps[pb], k_cw], writes=[ka])
            S.op('dve' if is_p else 'pool', lambda e, a_hi=a_hi, b_hi=b_hi: e.tensor_tensor(out=a_hi, in0=a_hi, in1=b_hi, op=ALU.add), reads=[kb], writes=[ka])
        pend_ep.append((ti, g, N, is_p))
        if len(pend_ep) > 1:
            conv_part2(*pend_ep.pop(0))

    def conv_part2(ti, g, N, is_p):
        S.op('act', lambda e: e.activation(out=ta[ti][0][:, 0:N], in_=ta[ti][0][:, 0:N], func=AF.Silu), reads=[k_ta[ti][0]], writes=[k_ta[ti][0]])
        if is_p:
            S.op('dve', lambda e: e.tensor_tensor(out=aP[:, g, :], in0=ta[ti][0][:, 0:512], in1=ta[ti][1][:, 0:512], op=ALU.mult),
                 reads=[k_ta[ti][0], k_ta[ti][1]], writes=[k_aP[g]])
        else:
            S.op('dve', lambda e: e.tensor_tensor(out=aO[:, g, :], in0=ta[ti][0][:, 1:257], in1=ta[ti][1][:, 1:257], op=ALU.mult),
                 reads=[k_ta[ti][0], k_ta[ti][1]], writes=[k_aO[g]])

    def ffn_mm(g, hT, khT, N, off):
        b = g % NWU
        base = 4 * (g % 2)
        for gv in range(2):
            pb = base + off + gv

            def mm(e, pb=pb, gv=gv):
                for kc in range(8):
                    ins = e.matmul(ps[pb][:, 0:N], lhsT=wu[b][:, kc, gv, :], rhs=hT[:, kc, 0:N], start=(kc == 0), stop=(kc == 7))
                return ins
            S.op('pe', mm, reads=[k_wu[b]] + khT, writes=[kps[pb]])
        conv_epilogue(base + off, base + off + 1, g, N, off == 0)

    nb['tp_base'] = 2
    for g in range(NPAIR + LAGO):
        if g < NPAIR:
            if g + 3 < NPAIR:
                load_wu(g + 3)
            if g in (5, 9, 13, 17):
                load_wd((g - 5) // 4)
            ffn_mm(g, h2P, k_h2P, 512, 0)
        if g < len(ns_thunks):
            ns_thunks[g]()
            if g == len(ns_thunks) - 1:
                ns_flags()
                S.free(*nb['xn'], *nb['junk'])
        if g >= LAGO:
            ffn_mm(g - LAGO, h2O, k_h2O, NO, 2)
    while pend_ep:
        conv_part2(*pend_ep.pop(0))
    S.free(h2P, h2O, *wu)
    for row in ta:
        S.free(*row)
    for row in tb:
        S.free(*row)
    if cut(6):
        return nc, S

    dcnt = {'b': 0}

    def wdown_tile(x_ap, k_x, aT, k_aT, cols, r, out_ap):
        def mm_fn(e, pst, hf):
            for g in range(NPAIR):
                ins = e.matmul(pst[:, :], lhsT=aT[:, g, cols], rhs=wd[:, g, hf * 512:(hf + 1) * 512], start=(g == 0), stop=(g == NPAIR - 1))
            return ins
        b0 = 2 * (dcnt['b'] % 4); dcnt['b'] += 1
        post_norm_residual(x_ap, k_x, 128, mm_fn, [k_wd] + k_aT, 1, r, (b0, b0 + 1))
        S.dma('sp', out_ap, x_ap, reads=[k_x])

    for tt in range(4):
        wdown_tile(xP[:, tt, :], k_xP[tt], aP, k_aP, slice(tt * 128, (tt + 1) * 128), 0, do["y_p"].ap()[tt * 128:(tt + 1) * 128, :])
    for tt in range(2):
        wdown_tile(xO[:, tt, :], k_xO[tt], aO, k_aO, slice(tt * 128, (tt + 1) * 128), 1, do["y_s"].ap()[tt * 128:(tt + 1) * 128, :])

    S.finish(list(ALL_TKS))
    return nc, S


def _host_consts(j):
    ws = [0, 0, 2, 2][j]
    qpos = np.concatenate([[256 * j - 1], 256 * j + np.arange(256), [256 * j + 256]])
    qvalid = (qpos >= 0) & (qpos < 1024)
    qp = np.clip(qpos, 0, 1023)
    kpos = 64 * ws + np.arange(NW)
    r, c = qp // 64, qp % 64
    rk, ck = kpos // 64, kpos % 64
    row_start = np.clip(r - 4, 0, 8)
    col_start = np.clip(c - 8, 0, 48)
    na_valid = ((rk[:, None] >= row_start[None, :]) & (rk[:, None] < row_start[None, :] + 8) &
                (ck[:, None] >= col_start[None, :]) & (ck[:, None] < col_start[None, :] + 16) & qvalid[None, :])
    na = np.where(na_valid, 0.0, NEG).astype(np.float32).reshape(7, 128, NO)
    namask = np.ascontiguousarray(na[:, ::-1, :])
    sw_valid = (np.abs(qp[None, :] - kpos[:, None]) <= 128) & qvalid[None, :]
    swmask = np.where(sw_valid, 0.0, NEG).astype(np.float32).reshape(7, 128, NO)

    def rope_tab(pos):
        n = 16
        inv = (1.0 / (10000.0 ** (np.arange(n, dtype=np.float32) / n))).astype(np.float32)
        pr = (pos // 64).astype(np.float32)
        pc = (pos % 64).astype(np.float32)
        cos = np.zeros((64, len(pos)), np.float32)
        sin = np.zeros((64, len(pos)), np.float32)
        for d in range(64):
            p_ = pr if d < 32 else pc
            ang = (p_ * inv[d % 16]).astype(np.float32)
            cos[d] = np.cos(ang)
            sin[d] = np.sin(ang)
        return np.stack([np.concatenate([cos, cos], 0), np.concatenate([sin, sin], 0)]).astype(np.float32)
    ropeq = rope_tab(qp)
    ropek = rope_tab(kpos)
    flags = np.zeros((128, 2), np.float32)
    flags[:, 0] = 1.0 if j > 0 else 0.0
    flags[:, 1] = 1.0 if j < 3 else 0.0
    return ws, namask, swmask, ropeq, ropek, flags


def _rpbsrc(rpb, j, ws):
    out = np.zeros((8, 19, 127), np.float32)
    delta = ws - 4 * j
    for up in range(19):
        u = 18 - up
        i = (u - 4) + delta + 7
        if 0 <= i < 15:
            out[:, up, 48:79] = rpb[:, i, ::-1]
    return out


_CACHE = {}


def kernel(x_prompt, x_sample, cache_a_k, cache_a_v, cache_b_k, cache_b_v, c, c_ctx,
           w_mod, b_mod, g_mix_pre, g_mix_post, g_ffn_pre, g_ffn_post, w_in, rpb_a, sink_b,
           g_grp_a, g_grp_b, w_out, w_up, conv_w, conv_b, w_down):
    f = lambda a: np.ascontiguousarray(np.asarray(a, dtype=np.float32))
    x_prompt, x_sample = f(x_prompt), f(x_sample)
    if 'nc' not in _CACHE:
        _CACHE['nc'] = build_program()
    nc, S = _CACHE['nc']
    shared = {
        "w_mod": f(w_mod[0]), "b_mod": f(b_mod[0]),
        "gains": f(np.concatenate([g_mix_pre[0], g_mix_post[0], g_ffn_pre[0], g_ffn_post[0]])),
        "w_in": f(w_in[0]), "w_out": f(w_out[0]), "w_up": f(w_up[0]), "w_down": f(w_down[0]),
        "cwT": f(np.asarray(conv_w[0]).T.reshape(44, 128, 3).transpose(1, 0, 2)),
        "cbT": f(np.asarray(conv_b[0]).reshape(44, 128).T),
        "ggT": f(np.concatenate([g_grp_a[0], g_grp_b[0]]).reshape(8, 128).T),
        "ident": np.eye(128, dtype=np.float32),
        "antij": np.ascontiguousarray(np.eye(128, dtype=np.float32)[::-1]),
        "sel": np.stack([np.stack([np.ones(128), np.zeros(128)]), np.stack([np.zeros(128), np.ones(128)])], 1).astype(np.float32),
    }
    sk = np.asarray(sink_b[0], np.float32).reshape(8)
    sinkT = np.zeros((128, 4), np.float32)
    for i in range(4):
        sinkT[0:64, i] = sk[2 * i]
        sinkT[64:128, i] = sk[2 * i + 1]
    shared["sinkT"] = sinkT
    in_maps = []
    for core in range(8):
        b, j = core // 4, core % 4
        ws, namask, swmask, ropeq, ropek, flags = _host_consts(j)
        xs = x_sample[b]
        halo = np.zeros((2, D), np.float32)
        if j > 0:
            halo[0] = xs[256 * j - 1]
        if j < 3:
            halo[1] = xs[256 * j + 256]
        cond = np.stack([np.asarray(c_ctx, np.float32), np.asarray(c[b], np.float32)], 1)
        m = dict(shared)
        m.update({
            "xp": f(x_prompt[2 * core:2 * core + 2].reshape(512, D)),
            "xo": f(xs[256 * j:256 * j + 256]), "xh": halo, "xw": f(xs[64 * ws:64 * ws + NW]),
            "cak": f(np.asarray(cache_a_k)[b, 0].reshape(256, 512)), "cav": f(np.asarray(cache_a_v)[b, 0].reshape(256, 512)),
            "cbk": f(np.asarray(cache_b_k)[b, 0].reshape(256, 128)), "cbv": f(np.asarray(cache_b_v)[b, 0].reshape(256, 128)),
            "condT": f(cond.reshape(8, 128, 2).transpose(1, 0, 2)),
            "rpbsrc": _rpbsrc(np.asarray(rpb_a[0], np.float32), j, ws),
            "namask": namask, "swmask": swmask, "ropeq": ropeq, "ropek": ropek, "flags": flags,
        })
        in_maps.append(m)
    res = run_bass_kernel_spmd(nc, in_maps, core_ids=list(range(8)))
    R = res.results
    y_p = np.concatenate([R[i]["y_p"].reshape(2, 256, D) for i in range(8)], 0)
    y_s = np.stack([np.concatenate([R[4 * b + j]["y_s"] for j in range(4)], 0) for b in range(2)], 0)
    nak = np.concatenate([R[i]["nak"].reshape(2, 1, 256, 8, 64) for i in range(8)], 0)
    nav = np.concatenate([R[i]["nav"].reshape(2, 1, 256, 8, 64) for i in range(8)], 0)
    nbk = np.concatenate([R[i]["nbk"].reshape(2, 1, 256, 2, 64) for i in range(8)], 0)
    nbv = np.concatenate([R[i]["nbv"].reshape(2, 1, 256, 2, 64) for i in range(8)], 0)
    return (y_p.astype(np.float32), y_s.astype(np.float32), nak.astype(np.float32), nav.astype(np.float32),
            nbk.astype(np.float32), nbv.astype(np.float32))
```

```python
import numpy as np
import concourse.bass as bass
import concourse.mybir as mybir
from concourse.bass_utils import run_bass_kernel_spmd

F32 = mybir.dt.float32
BF16 = mybir.dt.bfloat16
AF = mybir.ActivationFunctionType
ALU = mybir.AluOpType

NDS = 48
EPS = 1e-6
NEG = -30000.0
D = 1024
NW = 896
NO = 258
DFF = 2816
NPAIR = 22


def _merge(d, o):
    for k, v in o.items():
        if d.get(k, 0) < v:
            d[k] = v


ALL_TKS = []


class Tk:
    def __init__(s, name="t", excl=False):
        ALL_TKS.append(s)
        s.name = name
        s.w = {}
        s.r = {}
        s.old = {}
        s.excl = excl
        s.acc = {}

    def all_tokens(s):
        d = {}
        _merge(d, s.w); _merge(d, s.r); _merge(d, s.old)
        return d


class Buf:
    def __init__(s, t, lo, hi, name):
        s.t = t
        s.lo = lo
        s.hi = hi
        s.name = name
        s.tks = []
        s.ghost = {}

    def tk(s, name=None):
        k = Tk(name or s.name)
        _merge(k.old, s.ghost)
        s.tks.append(k)
        return k

    def tks_n(s, n):
        return [s.tk("%s%d" % (s.name, i)) for i in range(n)]

    def __getitem__(s, key):
        return s.t[key]


class Sched:
    def __init__(s, nc, sbuf_lo=16512, sbuf_hi=229312):
        s.nc = nc
        s.eng = {'pe': nc.tensor, 'act': nc.scalar, 'dve': nc.vector, 'pool': nc.gpsimd, 'sp': nc.sync}
        s.sem = {k: nc.alloc_semaphore("sem_" + k) for k in s.eng}
        s.cnt = {k: 0 for k in s.eng}
        s.waited = {k: {} for k in s.eng}
        s.dsem = [nc.alloc_semaphore("dsem%d" % i) for i in range(NDS)]
        s.dcnt = [0] * NDS
        s.dnext2 = [0, 0]
        s.semobj = {}
        for k in s.eng:
            s.semobj[('e', k)] = s.sem[k]
        for i in range(NDS):
            s.semobj[('d', i)] = s.dsem[i]
        s.nwaits = 0
        s.nops = {k: 0 for k in s.eng}
        s.lo = sbuf_lo
        s.hi = sbuf_hi
        s.live = []
        s.ghosts = []
        s.uid = 0
        s.peak = 0

    def alloc(s, name, shape, dtype, align=64):
        size = int(np.prod(shape[1:])) * mybir.dt.size(dtype)
        size = (size + align - 1) // align * align
        ivs = sorted((b.lo, b.hi) for b in s.live)
        pos = s.lo
        found = None
        for lo, hi in ivs:
            if lo - pos >= size:
                found = pos
                break
            pos = max(pos, hi)
        if found is None:
            if s.hi - pos >= size:
                found = pos
            else:
                raise RuntimeError("SBUF OOM allocating %s size %d; live=%s" % (
                    name, size, [(b.name, b.lo, b.hi) for b in s.live]))
        s.uid += 1
        t = s.nc.alloc_sbuf_tensor_at("%s_%d" % (name, s.uid), list(shape), dtype, offset=found)
        b = Buf(t, found, found + size, name)
        g = {}
        for lo, hi, tok in s.ghosts:
            if lo < b.hi and b.lo < hi:
                _merge(g, tok)
        b.ghost = g
        s.live.append(b)
        s.peak = max(s.peak, b.hi)
        return b

    def free(s, *bufs):
        for b in bufs:
            tok = dict(b.ghost)
            for k in b.tks:
                _merge(tok, k.all_tokens())
            s.ghosts.append((b.lo, b.hi, tok))
            s.live.remove(b)

    def _deps(s, reads, writes, pwrites, me=None):
        d = {}
        for t in list(reads) + list(writes) + list(pwrites):
            if t.excl:
                for k, v in t.acc.items():
                    if k != me and d.get(k, 0) < v:
                        d[k] = v
        for t in reads:
            _merge(d, t.w)
        for t in writes:
            _merge(d, t.w); _merge(d, t.r); _merge(d, t.old)
        for t in pwrites:
            if t.r:
                _merge(t.old, t.w); _merge(t.old, t.r)
                t.w = {}; t.r = {}
            _merge(d, t.old)
        return d

    def _wait(s, e, d):
        for key, val in d.items():
            if key == ('e', 'pe') and e == 'pe':
                continue
            if s.waited[e].get(key, 0) < val:
                s.eng[e].wait_ge(s.semobj[key], val)
                s.waited[e][key] = val
                s.nwaits += 1

    def _mark(s, key, val, reads, writes, pwrites):
        for t in list(reads) + list(writes) + list(pwrites):
            if t.excl:
                t.acc[key] = val
        for t in reads:
            if t.r.get(key, 0) < val:
                t.r[key] = val
        for t in writes:
            t.w = {key: val}; t.r = {}; t.old = {}
        for t in pwrites:
            if t.w.get(key, 0) < val:
                t.w[key] = val

    def op(s, e, fn, reads=(), writes=(), pwrites=()):
        s._wait(e, s._deps(reads, writes, pwrites, ('e', e)))
        ins = fn(s.eng[e])
        s.cnt[e] += 1
        s.nops[e] += 1
        ins.then_inc(s.sem[e], 1)
        s._mark(('e', e), s.cnt[e], reads, writes, pwrites)

    def dma(s, q, out, in_, reads=(), writes=(), pwrites=(), **kw):
        d = s._deps(reads, writes, pwrites)
        half = NDS // 2
        qi = 0 if q == 'sp' else 1
        i = qi * half + s.dnext2[qi]
        s.dnext2[qi] = (s.dnext2[qi] + 1) % half
        key = ('d', i)
        if s.dcnt[i] > 0:
            if d.get(key, 0) < s.dcnt[i]:
                d[key] = s.dcnt[i]
        s._wait(q, d)
        ins = s.eng[q].dma_start(out=out, in_=in_, **kw)
        s.dcnt[i] += 16
        s.nops[q] += 1
        ins.then_inc(s.dsem[i], 16)
        s._mark(key, s.dcnt[i], reads, writes, pwrites)

    def finish(s, tks):
        d = {}
        for t in tks:
            _merge(d, t.all_tokens())
        s._wait('sp', d)


IN_SPECS = [
    ("xp", [512, D]), ("xo", [256, D]), ("xh", [2, D]), ("xw", [NW, D]),
    ("cak", [256, 512]), ("cav", [256, 512]), ("cbk", [256, 128]), ("cbv", [256, 128]),
    ("condT", [128, 8, 2]), ("w_mod", [D, 6 * D]), ("b_mod", [6 * D]), ("gains", [4 * D]),
    ("w_in", [D, 2304]), ("w_out", [D, D]), ("w_up", [D, 2 * DFF]), ("w_down", [DFF, D]),
    ("cwT", [128, 44, 3]), ("cbT", [128, 44]), ("sinkT", [128, 4]), ("ggT", [128, 8]),
    ("rpbsrc", [8, 19, 127]), ("namask", [7, 128, NO]), ("swmask", [7, 128, NO]),
    ("ropeq", [2, 128, NO]), ("ropek", [2, 128, NW]), ("flags", [128, 2]),
    ("ident", [128, 128]), ("antij", [128, 128]), ("sel", [2, 2, 128]),
]
OUT_SPECS = [
    ("y_p", [512, D]), ("y_s", [256, D]), ("nak", [512, 512]), ("nav", [512, 512]),
    ("nbk", [512, 128]), ("nbv", [512, 128]),
]


def build_program(STOP=99):
    nc = bass.Bass("TRN2", target_bir_lowering=False)
    S = Sched(nc)
    del ALL_TKS[:]

    def cut(n):
        if STOP == n:
            if n >= 5:
                for tt in range(4):
                    S.dma('sp', do["y_p"].ap()[tt * 128:(tt + 1) * 128, :], xP[:, tt, :], reads=[k_xP[tt]])
                for tt in range(2):
                    S.dma('sp', do["y_s"].ap()[tt * 128:(tt + 1) * 128, :], xO[:, tt, :], reads=[k_xO[tt]])
            S.finish(list(ALL_TKS))
            return True
        return False

    di = {n: nc.dram_tensor(n, list(sh), F32, kind="ExternalInput") for n, sh in IN_SPECS}
    do = {n: nc.dram_tensor(n, list(sh), F32, kind="ExternalOutput") for n, sh in OUT_SPECS}

    ps = [nc.alloc_psum_tensor("ps%d" % i, [128, 512], F32) for i in range(8)]
    kps = [Tk("ps%d" % i, excl=True) for i in range(8)]
    psb = [p.ap().bitcast(BF16) for p in ps]

    rr = {'ev': 0}

    def evac_eng():
        rr['ev'] += 1
        return 'act' if rr['ev'] % 2 == 0 else 'dve'

    def evac_copy(eng, out, in_, reads, writes=(), pwrites=(), scale=None):
        if eng == 'act':
            if scale is None:
                S.op('act', lambda e: e.activation(out=out, in_=in_, func=AF.Copy), reads=reads, writes=writes, pwrites=pwrites)
            else:
                S.op('act', lambda e: e.activation(out=out, in_=in_, func=AF.Copy, scale=scale), reads=reads, writes=writes, pwrites=pwrites)
        else:
            if scale is None:
                S.op(eng, lambda e: e.tensor_copy(out=out, in_=in_), reads=reads, writes=writes, pwrites=pwrites)
            else:
                S.op(eng, lambda e: e.tensor_scalar_mul(out=out, in0=in_, scalar1=scale), reads=reads, writes=writes, pwrites=pwrites)

    identb = S.alloc("identb", [128, 128], BF16); k_identb = identb.tk()
    identf = S.alloc("identf", [128, 128], F32); k_identf = identf.tk()
    antib = S.alloc("antib", [128, 128], BF16); k_antib = antib.tk()
    selb = S.alloc("selb", [2, 2, 128], F32); k_sel = selb.tk()
    ones64 = S.alloc("ones64", [128, 64], BF16); k_ones64 = ones64.tk()
    onesb = S.alloc("onesb", [128, 128], BF16); k_onesb = onesb.tk()
    epsb = S.alloc("epsb", [128, 1], F32); k_eps = epsb.tk()
    ggT = S.alloc("ggT", [128, 8], F32); k_gg = ggT.tk()
    cwT = S.alloc("cwT", [128, 44, 3], F32); k_cw = cwT.tk()
    cbT = S.alloc("cbT", [128, 44], F32); k_cb = cbT.tk()
    esT = S.alloc("esT", [128, 4], F32); k_es = esT.tk()
    flg = S.alloc("flg", [128, 2], F32); k_flg = flg.tk()
    ABT = S.alloc("ABT", [128, 4, 8, 2], F32); k_ABT = [ABT.tk("ABT0"), ABT.tk("ABT1")]
    Gbc = S.alloc("Gbc", [128, 2, 2, D], F32); k_Gbc = Gbc.tk()
    small = S.alloc("small", [128, 4, 4], F32); k_small = [small.tk("small%d" % i) for i in range(4)]
    NSM = 4

    S.dma('pool', identb[:], di["ident"].ap(), writes=[k_identb])
    S.dma('sp', identf[:], di["ident"].ap(), writes=[k_identf])
    S.dma('pool', antib[:], di["antij"].ap(), writes=[k_antib])
    S.dma('sp', selb[:], di["sel"].ap(), writes=[k_sel])
    S.dma('sp', ggT[:], di["ggT"].ap(), writes=[k_gg])
    S.dma('sp', cwT[:], di["cwT"].ap(), writes=[k_cw])
    S.dma('sp', cbT[:], di["cbT"].ap(), writes=[k_cb])
    S.dma('sp', esT[:], di["sinkT"].ap(), writes=[k_es])
    S.dma('sp', flg[:], di["flags"].ap(), writes=[k_flg])
    S.op('dve', lambda e: e.memset(ones64[:], 1.0), writes=[k_ones64])
    S.op('dve', lambda e: e.memset(onesb[:], 1.0), writes=[k_onesb])
    S.op('dve', lambda e: e.memset(epsb[:], EPS), writes=[k_eps])

    xP = S.alloc("xP", [128, 4, D], F32); k_xP = xP.tks_n(4)
    xO = S.alloc("xO", [128, 3, D], F32); k_xO = xO.tks_n(3)
    OTOK = [(0, 128), (1, 128), (2, 2)]

    def ocols(tt):
        if tt < 2:
            return slice(1 + 128 * tt, 129 + 128 * tt)
        return slice(0, NO, NO - 1)

    condT = S.alloc("condT", [128, 8, 2], F32); k_condT = condT.tk()
    sT = S.alloc("sT", [128, 8, 2], BF16); k_sT = sT.tk()
    m_sb = S.alloc("m_sb", [2, 6 * D], F32); k_m = m_sb.tks_n(12)
    gains2 = S.alloc("gains2", [2, 4, D], F32); k_g2 = gains2.tk()
    rows = S.alloc("rows", [2, 4, D], F32); k_rows = rows.tks_n(4)
    wmb = [S.alloc("wm%d" % i, [128, 8, 512], BF16) for i in range(3)]
    k_wm = [b.tk() for b in wmb]

    S.dma('sp', condT[:], di["condT"].ap(), writes=[k_condT])
    S.dma('sp', m_sb[:], bass.AP(di["b_mod"], 0, [[0, 2], [1, 6 * D]]), writes=k_m)
    S.dma('sp', gains2[:], bass.AP(di["gains"], 0, [[0, 2], [D, 4], [1, D]]), writes=[k_g2])
    for tt in range(4):
        S.dma('sp', xP[:, tt, :], di["xp"].ap()[tt * 128:(tt + 1) * 128, :], writes=[k_xP[tt]])
    for tt in range(2):
        S.dma('sp', xO[:, tt, :], di["xo"].ap()[tt * 128:(tt + 1) * 128, :], writes=[k_xO[tt]])
    S.dma('sp', xO[0:2, 2, :], di["xh"].ap(), writes=[k_xO[2]])
    S.op('act', lambda e: e.activation(out=sT[:], in_=condT[:], func=AF.Silu), reads=[k_condT], writes=[k_sT])

    wmod_v = di["w_mod"].ap().rearrange("(kc p) n -> p kc n", p=128)
    ada = {'n': 0}

    def ada_dma(jc):
        b = jc % 3
        S.dma('pool', wmb[b][:], wmod_v[:, :, jc * 512:(jc + 1) * 512], writes=[k_wm[b]])

    def ada_chunk(jc):
        b = jc % 3
        pb = 2 + (ada['n'] % 2); ada['n'] += 1

        def mm(e):
            for kc in range(8):
                ins = e.matmul(ps[pb][0:2, :], lhsT=sT[:, kc, :], rhs=wmb[b][:, kc, :], start=(kc == 0), stop=(kc == 7))
            return ins
        S.op('pe', mm, reads=[k_sT, k_wm[b]], writes=[kps[pb]])
        sl = slice(jc * 512, (jc + 1) * 512)
        S.op('dve', lambda e: e.tensor_tensor(out=m_sb[0:2, sl], in0=ps[pb][0:2, :], in1=m_sb[0:2, sl], op=ALU.add),
             reads=[kps[pb]], writes=[k_m[jc]])
        if jc + 3 < 12:
            ada_dma(jc + 3)

    for jc in range(3):
        ada_dma(jc)
    for jc in range(4):
        ada_chunk(jc)
    S.op('dve', lambda e: e.scalar_tensor_tensor(out=rows[0:2, 0, :], in0=m_sb[0:2, D:2 * D], scalar=1.0, in1=gains2[0:2, 0, :],
                                                 op0=ALU.add, op1=ALU.mult), reads=[k_m[2], k_m[3], k_g2], writes=[k_rows[0]])

    def tpAB(which):
        srcs = [rows[0:2, which, :], m_sb[0:2, 3 * which * D:(3 * which + 1) * D]]

        def fn(e):
            for idx in range(2):
                for kc in range(8):
                    o = (idx * 8 + kc) * 2
                    ins = e.transpose(out=ps[4][:, o:o + 2], in_=srcs[idx][:, kc * 128:(kc + 1) * 128], identity=identf[0:2, 0:2])
            return ins
        S.op('pe', fn, reads=[k_rows[which], k_m[6 * which], k_m[6 * which + 1], k_identf], writes=[kps[4]])
        S.op('dve', lambda e: e.tensor_copy(out=ABT[:, 2 * which:2 * which + 2, :, :].rearrange("p a k r -> p (a k r)"), in_=ps[4][:, 0:32]),
             reads=[kps[4]], writes=[k_ABT[which]])
    tpAB(0)

    nb = {}

    def nonlocal_set(xn_list, junk_list):
        nb['xn'] = xn_list; nb['k_xn'] = [b.tk() for b in xn_list]
        nb['junk'] = junk_list; nb['k_junk'] = [b.tk() for b in junk_list]
    nonlocal_set([S.alloc("xn%d" % i, [128, D], BF16) for i in range(2)], [S.alloc("junk%d" % i, [128, D], BF16) for i in range(2)])
    cnt = {'nt': 0, 'sm': 0}

    def rstd_from_ssq(col_in, col_out, p, ksm, scale):
        S.op('act', lambda e: e.activation(out=col_out, in_=col_in, func=AF.Ln, scale=scale, bias=epsb[0:p, 0:1]),
             reads=[ksm, k_eps], writes=[ksm])
        S.op('act', lambda e: e.activation(out=col_out, in_=col_out, func=AF.Exp, scale=-0.5), reads=[ksm], writes=[ksm])

    def norm_S(tile):
        (x_ap, k_x, p, hT, k_hT, colsel, ab, r, pre) = tile
        if pre is not None:
            pre()
        i = cnt['nt']; cnt['nt'] += 1
        si = cnt['sm'] % NSM; cnt['sm'] += 1
        sm = small[0:p, si, :]; ksm = k_small[si]
        jb = nb['junk'][i % 2]; kj = nb['k_junk'][i % 2]
        xb = nb['xn'][i % 2]; kxb = nb['k_xn'][i % 2]
        S.op('act', lambda e: e.activation(out=jb[0:p, :], in_=x_ap, func=AF.Square, accum_out=sm[:, 0:1]), reads=[k_x], writes=[kj, ksm])
        rstd_from_ssq(sm[:, 0:1], sm[:, 1:2], p, ksm, 1.0 / D)
        S.op('dve', lambda e: e.tensor_scalar_mul(out=xb[0:p, :], in0=x_ap, scalar1=sm[:, 1:2]), reads=[k_x, ksm], writes=[kxb])
        return (i, xb, kxb)

    def norm_T(tile, st):
        (x_ap, k_x, p, hT, k_hT, colsel, ab, r, pre) = tile
        i, xb, kxb = st
        pb = nb.get('tp_base', 0) + i % 2

        def tp(e):
            for kc in range(8):
                ins = e.transpose(out=psb[pb][:, kc * 128:kc * 128 + p], in_=xb[0:p, kc * 128:(kc + 1) * 128], identity=identb[0:p, 0:p])
            return ins
        S.op('pe', tp, reads=[kxb, k_identb], writes=[kps[pb]])
        return (i, pb)

    def norm_A(tile):
        return norm_T(tile, norm_S(tile))

    def norm_B(tile, st, force_eng=None):
        (x_ap, k_x, p, hT, k_hT, colsel, ab, r, pre) = tile
        i, pb = st
        eng = 'act' if i % 2 == 1 else 'dve'
        if force_eng is not None:
            tmpN = nb['tmpN']; ktmp = nb['k_tmpN']
            if isinstance(tmpN, Buf):
                tmpN = tmpN.t[:]
            a_b = bass.AP(ABT.t, (2 * ab) * 16 + r, [[64, 128], [2, 8], [0, p]])
            b_b = bass.AP(ABT.t, (2 * ab + 1) * 16 + r, [[64, 128], [2, 8], [0, p]])
            S.op('dve', lambda e: e.tensor_tensor(out=tmpN[:, :, 0:p], in0=psb[pb][:, :].rearrange("q (k n) -> q k n", k=8)[:, :, 0:p], in1=a_b, op=ALU.mult),
                 reads=[kps[pb], k_ABT[ab]], writes=[ktmp])
            S.op('dve', lambda e: e.tensor_tensor(out=hT[:, :, colsel], in0=tmpN[:, :, 0:p], in1=b_b, op=ALU.add),
                 reads=[ktmp, k_ABT[ab]], pwrites=[k_hT])
            return
        for kc in range(8):
            o = hT[:, kc, colsel]
            i_ = psb[pb][:, kc * 128:kc * 128 + p]
            a_ = ABT[:, 2 * ab, kc, r:r + 1]
            b_ = ABT[:, 2 * ab + 1, kc, r:r + 1]
            if eng == 'act':
                S.op('act', lambda e, o=o, i_=i_, a_=a_, b_=b_: e.activation(out=o, in_=i_, func=AF.Identity, bias=b_, scale=a_),
                     reads=[kps[pb], k_ABT[ab]], pwrites=[k_hT])
            else:
                S.op('dve', lambda e, o=o, i_=i_, a_=a_, b_=b_: e.tensor_scalar(out=o, in0=i_, scalar1=a_, scalar2=b_, op0=ALU.mult, op1=ALU.add),
                     reads=[kps[pb], k_ABT[ab]], pwrites=[k_hT])

    def norm_steps(tiles, force_eng=None):
        state = {}
        thunks = []
        for k in range(len(tiles) + 1):
            def th(k=k):
                if k < len(tiles):
                    state[k] = norm_S(tiles[k])
                if k >= 1:
                    st = norm_T(tiles[k - 1], state[k - 1])
                    norm_B(tiles[k - 1], st, force_eng)
            thunks.append(th)
        return thunks

    def norm_pipeline(tiles, extras=()):
        extras = list(extras)
        prev = None
        for n, t in enumerate(tiles):
            st = norm_A(t)
            if prev is not None:
                norm_B(*prev)
            prev = (t, st)
            if n % 2 == 1 and extras:
                extras.pop(0)()
        norm_B(*prev)
        for ex in extras:
            ex()

    hT_P = S.alloc("hT_P", [128, 8, 512], BF16); k_hTP = hT_P.tks_n(4)
    hT_W = S.alloc("hT_W", [128, 8, NW], BF16); k_hTW = hT_W.tks_n(7)
    hT_O = S.alloc("hT_O", [128, 8, NO], BF16); k_hTO = hT_O.tks_n(3)
    xWs = [S.alloc("xWs%d" % i, [128, D], F32) for i in range(3)]
    k_xWs = [b.tk() for b in xWs]

    tiles = []
    for tt in range(4):
        tiles.append((xP[:, tt, :], k_xP[tt], 128, hT_P, k_hTP[tt], slice(tt * 128, (tt + 1) * 128), 0, 0, None))
    for tt in range(7):
        b = tt % 3

        def pre(tt=tt, b=b):
            S.dma('sp', xWs[b][:], di["xw"].ap()[tt * 128:(tt + 1) * 128, :], writes=[k_xWs[b]])
        tiles.append((xWs[b][:], k_xWs[b], 128, hT_W, k_hTW[tt], slice(tt * 128, (tt + 1) * 128), 0, 1, pre))
    for tt, p in OTOK:
        tiles.append((xO[0:p, tt, :], k_xO[tt], p, hT_O, k_hTO[tt], ocols(tt), 0, 1, None))
    norm_pipeline(tiles, [lambda jc=jc: ada_chunk(jc) for jc in range(4, 12)])
    S.free(*xWs)

    S.op('dve', lambda e: e.scalar_tensor_tensor(out=rows[0:2, 1, :], in0=m_sb[0:2, 4 * D:5 * D], scalar=1.0, in1=gains2[0:2, 2, :],
                                                 op0=ALU.add, op1=ALU.mult), reads=[k_m[8], k_m[9], k_g2], writes=[k_rows[1]])
    S.op('dve', lambda e: e.tensor_tensor(out=rows[0:2, 2, :], in0=m_sb[0:2, 2 * D:3 * D], in1=gains2[0:2, 1, :], op=ALU.mult),
         reads=[k_m[4], k_m[5], k_g2], writes=[k_rows[2]])
    S.op('dve', lambda e: e.tensor_tensor(out=rows[0:2, 3, :], in0=m_sb[0:2, 5 * D:6 * D], in1=gains2[0:2, 3, :], op=ALU.mult),
         reads=[k_m[10], k_m[11], k_g2], writes=[k_rows[3]])
    tpAB(1)
    n = 0
    for which in range(2):
        for r in range(2):
            for hf in range(2):
                pb = 5 + (n % 2)
                n += 1
                S.op('pe', lambda e, pb=pb, which=which, r=r, hf=hf: e.matmul(
                    ps[pb][:, :], lhsT=selb[0:2, r, :], rhs=rows[0:2, 2 + which, hf * 512:(hf + 1) * 512], start=True, stop=True),
                    reads=[k_sel, k_rows[2 + which]], writes=[kps[pb]])
                evac_copy(evac_eng(), Gbc[:, which, r, hf * 512:(hf + 1) * 512], ps[pb][:, :], reads=[kps[pb]], pwrites=[k_Gbc])
    S.op('act', lambda e: e.activation(out=esT[:], in_=esT[:], func=AF.Exp), reads=[k_es], writes=[k_es])
    S.free(condT, sT, m_sb, gains2, rows, *wmb)
    S.free(*nb['xn'], *nb['junk'])
    if cut(1):
        return nc, S

    QT_P = S.alloc("QT_P", [128, 8, 512], BF16); k_QTP = QT_P.tks_n(8)
    KT_P = S.alloc("KT_P", [128, 6, 512], BF16); k_KTP = KT_P.tks_n(6)
    V_P = S.alloc("V_P", [128, 4, 10, 64], BF16); k_VP = V_P.tks_n(4)
    QT_O = S.alloc("QT_O", [128, 8, NO], BF16); k_QTO = QT_O.tks_n(8)
    KT_W = S.alloc("KT_W", [128, 6, NW], BF16); k_KTW = KT_W.tks_n(6)
    V_W = S.alloc("V_W", [128, 7, 10, 64], BF16); k_VW = V_W.tks_n(7)
    KT_C = S.alloc("KT_C", [128, 6, 256], BF16); k_KTC = KT_C.tks_n(6)
    V_C = S.alloc("V_C", [128, 2, 10, 64], BF16); k_VC = V_C.tks_n(2)
    rope_q = S.alloc("rope_q", [128, 2, NO], F32); k_rq = rope_q.tk()
    rope_k = S.alloc("rope_k", [128, 2, NW], F32); k_rk = rope_k.tk()
    wg = [S.alloc("wg%d" % i, [128, 8, 512], BF16) for i in range(3)]
    k_wg = [b.tk() for b in wg]
    wrot = S.alloc("wrot", [128, 8, 512], BF16); k_wrot = wrot.tk()
    wdup = S.alloc("wdup", [128, 8, 2, 128], BF16); k_wdup = wdup.tk()
    wdupr = S.alloc("wdupr", [128, 8, 2, 128], BF16); k_wdupr = wdupr.tk()
    stg = [S.alloc("stg%d" % i, [128, 512], F32) for i in range(3)]
    k_stg = [b.tk() for b in stg]
    rtmp = [S.alloc("rtmp%d" % i, [128, 512], F32) for i in range(2)]
    k_rtmp = [b.tk() for b in rtmp]

    win_v = di["w_in"].ap().rearrange("(kc p) n -> p kc n", p=128)
    GCOLS = [(0, 512), (512, 512), (1024, 512), (1536, 512), (2048, 256)]
    for g in range(3):
        c0, w = GCOLS[g]
        S.dma('pool', wg[g][:, :, 0:w], win_v[:, :, c0:c0 + w], writes=[k_wg[g]])
    S.dma('sp', rope_q[:], di["ropeq"].ap().rearrange("t p n -> p t n"), writes=[k_rq])
    S.dma('sp', rope_k[:], di["ropek"].ap().rearrange("t p n -> p t n"), writes=[k_rk])
    ckst = S.alloc("ckst", [128, 2, 768], BF16); k_ckst = ckst.tk()
    for t in range(2):
        S.dma('pool', ckst[:, t, 0:512], di["cak"].ap()[t * 128:(t + 1) * 128, :], pwrites=[k_ckst])
        for kv in range(2):
            for hf in range(2):
                S.dma('pool', ckst[:, t, 512 + kv * 128 + hf * 64: 512 + kv * 128 + hf * 64 + 64],
                      di["cbk"].ap()[t * 128:(t + 1) * 128, kv * 64:(kv + 1) * 64], pwrites=[k_ckst])
        S.dma('pool', V_C[:, t, 0:8, :], di["cav"].ap()[t * 128:(t + 1) * 128, :].rearrange("p (h d) -> p h d", d=64), pwrites=[k_VC[t]])
        S.dma('pool', V_C[:, t, 8:10, :], di["cbv"].ap()[t * 128:(t + 1) * 128, :].rearrange("p (h d) -> p h d", d=64), pwrites=[k_VC[t]])

    pcnt = {'i': 0, 'st': 0, 'rt': 0}

    def nxt_ps():
        pcnt['i'] += 1
        return 2 + (pcnt['i'] % 6)

    def run_fm(w_ap_fn, w_tks, hT, k_hT_list, n0, n):
        pb = nxt_ps()

        def mm(e):
            for kc in range(8):
                ins = e.matmul(ps[pb][:, 0:n], lhsT=w_ap_fn(kc), rhs=hT[:, kc, n0:n0 + n], start=(kc == 0), stop=(kc == 7))
            return ins
        S.op('pe', mm, reads=list(w_tks) + list(k_hT_list), writes=[kps[pb]])
        return pb

    def tm_proj(hT, k_hT, colsel, p, w_ap_fn, w_tks, ncols):
        pb = nxt_ps()

        def mm(e):
            for kc in range(8):
                ins = e.matmul(ps[pb][0:p, 0:ncols], lhsT=hT[:, kc, colsel], rhs=w_ap_fn(kc), start=(kc == 0), stop=(kc == 7))
            return ins
        S.op('pe', mm, reads=list(w_tks) + [k_hT], writes=[kps[pb]])
        return pb

    WSEG = [(0, 512), (512, 384)]

    g = 0
    for c in range(4):
        wf = lambda kc, c=c, g=g: wg[g][:, kc, c * 128:(c + 1) * 128]
        pb = run_fm(wf, [k_wg[g]], hT_P, k_hTP, 0, 512)
        evac_copy(evac_eng(), QT_P[:, c, :], ps[pb][:, 0:512], reads=[kps[pb]], writes=[k_QTP[c]], scale=0.125)
        pb = run_fm(wf, [k_wg[g]], hT_O, k_hTO, 0, NO)
        evac_copy(evac_eng(), QT_O[:, c, :], ps[pb][:, 0:NO], reads=[kps[pb]], writes=[k_QTO[c]], scale=0.125)
    g = 1
    for c in range(4):
        wf = lambda kc, c=c, g=g: wg[g][:, kc, c * 128:(c + 1) * 128]
        pb = run_fm(wf, [k_wg[g]], hT_P, k_hTP, 0, 512)
        evac_copy(evac_eng(), KT_P[:, c, :], ps[pb][:, 0:512], reads=[kps[pb]], writes=[k_KTP[c]])
        for (n0, n) in WSEG:
            pb = run_fm(wf, [k_wg[g]], hT_W, k_hTW, n0, n)
            evac_copy(evac_eng(), KT_W[:, c, n0:n0 + n], ps[pb][:, 0:n], reads=[kps[pb]], pwrites=[k_KTW[c]])
    for tt in range(4):
        pb = tm_proj(hT_P, k_hTP[tt], slice(tt * 128, (tt + 1) * 128), 128, lambda kc, g=g: wg[g][:, kc, :], [k_wg[g]], 512)
        si = pcnt['st'] % 3; pcnt['st'] += 1
        evac_copy(evac_eng(), stg[si][:], ps[pb][:, :], reads=[kps[pb]], writes=[k_stg[si]])
        S.dma('sp', do["nak"].ap()[tt * 128:(tt + 1) * 128, :], stg[si][:], reads=[k_stg[si]])
    S.dma('pool', wg[0][:, :, 0:512], win_v[:, :, 1536:2048], writes=[k_wg[0]])
    g = 2
    for tt in range(4):
        pb = tm_proj(hT_P, k_hTP[tt], slice(tt * 128, (tt + 1) * 128), 128, lambda kc, g=g: wg[g][:, kc, :], [k_wg[g]], 512)
        si = pcnt['st'] % 3; pcnt['st'] += 1
        evac_copy('dve', stg[si][:], ps[pb][:, :], reads=[kps[pb]], writes=[k_stg[si]])
        evac_copy('act', V_P[:, tt, 0:8, :], ps[pb][:, :].rearrange("p (h d) -> p h d", d=64), reads=[kps[pb]], pwrites=[k_VP[tt]])
        S.dma('sp', do["nav"].ap()[tt * 128:(tt + 1) * 128, :], stg[si][:], reads=[k_stg[si]])
    for tt in range(7):
        pb = tm_proj(hT_W, k_hTW[tt], slice(tt * 128, (tt + 1) * 128), 128, lambda kc, g=g: wg[g][:, kc, :], [k_wg[g]], 512)
        evac_copy(evac_eng(), V_W[:, tt, 0:8, :], ps[pb][:, :].rearrange("p (h d) -> p h d", d=64), reads=[kps[pb]], pwrites=[k_VW[tt]])
    S.dma('pool', wg[1][:, :, 0:256], win_v[:, :, 2048:2304], writes=[k_wg[1]])

    def build_rot(dst, k_dst, src, k_src):
        sv = src.rearrange("p k (a t s) -> p (k a) t s", t=2, s=16)
        dv = dst.rearrange("p k (a t s) -> p (k a) t s", t=2, s=16)
        S.op('dve', lambda e: e.tensor_scalar_mul(out=dv[:, :, 0, :], in0=sv[:, :, 1, :], scalar1=-1.0), reads=[k_src], pwrites=[k_dst])
        S.op('dve', lambda e: e.tensor_copy(out=dv[:, :, 1, :], in_=sv[:, :, 0, :]), reads=[k_src], pwrites=[k_dst])

    build_rot(wrot[:, :, :], k_wrot, wg[0][:, :, :], k_wg[0])
    for c in range(4):
        wf = lambda kc, c=c: wg[0][:, kc, c * 128:(c + 1) * 128]
        wfr = lambda kc, c=c: wrot[:, kc, c * 128:(c + 1) * 128]
        pb = run_fm(wf, [k_wg[0]], hT_P, k_hTP, 0, 512)
        evac_copy(evac_eng(), QT_P[:, 4 + c, :], ps[pb][:, 0:512], reads=[kps[pb]], writes=[k_QTP[4 + c]], scale=0.125)
        pb1 = run_fm(wf, [k_wg[0]], hT_O, k_hTO, 0, NO)
        pb2 = run_fm(wfr, [k_wrot], hT_O, k_hTO, 0, NO)
        ri = pcnt['rt'] % 2; pcnt['rt'] += 1
        ri2 = pcnt['rt'] % 2; pcnt['rt'] += 1
        S.op('dve', lambda e, pb1=pb1, ri=ri: e.scalar_tensor_tensor(out=rtmp[ri][:, 0:NO], in0=ps[pb1][:, 0:NO], scalar=0.125, in1=rope_q[:, 0, :],
                                                                     op0=ALU.mult, op1=ALU.mult), reads=[kps[pb1], k_rq], writes=[k_rtmp[ri]])
        S.op('dve', lambda e, pb2=pb2, ri2=ri2: e.scalar_tensor_tensor(out=rtmp[ri2][:, 0:NO], in0=ps[pb2][:, 0:NO], scalar=0.125, in1=rope_q[:, 1, :],
                                                                       op0=ALU.mult, op1=ALU.mult), reads=[kps[pb2], k_rq], writes=[k_rtmp[ri2]])
        S.op('pool', lambda e, ri=ri, ri2=ri2, c=c: e.tensor_tensor(out=QT_O[:, 4 + c, :], in0=rtmp[ri][:, 0:NO], in1=rtmp[ri2][:, 0:NO], op=ALU.add),
             reads=[k_rtmp[ri], k_rtmp[ri2]], writes=[k_QTO[4 + c]])
    for kv in range(2):
        for hf in range(2):
            S.op('dve', lambda e, kv=kv, hf=hf: e.tensor_copy(out=wdup[:, :, kv, hf * 64:(hf + 1) * 64], in_=wg[1][:, :, kv * 64:(kv + 1) * 64]),
                 reads=[k_wg[1]], pwrites=[k_wdup])
    build_rot(wdupr[:].rearrange("p k v n -> p k (v n)"), k_wdupr, wdup[:].rearrange("p k v n -> p k (v n)"), k_wdup)
    for kv in range(2):
        wf = lambda kc, kv=kv: wdup[:, kc, kv, :]
        wfr = lambda kc, kv=kv: wdupr[:, kc, kv, :]
        pb = run_fm(wf, [k_wdup], hT_P, k_hTP, 0, 512)
        evac_copy(evac_eng(), KT_P[:, 4 + kv, :], ps[pb][:, 0:512], reads=[kps[pb]], writes=[k_KTP[4 + kv]])
        for (n0, n) in WSEG:
            pb1 = run_fm(wf, [k_wdup], hT_W, k_hTW, n0, n)
            pb2 = run_fm(wfr, [k_wdupr], hT_W, k_hTW, n0, n)
            ri = pcnt['rt'] % 2; pcnt['rt'] += 1
            ri2 = pcnt['rt'] % 2; pcnt['rt'] += 1
            S.op('dve', lambda e, pb1=pb1, ri=ri, n0=n0, n=n: e.tensor_tensor(out=rtmp[ri][:, 0:n], in0=ps[pb1][:, 0:n], in1=rope_k[:, 0, n0:n0 + n], op=ALU.mult),
                 reads=[kps[pb1], k_rk], writes=[k_rtmp[ri]])
            S.op('dve', lambda e, pb2=pb2, ri2=ri2, n0=n0, n=n: e.tensor_tensor(out=rtmp[ri2][:, 0:n], in0=ps[pb2][:, 0:n], in1=rope_k[:, 1, n0:n0 + n], op=ALU.mult),
                 reads=[kps[pb2], k_rk], writes=[k_rtmp[ri2]])
            S.op('pool', lambda e, ri=ri, ri2=ri2, n0=n0, n=n, kv=kv: e.tensor_tensor(out=KT_W[:, 4 + kv, n0:n0 + n], in0=rtmp[ri][:, 0:n], in1=rtmp[ri2][:, 0:n], op=ALU.add),
                 reads=[k_rtmp[ri], k_rtmp[ri2]], pwrites=[k_KTW[4 + kv]])
    for tt in range(4):
        pb = tm_proj(hT_P, k_hTP[tt], slice(tt * 128, (tt + 1) * 128), 128, lambda kc: wg[1][:, kc, 0:256], [k_wg[1]], 256)
        si = pcnt['st'] % 3; pcnt['st'] += 1
        evac_copy('dve', stg[si][:, 0:256], ps[pb][:, 0:256], reads=[kps[pb]], writes=[k_stg[si]])
        evac_copy('act', V_P[:, tt, 8:10, :], ps[pb][:, 128:256].rearrange("p (h d) -> p h d", d=64), reads=[kps[pb]], pwrites=[k_VP[tt]])
        S.dma('sp', do["nbk"].ap()[tt * 128:(tt + 1) * 128, :], stg[si][:, 0:128], reads=[k_stg[si]])
        S.dma('sp', do["nbv"].ap()[tt * 128:(tt + 1) * 128, :], stg[si][:, 128:256], reads=[k_stg[si]])
    for tt in range(7):
        pb = tm_proj(hT_W, k_hTW[tt], slice(tt * 128, (tt + 1) * 128), 128, lambda kc: wg[1][:, kc, 128:256], [k_wg[1]], 128)
        evac_copy(evac_eng(), V_W[:, tt, 8:10, :], ps[pb][:, 0:128].rearrange("p (h d) -> p h d", d=64), reads=[kps[pb]], pwrites=[k_VW[tt]])
    for t in range(2):
        pb = nxt_ps()

        def tpc(e, t=t, pb=pb):
            for c in range(6):
                ins = e.transpose(out=psb[pb][:, c * 128:(c + 1) * 128], in_=ckst[:, t, c * 128:(c + 1) * 128], identity=identb[:])
            return ins
        S.op('pe', tpc, reads=[k_ckst, k_identb], writes=[kps[pb]])
        evac_copy(evac_eng(), KT_C[:, :, t * 128:(t + 1) * 128], psb[pb][:, 0:768].rearrange("p (c n) -> p c n", n=128), reads=[kps[pb]],
                  pwrites=k_KTC)
    S.free(hT_P, hT_W, hT_O, rope_q, rope_k, wrot, wdup, wdupr, ckst, *wg, *stg, *rtmp)
    if cut(2):
        return nc, S

    wo = S.alloc("wo", [128, 8, D], BF16); k_wo = wo.tk()
    S.dma('pool', wo[:], di["w_out"].ap().rearrange("(kc p) n -> p kc n", p=128), writes=[k_wo])
    NPT = 3
    PT = [[S.alloc("PT%d_%d" % (i, j), [128, 512], BF16) for j in range(2)] for i in range(NPT)]
    k_PT = [[b.tk() for b in row] for row in PT]
    SB = [[S.alloc("SB%d_%d" % (i, j), [128, NO], F32) for j in range(2)] for i in range(2)]
    k_SB = [[b.tk() for b in row] for row in SB]
    OT = S.alloc("OT", [128, 8, NO], F32); k_OT = OT.tks_n(8)
    OTn_b = [S.alloc("OTn%d" % i, [128, 8, NO], BF16) for i in range(2)]
    k_OTn_b = [b.tks_n(8) for b in OTn_b]
    sq = S.alloc("sq", [128, 8, NO], BF16); k_sq = sq.tks_n(8)
    Rb = [S.alloc("Rb%d" % i, [128, NO], F32) for i in range(2)]
    k_Rb = [b.tk() for b in Rb]
    rg = S.alloc("rg", [128, NO], F32); k_rg = rg.tk()
    ytmp = [S.alloc("ytmp%d" % i, [128, D], F32) for i in range(2)]
    k_ytmp = [b.tk() for b in ytmp]
    yjunk = [S.alloc("yjunk%d" % i, [128, 512], BF16) for i in range(2)]
    k_yjunk = [b.tk() for b in yjunk]
    HB = S.alloc("HB", [128, 7, 8, NO], BF16); k_HB = HB.tks_n(7)
    SWM = S.alloc("SWM", [128, 7, NO], BF16); k_SWM = SWM.tk()
    NAM = S.alloc("NAM", [128, 7, NO], BF16); k_NAM = NAM.tk()
    hbst = [S.alloc("hbst%d" % i, [128, 8, NO], F32) for i in range(2)]
    k_hbst = [b.tk() for b in hbst]
    S.dma('pool', SWM[:], di["swmask"].ap().rearrange("k p n -> p k n"), writes=[k_SWM])
    S.dma('pool', NAM[:], di["namask"].ap().rearrange("k p n -> p k n"), writes=[k_NAM])
    rs = di["rpbsrc"]
    def hb_step(kt):
        b = kt % 2
        for a in range(2):
            u0 = 13 - 2 * kt + a
            for qr in range(4):
                S.dma('sp', hbst[b][64 * a:64 * a + 64, :, 1 + 64 * qr:65 + 64 * qr],
                      bass.AP(rs, (u0 + qr) * 127, [[1, 64], [19 * 127, 8], [1, 64]]), pwrites=[k_hbst[b]])
            S.dma('sp', hbst[b][64 * a:64 * a + 64, :, 0:1],
                  bass.AP(rs, (u0 - 1) * 127 + 63, [[1, 64], [19 * 127, 8], [1, 1]]), pwrites=[k_hbst[b]], allow_slow_non_contiguous=True)
            S.dma('sp', hbst[b][64 * a:64 * a + 64, :, 257:258],
                  bass.AP(rs, (u0 + 4) * 127, [[1, 64], [19 * 127, 8], [1, 1]]), pwrites=[k_hbst[b]], allow_slow_non_contiguous=True)
        nam_b = bass.AP(NAM.t, kt * NO, [[7 * NO, 128], [0, 8], [1, NO]])
        S.op('pool', lambda e, kt=kt, b=b, nam_b=nam_b: e.tensor_tensor(out=HB[:, kt, :, :], in0=hbst[b][:], in1=nam_b, op=ALU.add),
             reads=[k_hbst[b], k_NAM], writes=[k_HB[kt]])
    if cut(3):
        return nc, S

    acnt = {'s': 0, 'pt': 0, 'od': 0, 'od4': 0, 'rb': 0, 'y': 0}

    def s_slot():
        sl = acnt['s'] % 2; acnt['s'] += 1
        return sl

    def attention(dus):
        LAG = 2
        FINLAG = 1
        FINLAG_B = 4
        finq = []
        finq_b = []
        pend = []
        pair_od = {}
        for idx in range(len(dus) + LAG):
            if idx < len(dus):
                u = dus[idx]
                sl = s_slot()
                pt = acnt['pt'] % NPT; acnt['pt'] += 1
                b0, b1 = 2 * sl, 2 * sl + 1
                S.op('pe', lambda e, u=u, b0=b0, b1=b1: u['qk'](e, ps[b0], ps[b1]), reads=u['qk_reads'], writes=[kps[b0], kps[b1]])
                n = u['n']
                for h2 in range(2):
                    bk = (b0, b1)[h2]
                    bias = u['bias'][h2]
                    if bias is not None:
                        bap, btk = bias
                        S.op('dve', lambda e, bk=bk, sl=sl, h2=h2, bap=bap, n=n: e.tensor_tensor(out=SB[sl][h2][:, 0:n], in0=ps[bk][:, 0:n], in1=bap, op=ALU.add),
                             reads=[kps[bk], btk], writes=[k_SB[sl][h2]])
                        S.op('act', lambda e, sl=sl, h2=h2, pt=pt, n=n: e.activation(out=PT[pt][h2][:, 0:n], in_=SB[sl][h2][:, 0:n], func=AF.Exp),
                             reads=[k_SB[sl][h2]], writes=[k_PT[pt][h2]])
                    else:
                        S.op('act', lambda e, bk=bk, h2=h2, pt=pt, n=n: e.activation(out=PT[pt][h2][:, 0:n], in_=ps[bk][:, 0:n], func=AF.Exp),
                             reads=[kps[bk]], writes=[k_PT[pt][h2]])
                pend.append((u, pt))
            if idx >= LAG:
                u, pt = pend[idx - LAG]
                pi = u['pair']
                if pi not in pair_od:
                    if u['n'] == 512:
                        b = 4 + (acnt['od4'] % 4); acnt['od4'] += 1
                        pair_od[pi] = (b, b, ps[b][:, 0:256], ps[b][:, 256:512])
                    else:
                        od = acnt['od'] % 2; acnt['od'] += 1
                        pair_od[pi] = (4 + od, 6 + od, ps[4 + od], ps[6 + od])
                bO, bD, apO, apD = pair_od[pi]
                first = u['first']
                bks = [kps[bO]] if bO == bD else [kps[bO], kps[bD]]
                S.op('pe', lambda e, u=u, pt=pt, apO=apO, apD=apD: u['pv'](e, PT[pt][0], PT[pt][1], apO, apD),
                     reads=[k_PT[pt][0], k_PT[pt][1]] + u['pv_reads'],
                     writes=(bks if first else []), pwrites=([] if first else bks))
                if u['last']:
                    finq.append((idx + FINLAG, u['fin'], pair_od[pi]))
                    if u.get('fin_b') is not None:
                        finq_b.append((idx + FINLAG_B, u['fin_b']))
                for hk in u['hooks']:
                    hk()
            while finq and (finq[0][0] <= idx or idx == len(dus) + LAG - 1):
                _, f, od_ = finq.pop(0)
                f(od_)
            while finq_b and (finq_b[0][0] <= idx or idx == len(dus) + LAG - 1):
                _, f = finq_b.pop(0)
                f()

    def finish_pair_common(i, od, N, is_b):
        bO, bD, apO, apD = od
        rb = acnt['rb'] % 2; acnt['rb'] += 1
        if is_b:
            S.op('act', lambda e: e.activation(out=Rb[rb][:, 0:N], in_=apD[:, 0:N], func=AF.Ln, bias=esT[:, i - 4:i - 3]),
                 reads=[kps[bD], k_es], writes=[k_Rb[rb]])
        else:
            S.op('act', lambda e: e.activation(out=Rb[rb][:, 0:N], in_=apD[:, 0:N], func=AF.Ln), reads=[kps[bD]], writes=[k_Rb[rb]])
        S.op('act', lambda e: e.activation(out=Rb[rb][:, 0:N], in_=Rb[rb][:, 0:N], func=AF.Exp, scale=-1.0), reads=[k_Rb[rb]], writes=[k_Rb[rb]])
        S.op('dve', lambda e: e.tensor_tensor(out=OT[:, i, 0:N], in0=apO[:, 0:N], in1=Rb[rb][:, 0:N], op=ALU.mult),
             reads=[kps[bO], k_Rb[rb]], writes=[k_OT[i]])
        S.op('dve', lambda e: e.tensor_tensor(out=sq[:, i, 0:N], in0=OT[:, i, 0:N], in1=OT[:, i, 0:N], op=ALU.mult),
             reads=[k_OT[i]], writes=[k_sq[i]])

    def group_norm(grp, N, ob):
        OTn = OTn_b[ob]; k_OTn = k_OTn_b[ob]
        sb = 2 * s_slot()

        def mm(e):
            for c in range(4):
                ins = e.matmul(ps[sb][:, 0:N], lhsT=onesb[:], rhs=sq[:, grp * 4 + c, 0:N], start=(c == 0), stop=(c == 3))
            return ins
        S.op('pe', mm, reads=[k_onesb] + k_sq[grp * 4:grp * 4 + 4], writes=[kps[sb]])
        S.op('act', lambda e: e.activation(out=rg[:, 0:N], in_=ps[sb][:, 0:N], func=AF.Ln, scale=1.0 / 512, bias=epsb[:, 0:1]),
             reads=[kps[sb], k_eps], writes=[k_rg])
        S.op('act', lambda e: e.activation(out=rg[:, 0:N], in_=rg[:, 0:N], func=AF.Exp, scale=-0.5), reads=[k_rg], writes=[k_rg])
        for c in range(4):
            i = grp * 4 + c
            S.op('dve', lambda e, i=i: e.scalar_tensor_tensor(out=OTn[:, i, 0:N], in0=OT[:, i, 0:N], scalar=ggT[:, i:i + 1], in1=rg[:, 0:N],
                                                              op0=ALU.mult, op1=ALU.mult), reads=[k_OT[i], k_gg, k_rg], writes=[k_OTn[i]])

    def post_norm_residual(x_ap, k_x, p, mm_fn, mm_reads, which, r, banks):
        yi = acnt['y'] % 2; acnt['y'] += 1
        si = cnt['sm'] % NSM; cnt['sm'] += 1
        sm = small[0:p, si, :]; ksm = k_small[si]
        for hf in range(2):
            pb = banks[hf]
            S.op('pe', lambda e, hf=hf, pb=pb: mm_fn(e, ps[pb], hf), reads=mm_reads, writes=[kps[pb]])
            S.op('act', lambda e, hf=hf, pb=pb: e.activation(out=yjunk[hf][0:p, :], in_=ps[pb][0:p, :], func=AF.Square, accum_out=sm[:, hf:hf + 1]),
                 reads=[kps[pb]], writes=[k_yjunk[hf]], pwrites=[ksm])
            S.op('dve', lambda e, hf=hf, pb=pb: e.tensor_copy(out=ytmp[yi][0:p, hf * 512:(hf + 1) * 512], in_=ps[pb][0:p, :]),
                 reads=[kps[pb]], pwrites=[k_ytmp[yi]])
        S.op('dve', lambda e: e.tensor_tensor(out=sm[:, 2:3], in0=sm[:, 0:1], in1=sm[:, 1:2], op=ALU.add), reads=[ksm], writes=[ksm])
        rstd_from_ssq(sm[:, 2:3], sm[:, 3:4], p, ksm, 1.0 / D)
        S.op('dve', lambda e: e.scalar_tensor_tensor(out=ytmp[yi][0:p, :], in0=ytmp[yi][0:p, :], scalar=sm[:, 3:4], in1=Gbc[0:p, which, r, :],
                                                     op0=ALU.mult, op1=ALU.mult), reads=[ksm, k_Gbc], writes=[k_ytmp[yi]])
        S.op('dve', lambda e: e.tensor_tensor(out=x_ap, in0=x_ap, in1=ytmp[yi][0:p, :], op=ALU.add), reads=[k_ytmp[yi]], writes=[k_x])

    def wout_tile(x_ap, k_x, p, cols, r, ob, banks=None):
        OTn = OTn_b[ob]; k_OTn = k_OTn_b[ob]

        def mm_fn(e, pst, hf):
            for c in range(8):
                ins = e.matmul(pst[0:p, :], lhsT=OTn[:, c, cols], rhs=wo[:, c, hf * 512:(hf + 1) * 512], start=(c == 0), stop=(c == 7))
            return ins
        if banks is None:
            sl = s_slot()
            banks = (2 * sl, 2 * sl + 1)
        post_norm_residual(x_ap, k_x, p, mm_fn, [k_wo] + k_OTn, 0, r, banks)

    def head_src(i, pi_):
        if i < 4:
            return i, 2 * i + pi_, 2 * i + pi_
        hb = 2 * (i - 4) + pi_
        kvh = hb // 4
        return 4 + kvh, 8 + kvh, None

    all_dus = []
    for s in range(2):
        for i in range(8):
            src = [head_src(i, 0), head_src(i, 1)]

            def qk(e, pA, pB, i=i, s=s, src=src):
                for kt in range(2):
                    for pi_, pst in ((0, pA), (1, pB)):
                        lo = 64 * pi_
                        kc = src[pi_][0]
                        ins = e.matmul(pst[:, kt * 256:(kt + 1) * 256], lhsT=KT_P[lo:lo + 64, kc, s * 256 + kt * 128: s * 256 + (kt + 1) * 128],
                                       rhs=QT_P[lo:lo + 64, i, s * 256:(s + 1) * 256], start=True, stop=True)
                return ins

            def pv(e, pt0, pt1, pO, pD, s=s, src=src):
                for kt in range(2):
                    for pi_, ptb in ((0, pt0), (1, pt1)):
                        lo = 64 * pi_
                        e.matmul(pO[lo:lo + 64, 0:256], lhsT=V_P[:, 2 * s + kt, src[pi_][1], :], rhs=ptb[:, kt * 256:(kt + 1) * 256],
                                 start=(kt == 0), stop=(kt == 1), tile_position=(0, lo))
                for kt in range(2):
                    for pi_, ptb in ((0, pt0), (1, pt1)):
                        lo = 64 * pi_
                        ins = e.matmul(pD[lo:lo + 64, 0:256], lhsT=ones64[:], rhs=ptb[:, kt * 256:(kt + 1) * 256],
                                       start=(kt == 0), stop=(kt == 1), tile_position=(0, lo))
                return ins

            def fin(od, i=i, s=s):
                finish_pair_common(i, od, 256, i >= 4)
                if s == 0 and i < 7:
                    hb_step(i)
            fin_b = (lambda i=i, s=s: group_norm(i // 4, 256, s)) if i % 4 == 3 else None
            all_dus.append(dict(qk=qk, qk_reads=[k_KTP[src[0][0]], k_KTP[src[1][0]], k_QTP[i]], pv=pv,
                                pv_reads=[k_VP[2 * s], k_VP[2 * s + 1], k_ones64], n=512, bias=(None, None), pair=(s, i),
                                first=True, last=True, fin=fin, fin_b=fin_b, hooks=[]))

    def wout_prompt(s):
        for t2 in range(2):
            tt = 2 * s + t2
            wout_tile(xP[:, tt, :], k_xP[tt], 128, slice(t2 * 128, (t2 + 1) * 128), 0, s)
    all_dus[8 + 6]['hooks'].append(lambda: wout_prompt(0))

    for i in (4, 5, 6, 7, 0, 1, 2, 3):
        src = [head_src(i, 0), head_src(i, 1)]
        for kt in range(9):
            if kt < 7:
                ks = [KT_W[64 * p_:64 * p_ + 64, src[p_][0], kt * 128:(kt + 1) * 128] for p_ in range(2)]
                kk = [k_KTW[src[0][0]], k_KTW[src[1][0]]]
                vs = [V_W[:, kt, src[p_][1], :] for p_ in range(2)]
                kv_ = k_VW[kt]
                if i < 4:
                    bias = (None, None)
                    hb = [HB[:, kt, src[p_][2], :] for p_ in range(2)]
                    kk = kk + [k_antib, k_HB[kt]]
                else:
                    bias = ((SWM[:, kt, :], k_SWM), (SWM[:, kt, :], k_SWM))
                    hb = None
            else:
                ks = [KT_C[64 * p_:64 * p_ + 64, src[p_][0], (kt - 7) * 128:(kt - 6) * 128] for p_ in range(2)]
                kk = [k_KTC[src[0][0]], k_KTC[src[1][0]]]
                vs = [V_C[:, kt - 7, src[p_][1], :] for p_ in range(2)]
                kv_ = k_VC[kt - 7]
                bias = (None, None)
                hb = None

            def qk(e, pA, pB, i=i, ks=ks, hb=hb):
                e.matmul(pA[:, 0:NO], lhsT=ks[0], rhs=QT_O[0:64, i, :], start=True, stop=(hb is None))
                ins = e.matmul(pB[:, 0:NO], lhsT=ks[1], rhs=QT_O[64:128, i, :], start=True, stop=(hb is None))
                if hb is not None:
                    e.matmul(pA[:, 0:NO], lhsT=antib[:], rhs=hb[0], start=False, stop=True)
                    ins = e.matmul(pB[:, 0:NO], lhsT=antib[:], rhs=hb[1], start=False, stop=True)
                return ins

            def pv(e, pt0, pt1, pO, pD, vs=vs, kt=kt):
                e.matmul(pO[0:64, 0:NO], lhsT=vs[0], rhs=pt0[:, 0:NO], start=(kt == 0), stop=(kt == 8), tile_position=(0, 0))
                e.matmul(pO[64:128, 0:NO], lhsT=vs[1], rhs=pt1[:, 0:NO], start=(kt == 0), stop=(kt == 8), tile_position=(0, 64))
                e.matmul(pD[0:64, 0:NO], lhsT=ones64[:], rhs=pt0[:, 0:NO], start=(kt == 0), stop=(kt == 8), tile_position=(0, 0))
                return e.matmul(pD[64:128, 0:NO], lhsT=ones64[:], rhs=pt1[:, 0:NO], start=(kt == 0), stop=(kt == 8), tile_position=(0, 64))

            def fin_s(od, i=i):
                finish_pair_common(i, od, NO, i >= 4)
            fin_b = (lambda i=i: group_norm(i // 4, NO, 0)) if (i % 4 == 3 and kt == 8) else None
            all_dus.append(dict(qk=qk, qk_reads=kk + [k_QTO[i]], pv=pv, pv_reads=[kv_, k_ones64], n=NO, bias=bias, pair=(2, i),
                                first=(kt == 0), last=(kt == 8), fin=fin_s, fin_b=fin_b, hooks=[]))
    all_dus[16 + 7]['hooks'].append(lambda: wout_prompt(1))
    ffn_pre = {}

    def np_setup():
        S.free(*hbst, NAM)
        ffn_pre['xn'] = [S.alloc("xn%d" % i, [128, D], BF16) for i in range(2)]
        ffn_pre['junk'] = [S.alloc("junk%d" % i, [128, D], BF16) for i in range(2)]
        ffn_pre['h2P'] = S.alloc("h2P", [128, 8, 512], BF16)
        ffn_pre['k_h2P'] = ffn_pre['h2P'].tks_n(4)
        nonlocal_set(ffn_pre['xn'], ffn_pre['junk'])
        ffn_pre['tmpN'] = S.alloc("tmpN", [128, 8, 128], F32)
        nb['tmpN'] = ffn_pre['tmpN']; nb['k_tmpN'] = ffn_pre['tmpN'].tk()
        tl = []
        for tt in range(4):
            tl.append((xP[:, tt, :], k_xP[tt], 128, ffn_pre['h2P'], ffn_pre['k_h2P'][tt], slice(tt * 128, (tt + 1) * 128), 1, 0, None))
        ffn_pre['thunks'] = norm_steps(tl, 'dve')
    def wu_early():
        S.free(QT_P, KT_P, V_P)
        ffn_pre['wu'] = [S.alloc("wu%d" % i, [128, 8, 2, 128], BF16) for i in range(4)]
        ffn_pre['k_wu'] = [b.tk() for b in ffn_pre['wu']]
        for g in range(3):
            for gv in range(2):
                S.dma('pool', ffn_pre['wu'][g][:, :, gv, :], wup_v[:, :, gv * DFF + g * 128:gv * DFF + (g + 1) * 128], pwrites=[ffn_pre['k_wu'][g]])
    wup_v = di["w_up"].ap().rearrange("(kc p) n -> p kc n", p=128)
    all_dus[16 + 20]['hooks'].append(wu_early)
    all_dus[16 + 36]['hooks'].append(np_setup)
    for k in range(5):
        all_dus[16 + 38 + 6 * k]['hooks'].append(lambda k=k: ffn_pre['thunks'][k]())

    def np_done():
        S.free(ffn_pre['tmpN'])
        nb['tmpN'] = ytmp[0].t[:].rearrange("p (k n) -> p k n", k=8)
        nb['k_tmpN'] = k_ytmp[0]
    all_dus[16 + 38 + 6 * 4]['hooks'].append(np_done)
    attention(all_dus)
    for tt, p in OTOK:
        wout_tile(xO[0:p, tt, :], k_xO[tt], p, ocols(tt), 1, 0, banks=((6, 7), (2, 3), (6, 7))[tt])
    S.free(QT_O, KT_W, V_W, KT_C, V_C, HB, SWM, OT, sq, rg, wo, *Rb, *OTn_b)
    for row in PT:
        S.free(*row)
    for row in SB:
        S.free(*row)
    if cut(5):
        return nc, S

    wd = S.alloc("wd", [128, NPAIR, D], BF16); k_wd = wd.tk()
    h2P = ffn_pre['h2P']; k_h2P = ffn_pre['k_h2P']
    h2O = S.alloc("h2O", [128, 8, NO], BF16); k_h2O = h2O.tks_n(3)
    aP = S.alloc("aP", [128, NPAIR, 512], BF16); k_aP = aP.tks_n(NPAIR)
    aO = S.alloc("aO", [128, NPAIR, 256], BF16); k_aO = aO.tks_n(NPAIR)
    NWU = 7
    LAGO = 3
    wu = ffn_pre['wu'] + [S.alloc("wu%d" % i, [128, 8, 2, 128], BF16) for i in range(4, NWU)]
    k_wu = ffn_pre['k_wu'] + [b.tk() for b in wu[4:]]

    def load_wu(g):
        b = g % NWU
        S.dma('pool', wu[b][:, :, 0, :], wup_v[:, :, g * 128:(g + 1) * 128], pwrites=[k_wu[b]])
        S.dma('pool', wu[b][:, :, 1, :], wup_v[:, :, DFF + g * 128:DFF + (g + 1) * 128], pwrites=[k_wu[b]])
    wd_v = di["w_down"].ap().rearrange("(g p) n -> p g n", p=128)

    def load_wd(q):
        lo, hi = [(0, 6), (6, 12), (12, 17), (17, 22)][q]
        S.dma('pool', wd[:, lo:hi, :], wd_v[:, lo:hi, :], pwrites=[k_wd])
    tiles = []
    for tt, p in OTOK:
        tiles.append((xO[0:p, tt, :], k_xO[tt], p, h2O, k_h2O[tt], ocols(tt), 1, 1, None))
    ns_thunks = norm_steps(tiles, 'dve')

    def ns_flags():
        S.op('dve', lambda e: e.tensor_scalar_mul(out=h2O[:, :, 0:1], in0=h2O[:, :, 0:1], scalar1=flg[:, 0:1]), reads=[k_flg], writes=[k_h2O[2]])
        S.op('dve', lambda e: e.tensor_scalar_mul(out=h2O[:, :, NO - 1:NO], in0=h2O[:, :, NO - 1:NO], scalar1=flg[:, 1:2]), reads=[k_flg], writes=[k_h2O[2]])

    NT = 3
    pend_ep = []
    ta = [[S.alloc("ta%d_%d" % (i, j), [128, 512], F32) for j in range(2)] for i in range(NT)]
    k_ta = [[b.tk() for b in row] for row in ta]
    tb = [[S.alloc("tb%d_%d" % (i, j), [128, 512], F32) for j in range(2)] for i in range(2)]
    k_tb = [[b.tk() for b in row] for row in tb]
    fcnt = {'t': 0}

    def conv_epilogue(pbg, pbv, g, N, is_p):
        ti = fcnt['t'] % NT; fcnt['t'] += 1
        for gv, (pb, ch) in enumerate(((pbg, g), (pbv, NPAIR + g))):
            a_ = ta[ti][gv]; ka = k_ta[ti][gv]
            b_ = tb[ti % 2][gv]; kb = k_tb[ti % 2][gv]
            if is_p:
                a3 = a_[:, 0:512].rearrange("p (s n) -> p s n", s=2)
                b3 = b_[:, 0:512].rearrange("p (s n) -> p s n", s=2)
                u3 = ps[pb][:, 0:512].rearrange("p (s n) -> p s n", s=2)
                a_hi, a_lo, b_hi, u_lo, u_hi = a3[:, :, 1:256], a3[:, :, 0:255], b3[:, :, 1:256], u3[:, :, 0:255], u3[:, :, 1:256]
            else:
                a_hi, a_lo, b_hi, u_lo, u_hi = a_[:, 1:NO], a_[:, 0:NO - 1], b_[:, 1:NO], ps[pb][:, 0:NO - 1], ps[pb][:, 1:NO]
            S.op('act', lambda e, a_=a_, pb=pb, ch=ch: e.activation(out=a_[:, 0:N], in_=ps[pb][:, 0:N], func=AF.Identity,
                                                                    bias=cbT[:, ch:ch + 1], scale=cwT[:, ch, 1:2]),
                 reads=[kps[pb], k_cw, k_cb], writes=[ka])
            S.op('act', lambda e, b_hi=b_hi, u_lo=u_lo, ch=ch: e.activation(out=b_hi, in_=u_lo, func=AF.Copy, scale=cwT[:, ch, 0:1]),
                 reads=[kps[pb], k_cw], writes=[kb])
            S.op('dve', lambda e, a_lo=a_lo, u_hi=u_hi, ch=ch: e.scalar_tensor_tensor(out=a_lo, in0=u_hi, scalar=cwT[:, ch, 2:3], in1=a_lo,
                                                                                      op0=ALU.mult, op1=ALU.add), reads=[kps[pb], k_cw], writes=[ka])
            S.op('dve' if is_p else 'pool', lambda e, a_hi=a_hi, b_hi=b_hi: e.tensor_tensor(out=a_hi, in0=a_hi, in1=b_hi, op=ALU.add), reads=[kb], writes=[ka])
        pend_ep.append((ti, g, N, is_p))
        if len(pend_ep) > 1:
            conv_part2(*pend_ep.pop(0))

    def conv_part2(ti, g, N, is_p):
        S.op('act', lambda e: e.activation(out=ta[ti][0][:, 0:N], in_=ta[ti][0][:, 0:N], func=AF.Silu), reads=[k_ta[ti][0]], writes=[k_ta[ti][0]])
        if is_p:
            S.op('dve', lambda e: e.tensor_tensor(out=aP[:, g, :], in0=ta[ti][0][:, 0:512], in1=ta[ti][1][:, 0:512], op=ALU.mult),
                 reads=[k_ta[ti][0], k_ta[ti][1]], writes=[k_aP[g]])
        else:
            S.op('dve', lambda e: e.tensor_tensor(out=aO[:, g, :], in0=ta[ti][0][:, 1:257], in1=ta[ti][1][:, 1:257], op=ALU.mult),
                 reads=[k_ta[ti][0], k_ta[ti][1]], writes=[k_aO[g]])

    def ffn_mm(g, hT, khT, N, off):
        b = g % NWU
        base = 4 * (g % 2)
        for gv in range(2):
            pb = base + off + gv

            def mm(e, pb=pb, gv=gv):
                for kc in range(8):
                    ins = e.matmul(ps[pb][:, 0:N], lhsT=wu[b][:, kc, gv, :], rhs=hT[:, kc, 0:N], start=(kc == 0), stop=(kc == 7))
                return ins
            S.op('pe', mm, reads=[k_wu[b]] + khT, writes=[kps[pb]])
        conv_epilogue(base + off, base + off + 1, g, N, off == 0)

    nb['tp_base'] = 2
    for g in range(NPAIR + LAGO):
        if g < NPAIR:
            if g + 3 < NPAIR:
                load_wu(g + 3)
            if g in (5, 9, 13, 17):
                load_wd((g - 5) // 4)
            ffn_mm(g, h2P, k_h2P, 512, 0)
        if g < len(ns_thunks):
            ns_thunks[g]()
            if g == len(ns_thunks) - 1:
                ns_flags()
                S.free(*nb['xn'], *nb['junk'])
        if g >= LAGO:
            ffn_mm(g - LAGO, h2O, k_h2O, NO, 2)
    while pend_ep:
        conv_part2(*pend_ep.pop(0))
    S.free(h2P, h2O, *wu)
    for row in ta:
        S.free(*row)
    for row in tb:
        S.free(*row)
    if cut(6):
        return nc, S

    dcnt = {'b': 0}

    def wdown_tile(x_ap, k_x, aT, k_aT, cols, r, out_ap):
        def mm_fn(e, pst, hf):
            for g in range(NPAIR):
                ins = e.matmul(pst[:, :], lhsT=aT[:, g, cols], rhs=wd[:, g, hf * 512:(hf + 1) * 512], start=(g == 0), stop=(g == NPAIR - 1))
            return ins
        b0 = 2 * (dcnt['b'] % 4); dcnt['b'] += 1
        post_norm_residual(x_ap, k_x, 128, mm_fn, [k_wd] + k_aT, 1, r, (b0, b0 + 1))
        S.dma('sp', out_ap, x_ap, reads=[k_x])

    for tt in range(4):
        wdown_tile(xP[:, tt, :], k_xP[tt], aP, k_aP, slice(tt * 128, (tt + 1) * 128), 0, do["y_p"].ap()[tt * 128:(tt + 1) * 128, :])
    for tt in range(2):
        wdown_tile(xO[:, tt, :], k_xO[tt], aO, k_aO, slice(tt * 128, (tt + 1) * 128), 1, do["y_s"].ap()[tt * 128:(tt + 1) * 128, :])

    S.finish(list(ALL_TKS))
    return nc, S


def _host_consts(j):
    ws = [0, 0, 2, 2][j]
    qpos = np.concatenate([[256 * j - 1], 256 * j + np.arange(256), [256 * j + 256]])
    qvalid = (qpos >= 0) & (qpos < 1024)
    qp = np.clip(qpos, 0, 1023)
    kpos = 64 * ws + np.arange(NW)
    r, c = qp // 64, qp % 64
    rk, ck = kpos // 64, kpos % 64
    row_start = np.clip(r - 4, 0, 8)
    col_start = np.clip(c - 8, 0, 48)
    na_valid = ((rk[:, None] >= row_start[None, :]) & (rk[:, None] < row_start[None, :] + 8) &
                (ck[:, None] >= col_start[None, :]) & (ck[:, None] < col_start[None, :] + 16) & qvalid[None, :])
    na = np.where(na_valid, 0.0, NEG).astype(np.float32).reshape(7, 128, NO)
    namask = np.ascontiguousarray(na[:, ::-1, :])
    sw_valid = (np.abs(qp[None, :] - kpos[:, None]) <= 128) & qvalid[None, :]
    swmask = np.where(sw_valid, 0.0, NEG).astype(np.float32).reshape(7, 128, NO)

    def rope_tab(pos):
        n = 16
        inv = (1.0 / (10000.0 ** (np.arange(n, dtype=np.float32) / n))).astype(np.float32)
        pr = (pos // 64).astype(np.float32)
        pc = (pos % 64).astype(np.float32)
        cos = np.zeros((64, len(pos)), np.float32)
        sin = np.zeros((64, len(pos)), np.float32)
        for d in range(64):
            p_ = pr if d < 32 else pc
            ang = (p_ * inv[d % 16]).astype(np.float32)
            cos[d] = np.cos(ang)
            sin[d] = np.sin(ang)
        return np.stack([np.concatenate([cos, cos], 0), np.concatenate([sin, sin], 0)]).astype(np.float32)
    ropeq = rope_tab(qp)
    ropek = rope_tab(kpos)
    flags = np.zeros((128, 2), np.float32)
    flags[:, 0] = 1.0 if j > 0 else 0.0
    flags[:, 1] = 1.0 if j < 3 else 0.0
    return ws, namask, swmask, ropeq, ropek, flags


def _rpbsrc(rpb, j, ws):
    out = np.zeros((8, 19, 127), np.float32)
    delta = ws - 4 * j
    for up in range(19):
        u = 18 - up
        i = (u - 4) + delta + 7
        if 0 <= i < 15:
            out[:, up, 48:79] = rpb[:, i, ::-1]
    return out


_CACHE = {}


def kernel(x_prompt, x_sample, cache_a_k, cache_a_v, cache_b_k, cache_b_v, c, c_ctx,
           w_mod, b_mod, g_mix_pre, g_mix_post, g_ffn_pre, g_ffn_post, w_in, rpb_a, sink_b,
           g_grp_a, g_grp_b, w_out, w_up, conv_w, conv_b, w_down):
    f = lambda a: np.ascontiguousarray(np.asarray(a, dtype=np.float32))
    x_prompt, x_sample = f(x_prompt), f(x_sample)
    if 'nc' not in _CACHE:
        _CACHE['nc'] = build_program()
    nc, S = _CACHE['nc']
    shared = {
        "w_mod": f(w_mod[0]), "b_mod": f(b_mod[0]),
        "gains": f(np.concatenate([g_mix_pre[0], g_mix_post[0], g_ffn_pre[0], g_ffn_post[0]])),
        "w_in": f(w_in[0]), "w_out": f(w_out[0]), "w_up": f(w_up[0]), "w_down": f(w_down[0]),
        "cwT": f(np.asarray(conv_w[0]).T.reshape(44, 128, 3).transpose(1, 0, 2)),
        "cbT": f(np.asarray(conv_b[0]).reshape(44, 128).T),
        "ggT": f(np.concatenate([g_grp_a[0], g_grp_b[0]]).reshape(8, 128).T),
        "ident": np.eye(128, dtype=np.float32),
        "antij": np.ascontiguousarray(np.eye(128, dtype=np.float32)[::-1]),
        "sel": np.stack([np.stack([np.ones(128), np.zeros(128)]), np.stack([np.zeros(128), np.ones(128)])], 1).astype(np.float32),
    }
    sk = np.asarray(sink_b[0], np.float32).reshape(8)
    sinkT = np.zeros((128, 4), np.float32)
    for i in range(4):
        sinkT[0:64, i] = sk[2 * i]
        sinkT[64:128, i] = sk[2 * i + 1]
    shared["sinkT"] = sinkT
    in_maps = []
    for core in range(8):
        b, j = core // 4, core % 4
        ws, namask, swmask, ropeq, ropek, flags = _host_consts(j)
        xs = x_sample[b]
        halo = np.zeros((2, D), np.float32)
        if j > 0:
            halo[0] = xs[256 * j - 1]
        if j < 3:
            halo[1] = xs[256 * j + 256]
        cond = np.stack([np.asarray(c_ctx, np.float32), np.asarray(c[b], np.float32)], 1)
        m = dict(shared)
        m.update({
            "xp": f(x_prompt[2 * core:2 * core + 2].reshape(512, D)),
            "xo": f(xs[256 * j:256 * j + 256]), "xh": halo, "xw": f(xs[64 * ws:64 * ws + NW]),
            "cak": f(np.asarray(cache_a_k)[b, 0].reshape(256, 512)), "cav": f(np.asarray(cache_a_v)[b, 0].reshape(256, 512)),
            "cbk": f(np.asarray(cache_b_k)[b, 0].reshape(256, 128)), "cbv": f(np.asarray(cache_b_v)[b, 0].reshape(256, 128)),
            "condT": f(cond.reshape(8, 128, 2).transpose(1, 0, 2)),
            "rpbsrc": _rpbsrc(np.asarray(rpb_a[0], np.float32), j, ws),
            "namask": namask, "swmask": swmask, "ropeq": ropeq, "ropek": ropek, "flags": flags,
        })
        in_maps.append(m)
    res = run_bass_kernel_spmd(nc, in_maps, core_ids=list(range(8)))
    R = res.results
    y_p = np.concatenate([R[i]["y_p"].reshape(2, 256, D) for i in range(8)], 0)
    y_s = np.stack([np.concatenate([R[4 * b + j]["y_s"] for j in range(4)], 0) for b in range(2)], 0)
    nak = np.concatenate([R[i]["nak"].reshape(2, 1, 256, 8, 64) for i in range(8)], 0)
    nav = np.concatenate([R[i]["nav"].reshape(2, 1, 256, 8, 64) for i in range(8)], 0)
    nbk = np.concatenate([R[i]["nbk"].reshape(2, 1, 256, 2, 64) for i in range(8)], 0)
    nbv = np.concatenate([R[i]["nbv"].reshape(2, 1, 256, 2, 64) for i in range(8)], 0)
    return (y_p.astype(np.float32), y_s.astype(np.float32), nak.astype(np.float32), nav.astype(np.float32),
            nbk.astype(np.float32), nbv.astype(np.float32))
```
